# Optimizing a Trainium2 kernel written in Bass

```python
import jax
import jax.numpy as jnp
from jax import lax
import numpy as np

D_MODEL = 2048
BATCH = 2
SEQ = 16384
DEPTH = 2

HEAD_DIM = 128
ROPE_DIM = HEAD_DIM // 4
ROPE_THETA = 500000.0
DILATED_GROUPS = ((128, 1), (512, 4), (2048, 16))
N_GROUPS = 3
ATTN_HEADS_PER_GROUP = D_MODEL // 512
ATTN_HEADS = N_GROUPS * ATTN_HEADS_PER_GROUP
ATTN_BLOCK = 128
GDN_HEADS = D_MODEL // 256
GDN_HEAD_DIM = 128
GDN_WIDTH = GDN_HEADS * GDN_HEAD_DIM
GDN_CHUNK = 64
SHORT_CONV = 4
CONV_CH = D_MODEL // 2
CONV_K = 31
D_FF = ((8 * D_MODEL + 3 * 256 - 1) // (3 * 256)) * 256
NORM_EPS = 1e-6
IN_SPLIT_SIZES = (ATTN_HEADS * HEAD_DIM, ATTN_HEADS * HEAD_DIM, ATTN_HEADS * HEAD_DIM, 3 * GDN_WIDTH, GDN_HEADS, GDN_HEADS, GDN_WIDTH, 2 * CONV_CH, 3 * D_MODEL)
N_IN = 3 * ATTN_HEADS * HEAD_DIM + 4 * GDN_WIDTH + 2 * GDN_HEADS + 2 * CONV_CH + 3 * D_MODEL

kernel_name = 'hybrid_dilated_attn_gdn_conformer_block'


def split_cols(t, sizes):
    idx = np.cumsum(np.array(sizes))[:-1].tolist()
    return jnp.split(t, idx, axis=-1)


def rms_norm(t, g):
    tf = t.astype(jnp.float32)
    y = tf * lax.rsqrt(jnp.mean(tf * tf, axis=-1, keepdims=True) + NORM_EPS)
    return (y * g.astype(jnp.float32)).astype(t.dtype)


def layer_norm(t, g, b):
    tf = t.astype(jnp.float32)
    mu = jnp.mean(tf, axis=-1, keepdims=True)
    var = jnp.mean(jnp.square(tf - mu), axis=-1, keepdims=True)
    y = (tf - mu) * lax.rsqrt(var + NORM_EPS)
    return (y * g.astype(jnp.float32) + b.astype(jnp.float32)).astype(t.dtype)


def l2_normalize(t):
    tf = t.astype(jnp.float32)
    return tf * lax.rsqrt(jnp.sum(tf * tf, axis=-1, keepdims=True) + NORM_EPS)


def adaln(c_act, w_mod, b_mod):
    mod = c_act @ w_mod + b_mod
    shift, scale, gate = jnp.split(mod, 3, axis=-1)
    return shift[:, None], scale[:, None], gate[:, None]


def causal_depthwise_conv(t, w):
    k, ch = w.shape
    return lax.conv_general_dilated(t, w[:, None, :].astype(t.dtype), window_strides=(1,), padding=[(k - 1, 0)], dimension_numbers=('NWC', 'WIO', 'NWC'), feature_group_count=ch)


def rope_tables(positions):
    inv = ROPE_THETA ** (-jnp.arange(0, ROPE_DIM, 2, dtype=jnp.float32) / ROPE_DIM)
    ang = positions.astype(jnp.float32)[..., None] * inv
    return jnp.cos(ang), jnp.sin(ang)


def apply_partial_rope(t, cos, sin):
    half = ROPE_DIM // 2
    x1 = t[..., :half].astype(jnp.float32)
    x2 = t[..., half:ROPE_DIM].astype(jnp.float32)
    rot = jnp.concatenate([x1 * cos - x2 * sin, x2 * cos + x1 * sin], axis=-1).astype(t.dtype)
    return jnp.concatenate([rot, t[..., ROPE_DIM:]], axis=-1)


def dilated_window_attention(q, k, v, dilation, steps):
    B, S, H, Dh = q.shape
    L = S // dilation
    nb = -(-L // ATTN_BLOCK)
    Lp = nb * ATTN_BLOCK

    def to_blocks(t):
        t = t.reshape(B, L, dilation, H, Dh).transpose(0, 2, 1, 3, 4)
        t = jnp.pad(t, ((0, 0), (0, 0), (0, Lp - L), (0, 0), (0, 0)))
        return t.reshape(B, dilation, nb, ATTN_BLOCK, H, Dh)

    def with_prev(t):
        prev = jnp.pad(t, ((0, 0), (0, 0), (1, 0), (0, 0), (0, 0), (0, 0)))[:, :, :-1]
        return jnp.concatenate([prev, t], axis=3)

    qb = to_blocks(q)
    kw = with_prev(to_blocks(k))
    vw = with_prev(to_blocks(v))
    s = jnp.einsum('brnqhd,brnkhd->brnhqk', qb, kw, preferred_element_type=jnp.float32) * (Dh ** -0.5)
    qi = jnp.arange(ATTN_BLOCK)[:, None]
    kj = jnp.arange(2 * ATTN_BLOCK)[None, :]
    dist = ATTN_BLOCK + qi - kj
    band = (dist >= 0) & (dist <= steps)
    key_idx = jnp.arange(nb)[:, None, None] * ATTN_BLOCK + kj[None] - ATTN_BLOCK
    valid = band[None] & (key_idx >= 0)
    s = jnp.where(valid[:, None], s, -jnp.inf)
    lse = jax.nn.logsumexp(s, axis=-1)
    p = jnp.exp(s - lse[..., None])
    o = jnp.einsum('brnhqk,brnkhd->brnqhd', p.astype(v.dtype), vw, preferred_element_type=jnp.float32)
    o = o.reshape(B, dilation, Lp, H, Dh)[:, :, :L].transpose(0, 2, 1, 3, 4).reshape(B, S, H, Dh)
    lse = lse.transpose(0, 1, 2, 4, 3).reshape(B, dilation, Lp, H)[:, :, :L].transpose(0, 2, 1, 3).reshape(B, S, H)
    return o, lse


def gated_delta_rule_chunked(q, k, v, g, beta):
    B, S, H, Dk = q.shape
    Dv = v.shape[-1]
    C = GDN_CHUNK
    N = S // C

    def chunks(t):
        t = t.reshape((B, N, C, H) + t.shape[3:])
        return jnp.moveaxis(t, 3, 1)

    q, k, v, g, beta = (chunks(t) for t in (q, k, v, g, beta))
    g = jnp.cumsum(g, axis=-1)
    causal = jnp.tril(jnp.ones((C, C), dtype=bool))
    strict = jnp.tril(jnp.ones((C, C), dtype=bool), -1)
    decay = jnp.exp(jnp.where(causal, g[..., :, None] - g[..., None, :], -jnp.inf))
    kk = jnp.einsum('bhnid,bhnjd->bhnij', k, k)
    a = jnp.where(strict, beta[..., :, None] * kk * decay, 0.0) + jnp.eye(C, dtype=q.dtype)
    rhs = jnp.concatenate([v * beta[..., None], k * (beta * jnp.exp(g))[..., None]], axis=-1)
    sol = lax.linalg.triangular_solve(a, rhs, left_side=True, lower=True, unit_diagonal=True)
    u, w = sol[..., :Dv], sol[..., Dv:]
    qk = jnp.einsum('bhnid,bhnjd->bhnij', q, k) * decay
    q_dec = q * jnp.exp(g)[..., None]
    g_last = g[..., -1]
    k_dec = k * jnp.exp(g_last[..., None] - g)[..., None]
    xs = tuple(jnp.moveaxis(t, 2, 0) for t in (q_dec, k_dec, u, w, qk, jnp.exp(g_last)))

    def step(state, inp):
        qc, kc, uc, wc, qkc, dc = inp
        v_new = uc - jnp.einsum('bhck,bhkv->bhcv', wc, state)
        o = jnp.einsum('bhck,bhkv->bhcv', qc, state) + jnp.einsum('bhij,bhjv->bhiv', qkc, v_new)
        state = state * dc[..., None, None] + jnp.einsum('bhck,bhcv->bhkv', kc, v_new)
        return state, o

    state0 = jnp.zeros((B, H, Dk, Dv), dtype=q.dtype)
    _, o = lax.scan(step, state0, xs)
    return o.transpose(1, 0, 3, 2, 4).reshape(B, S, H, Dv)


def hybrid_mixer(h, cos, sin, w_in, q_norm_g, k_norm_g, w_attn_o, gdn_conv_w, gdn_a_log, gdn_dt_bias, gdn_norm_g, w_gdn_o, conv_dw_w, conv_dw_b, conv_ln_g, conv_ln_b, w_conv_o, w_out):
    B, S, _ = h.shape
    proj = h @ w_in
    q_a, k_a, v_a, qkv_d, beta_d, alpha_d, z_d, u_c, gate_logits = split_cols(proj, IN_SPLIT_SIZES)

    heads = (B, S, N_GROUPS, ATTN_HEADS_PER_GROUP, HEAD_DIM)
    q_a = apply_partial_rope(rms_norm(q_a.reshape(heads), q_norm_g), cos, sin)
    k_a = apply_partial_rope(rms_norm(k_a.reshape(heads), k_norm_g), cos, sin)
    v_a = v_a.reshape(heads)
    outs, lses = [], []
    for gi, (window, dilation) in enumerate(DILATED_GROUPS):
        o, lse = dilated_window_attention(q_a[:, :, gi], k_a[:, :, gi], v_a[:, :, gi], dilation, window // dilation)
        outs.append(o)
        lses.append(lse)
    wts = jax.nn.softmax(jnp.stack(lses, axis=0), axis=0)
    o_a = jnp.einsum('gbsh,gbshd->bshd', wts, jnp.stack(outs, axis=0)).astype(h.dtype)
    y_a = o_a.reshape(B, S, ATTN_HEADS_PER_GROUP * HEAD_DIM) @ w_attn_o

    qkv = jax.nn.silu(causal_depthwise_conv(qkv_d, gdn_conv_w))
    q_d, k_d, v_d = jnp.split(qkv, 3, axis=-1)
    gh = (B, S, GDN_HEADS, GDN_HEAD_DIM)
    q_d = l2_normalize(q_d.reshape(gh)) * (GDN_HEAD_DIM ** -0.5)
    k_d = l2_normalize(k_d.reshape(gh))
    v_d = v_d.reshape(gh).astype(jnp.float32)
    beta = jax.nn.sigmoid(beta_d.astype(jnp.float32))
    log_decay = -jnp.exp(gdn_a_log.astype(jnp.float32)) * jax.nn.softplus(alpha_d.astype(jnp.float32) + gdn_dt_bias.astype(jnp.float32))
    o_d = gated_delta_rule_chunked(q_d, k_d, v_d, log_decay, beta)
    o_d = rms_norm(o_d, gdn_norm_g) * jax.nn.silu(z_d.reshape(gh).astype(jnp.float32))
    y_b = o_d.reshape(B, S, GDN_WIDTH).astype(h.dtype) @ w_gdn_o

    u_a, u_b = jnp.split(u_c, 2, axis=-1)
    u = u_a * jax.nn.sigmoid(u_b)
    u = causal_depthwise_conv(u, conv_dw_w) + conv_dw_b
    u = jax.nn.silu(layer_norm(u, conv_ln_g, conv_ln_b))
    y_c = u @ w_conv_o

    g_a, g_b, g_c = jnp.split(jax.nn.sigmoid(gate_logits), 3, axis=-1)
    return (g_a * y_a + g_b * y_b + g_c * y_c) @ w_out


def swiglu_ffn(h, w_gate_up, w_down):
    a, b = jnp.split(h @ w_gate_up, 2, axis=-1)
    return (jax.nn.silu(a) * b) @ w_down


def setup_inputs(seed: int = 0) -> dict:
    key = jax.random.key(seed)
    ks = list(jax.random.split(key, 32))
    f32 = jnp.float32
    L = DEPTH
    D = D_MODEL

    def nrm(k, shape, scale):
        return scale * jax.random.normal(k, shape, f32)

    def gain(k, shape):
        return 1.0 + 0.05 * jax.random.normal(k, shape, f32)

    x = jax.random.normal(ks[0], (BATCH, SEQ, D), f32)
    c = jax.random.normal(ks[1], (BATCH, D), f32)
    offset = jax.random.randint(ks[2], (BATCH, 1), 0, 4096, dtype=jnp.int32)
    positions = offset + jnp.arange(SEQ, dtype=jnp.int32)[None, :]
    attn_out_w = ATTN_HEADS_PER_GROUP * HEAD_DIM
    return {
        'x': x,
        'c': c,
        'positions': positions,
        'mix_mod_w': nrm(ks[3], (L, D, 3 * D), D ** -0.5),
        'mix_mod_b': nrm(ks[4], (L, 3 * D), 0.02),
        'mix_norm_g': gain(ks[5], (L, D)),
        'w_in': nrm(ks[6], (L, D, N_IN), D ** -0.5),
        'q_norm_g': gain(ks[7], (L, HEAD_DIM)),
        'k_norm_g': gain(ks[8], (L, HEAD_DIM)),
        'w_attn_o': nrm(ks[9], (L, attn_out_w, D), attn_out_w ** -0.5),
        'gdn_conv_w': nrm(ks[10], (L, SHORT_CONV, 3 * GDN_WIDTH), SHORT_CONV ** -0.5),
        'gdn_a_log': jnp.log(jax.random.uniform(ks[11], (L, GDN_HEADS), f32, 1.0, 16.0)),
        'gdn_dt_bias': nrm(ks[12], (L, GDN_HEADS), 0.1),
        'gdn_norm_g': gain(ks[13], (L, GDN_HEAD_DIM)),
        'w_gdn_o': nrm(ks[14], (L, GDN_WIDTH, D), GDN_WIDTH ** -0.5),
        'conv_dw_w': nrm(ks[15], (L, CONV_K, CONV_CH), CONV_K ** -0.5),
        'conv_dw_b': nrm(ks[16], (L, CONV_CH), 0.02),
        'conv_ln_g': gain(ks[17], (L, CONV_CH)),
        'conv_ln_b': nrm(ks[18], (L, CONV_CH), 0.02),
        'w_conv_o': nrm(ks[19], (L, CONV_CH, D), CONV_CH ** -0.5),
        'w_out': nrm(ks[20], (L, D, D), D ** -0.5),
        'ffn_mod_w': nrm(ks[21], (L, D, 3 * D), D ** -0.5),
        'ffn_mod_b': nrm(ks[22], (L, 3 * D), 0.02),
        'ffn_norm_g': gain(ks[23], (L, D)),
        'w_gate_up': nrm(ks[24], (L, D, 2 * D_FF), D ** -0.5),
        'w_down': nrm(ks[25], (L, D_FF, D), D_FF ** -0.5),
    }


def reference(x, c, positions, mix_mod_w, mix_mod_b, mix_norm_g, w_in, q_norm_g, k_norm_g, w_attn_o, gdn_conv_w, gdn_a_log, gdn_dt_bias, gdn_norm_g, w_gdn_o, conv_dw_w, conv_dw_b, conv_ln_g, conv_ln_b, w_conv_o, w_out, ffn_mod_w, ffn_mod_b, ffn_norm_g, w_gate_up, w_down):
    cos, sin = rope_tables(positions)
    cos = cos[:, :, None, None, :]
    sin = sin[:, :, None, None, :]
    c_act = jax.nn.silu(c)
    for l in range(DEPTH):
        shift, scale, gate = adaln(c_act, mix_mod_w[l], mix_mod_b[l])
        h = rms_norm(x, mix_norm_g[l]) * (1.0 + scale) + shift
        y = hybrid_mixer(h, cos, sin, w_in[l], q_norm_g[l], k_norm_g[l], w_attn_o[l], gdn_conv_w[l], gdn_a_log[l], gdn_dt_bias[l], gdn_norm_g[l], w_gdn_o[l], conv_dw_w[l], conv_dw_b[l], conv_ln_g[l], conv_ln_b[l], w_conv_o[l], w_out[l])
        x = x + (gate * y).astype(x.dtype)
        shift, scale, gate = adaln(c_act, ffn_mod_w[l], ffn_mod_b[l])
        h = rms_norm(x, ffn_norm_g[l]) * (1.0 + scale) + shift
        x = x + (gate * swiglu_ffn(h, w_gate_up[l], w_down[l])).astype(x.dtype)
    return x
```

```python
import contextlib
import numpy as np
import concourse.bass as bass
import concourse.mybir as mybir
from concourse.bass_utils import run_bass_kernel_spmd

F32 = mybir.dt.float32
BF16 = mybir.dt.bfloat16
I32 = mybir.dt.int32
AF = mybir.ActivationFunctionType
ALU = mybir.AluOpType
AX = mybir.AxisListType

D = 2048
KC = D // 128
EPS = 1e-6


class Buf:
    __slots__ = ("name", "w", "r", "dsem", "dcnt", "psum")

    def __init__(self, name="", psum=False):
        self.name = name
        self.psum = psum
        self.w = None
        self.r = []
        self.dsem = None
        self.dcnt = 0


class Eng:
    def __init__(self, kb, name, h, inorder_safe=False):
        self.kb = kb
        self.name = name
        self.h = h
        self.sem = kb.new_sem("e_" + name)
        self.n = 0
        self.seen = {}
        self.inorder_safe = inorder_safe

    def _wait(self, tok):
        sem, val, eng = tok
        if self.seen.get(id(sem), 0) >= val:
            return
        self.h.wait_ge(sem, val)
        self.seen[id(sem)] = val

    def _deps(self, reads, writes):
        for b in reads:
            if b.w is not None:
                if not (b.w[2] is self and self.inorder_safe):
                    self._wait(b.w)
        for b in writes:
            if b.w is not None:
                if not (b.w[2] is self and (self.inorder_safe or b.psum)):
                    self._wait(b.w)
            for t in b.r:
                if t[2] is not self:
                    self._wait(t)

    def op(self, fn, reads=(), writes=(), signal=True):
        if any(b.psum for b in reads):
            writes = list(writes) + [b for b in reads if b.psum]
            reads = [b for b in reads if not b.psum]
        self._deps(reads, writes)
        ins = fn(self.h)
        if signal:
            self.n += 1
            ins.then_inc(self.sem, 1)
            tok = (self.sem, self.n, self)
        else:
            tok = (self.sem, self.n + 1, self)
        for b in reads:
            b.r.append(tok)
            if len(b.r) > 12:
                b.r = _prune(b.r)
        for b in writes:
            b.w = tok
            b.r = []
        return ins

    def dma(self, out, in_, reads=(), writes=(), slot=None, **kw):
        self._deps(reads, writes)
        if slot.dsem is None:
            slot.dsem = self.kb.new_sem("d_" + slot.name)
            self.kb.dma_slots.append(slot)
        ins = self.h.dma_start(out=out, in_=in_, **kw)
        slot.dcnt += 16
        ins.then_inc(slot.dsem, 16)
        tok = (slot.dsem, slot.dcnt, None)
        for b in reads:
            b.r.append(tok)
            if len(b.r) > 12:
                b.r = _prune(b.r)
        for b in writes:
            b.w = tok
            b.r = []
        return ins


def _prune(toks):
    best = {}
    for t in toks:
        k = id(t[0])
        if k not in best or best[k][1] < t[1]:
            best[k] = t
    return list(best.values())


class KB:
    def __init__(self):
        self.nc = bass.Bass("TRN2", target_bir_lowering=False)
        self.es = contextlib.ExitStack()
        self.nsem = 0
        nc = self.nc
        self.pe = Eng(self, "pe", nc.tensor, inorder_safe=True)
        self.act = Eng(self, "act", nc.scalar)
        self.dve = Eng(self, "dve", nc.vector)
        self.pool = Eng(self, "pool", nc.gpsimd)
        self.sp = Eng(self, "sp", nc.sync)
        self.out_toks = []
        self.dma_slots = []
        self.scopes = []

    def new_sem(self, name):
        self.nsem += 1
        return self.es.enter_context(self.nc.semaphore(f"{name}_{self.nsem}"))

    def push(self):
        self.scopes.append(contextlib.ExitStack())

    def pop(self):
        self.scopes.pop().close()

    def sbuf(self, name, shape, dt):
        es = self.scopes[-1] if self.scopes else self.es
        return es.enter_context(self.nc.sbuf_tensor(name, list(shape), dt))

    def psum(self, name, shape, dt):
        return self.es.enter_context(self.nc.psum_tensor(name, list(shape), dt))

    def din(self, name, shape, dt):
        return self.nc.dram_tensor(name, list(shape), dt, kind="ExternalInput").ap()

    def dout(self, name, shape, dt):
        return self.nc.dram_tensor(name, list(shape), dt, kind="ExternalOutput").ap()

    def dscratch(self, name, shape, dt):
        return self.nc.dram_tensor(name, list(shape), dt, kind="Internal").ap()

    def finish(self, bufs):
        for b in bufs:
            if b.w is not None:
                self.sp._wait(b.w)
        self.es.close()
        return self.nc


def make_ident(kb, dt, name="ident"):
    t32 = kb.sbuf(name + "32", [128, 128], F32)
    b = Buf(name)
    kb.pool.op(lambda e: e.memset(t32[:], 0.0), writes=[b])
    kb.pool.op(lambda e: e.affine_select(out=t32[:], in_=t32[:], pattern=[[-1, 128]],
                                         compare_op=ALU.not_equal, fill=1.0, base=0,
                                         channel_multiplier=1), reads=[b], writes=[b])
    if dt == F32:
        return t32, b
    t = kb.sbuf(name, [128, 128], dt)
    b2 = Buf(name + "c")
    kb.dve.op(lambda e: e.tensor_copy(out=t[:], in_=t32[:]), reads=[b], writes=[b2])
    return t, b2


def emit_norm_transpose(kb, x_rows, xnT_out, ntok, ident, identb, x_buf, xnT_buf, tag,
                        pt_banks=None):
    nc = kb.nc
    NG = ntok // 512
    xt = [kb.sbuf(f"{tag}_x{i}", [128, D], F32) for i in range(2)]
    xtb = [Buf(f"{tag}_x{i}") for i in range(2)]
    sq = kb.sbuf(f"{tag}_sq", [128, D], BF16)
    sqb = Buf(f"{tag}_sq")
    ss = [kb.sbuf(f"{tag}_ss{i}", [128, 1], F32) for i in range(2)]
    ssb = [Buf(f"{tag}_ss{i}") for i in range(2)]
    xn = [kb.sbuf(f"{tag}_xn{i}", [128, D], BF16) for i in range(2)]
    xnb = [Buf(f"{tag}_xn{i}") for i in range(2)]
    xT = [kb.sbuf(f"{tag}_xT{i}", [128, KC, 512], BF16) for i in range(2)]
    xTb = [Buf(f"{tag}_xT{i}") for i in range(2)]
    epsc = kb.sbuf(f"{tag}_eps", [128, 1], F32)
    epsb = Buf(f"{tag}_eps")
    kb.pool.op(lambda e: e.memset(epsc[:], EPS), writes=[epsb])
    if pt_banks is None:
        pt_banks = [(kb.psum(f"{tag}_pt{i}", [128, 8, 128], BF16), Buf(f"{tag}_pt{i}", psum=True)) for i in range(2)]
    xo = xnT_out.rearrange("(k p) t -> p k t", p=128)
    it = 0
    for g in range(NG):
        gs = g % 2
        for j in range(4):
            s = it % 2
            it += 1
            r0 = g * 512 + j * 128
            kb.sp.dma(xt[s][:], x_rows[r0:r0 + 128, :], reads=[x_buf], writes=[xtb[s]], slot=xtb[s])
            kb.act.op(lambda e: e.activation(out=sq[:], in_=xt[s][:], func=AF.Square, accum_out=ss[s][:]),
                      reads=[xtb[s]], writes=[sqb, ssb[s]])
            kb.act.op(lambda e: e.activation(out=ss[s][:], in_=ss[s][:], func=AF.Sqrt, scale=1.0 / D, bias=epsc[:]),
                      reads=[ssb[s], epsb], writes=[ssb[s]])
            kb.dve.op(lambda e: e.reciprocal(out=ss[s][:], in_=ss[s][:]), reads=[ssb[s]], writes=[ssb[s]])
            kb.act.op(lambda e: e.activation(out=xn[s][:], in_=xt[s][:], func=AF.Copy, scale=ss[s][:]),
                      reads=[xtb[s], ssb[s]], writes=[xnb[s]])
            for half in range(2):
                pt, ptb = pt_banks[half]
                for kk in range(8):
                    k = half * 8 + kk
                    kb.pe.op(lambda e: e.transpose(pt[:, kk, :], xn[s][:, k * 128:(k + 1) * 128], ident[:]),
                             reads=[xnb[s], identb], writes=[ptb], signal=(kk == 7))
                eng = kb.dve if half == 0 else kb.act
                if half == 0:
                    kb.dve.op(lambda e: e.tensor_copy(out=xT[gs][:, 0:8, j * 128:(j + 1) * 128], in_=pt[:]),
                              reads=[ptb], writes=[xTb[gs]])
                else:
                    kb.act.op(lambda e: e.copy(out=xT[gs][:, 8:16, j * 128:(j + 1) * 128], in_=pt[:]),
                              reads=[ptb], writes=[xTb[gs]])
        kb.sp.dma(xo[:, :, g * 512:(g + 1) * 512], xT[gs][:], reads=[xTb[gs]], writes=[xnT_buf], slot=xTb[gs])


def build_p0(ntok):
    kb = KB()
    x = kb.din("x", [ntok, D], F32)
    c = kb.din("c", [2, D], F32)
    wm = kb.din("wm", [4, D, 768], F32)
    bm = kb.din("bm", [4, 768], F32)
    xnT = kb.dout("xnT", [D, ntok], BF16)
    modo = kb.dout("modo", [4, 2, 768], F32)
    xb, xnTb, modob = Buf("x"), Buf("xnT"), Buf("modo")
    ident, identb = make_ident(kb, BF16)

    cT = kb.sbuf("cT", [128, KC, 2], F32)
    cTb = Buf("cT")
    for b in range(2):
        kb.sp.dma(cT[:, :, b], c[b].rearrange("(k p) -> p k", p=128), writes=[cTb], slot=cTb,
                  allow_slow_non_contiguous=True)
    kb.act.op(lambda e: e.activation(out=cT[:], in_=cT[:], func=AF.Silu), reads=[cTb], writes=[cTb])
    wsl = [kb.sbuf(f"wm{i}", [128, KC, 768], F32) for i in range(2)]
    wslb = [Buf(f"wm{i}") for i in range(2)]
    bsl = kb.sbuf("bsl", [2, 4, 768], F32)
    bslb = Buf("bsl")
    for b in range(2):
        kb.sp.dma(bsl[b:b + 1, :, :], bm[None, :, :], writes=[bslb], slot=bslb)
    mo = kb.sbuf("mo", [2, 4, 768], F32)
    mob = Buf("mo")
    pm = [(kb.psum(f"pm{i}", [128, 512], F32), Buf(f"pm{i}", psum=True)) for i in range(2)]
    for m in range(4):
        s = m % 2
        kb.sp.dma(wsl[s][:], wm[m].rearrange("(k p) n -> p k n", p=128), writes=[wslb[s]], slot=wslb[s])
        for hf in range(2):
            p_, pb = pm[hf]
            for k in range(KC):
                kb.pe.op(lambda e: e.matmul(p_[0:2, 0:384], lhsT=cT[:, k, :], rhs=wsl[s][:, k, hf * 384:(hf + 1) * 384],
                                            start=(k == 0), stop=(k == KC - 1)),
                         reads=[cTb, wslb[s]], writes=[pb], signal=(k == KC - 1))
            kb.dve.op(lambda e: e.tensor_tensor(out=mo[:, m, hf * 384:(hf + 1) * 384], in0=p_[0:2, 0:384],
                                                in1=bsl[:, m, hf * 384:(hf + 1) * 384], op=ALU.add),
                      reads=[pb, bslb], writes=[mob])
    kb.sp.dma(modo.rearrange("m b n -> b m n"), mo[:], reads=[mob], writes=[modob], slot=mob)

    emit_norm_transpose(kb, x, xnT, ntok, ident, identb, xb, xnTb, "nt")
    return kb.finish([xnTb, modob])


class WStream:
    def __init__(self, kb, nslots=3, tag="ws"):
        self.kb = kb
        self.t = [kb.sbuf(f"{tag}{i}", [128, 16, 512], BF16) for i in range(nslots)]
        self.b = [Buf(f"{tag}{i}") for i in range(nslots)]
        self.i = 0

    def load(self, w, wbuf, k0, nk, c0, ncol=512):
        s = self.i % len(self.t)
        self.i += 1
        src = w[k0 * 128:(k0 + nk) * 128, c0:c0 + ncol].rearrange("(k p) n -> p k n", p=128)
        self.kb.sp.dma(self.t[s][:, 0:nk, 0:ncol], src, reads=[wbuf], writes=[self.b[s]], slot=self.b[s])
        return self.t[s], self.b[s]


def cast_weight(kb, dst, src, buf, rows_per=128):
    K = src.shape[0]
    for r in range(0, K, rows_per):
        kb.pool.dma(dst[r:r + rows_per, :], src[r:r + rows_per, :], writes=[buf], slot=buf)


def load_pk(kb, dst, src_vec, buf, nk):
    kb.sp.dma(dst, src_vec.rearrange("(k p) -> p k", p=128), writes=[buf], slot=buf,
              allow_slow_non_contiguous=True)


class NormT:
    def __init__(self, kb, tag, ident, identb, pt_banks):
        self.kb = kb
        self.ident, self.identb = ident, identb
        self.xt = [kb.sbuf(f"{tag}_x{i}", [128, D], F32) for i in range(2)]
        self.xtb = [Buf(f"{tag}_x{i}") for i in range(2)]
        self.ss = [kb.sbuf(f"{tag}_ss{i}", [128, 1], F32) for i in range(2)]
        self.ssb = [Buf(f"{tag}_ss{i}") for i in range(2)]
        self.xn = [kb.sbuf(f"{tag}_xn{i}", [128, D], BF16) for i in range(2)]
        self.xnb = [Buf(f"{tag}_xn{i}") for i in range(2)]
        self.epsc = kb.sbuf(f"{tag}_eps", [128, 1], F32)
        self.epsb = Buf(f"{tag}_eps")
        kb.pool.op(lambda e: e.memset(self.epsc[:], EPS), writes=[self.epsb])
        self.pt = pt_banks
        self.it = 0

    def tile(self, src_rows, src_buf, dstT, dstTb, col0):
        kb = self.kb
        s = self.it % 2
        self.it += 1
        xt, xtb, ss, ssb, xn, xnb = self.xt[s], self.xtb[s], self.ss[s], self.ssb[s], self.xn[s], self.xnb[s]
        kb.sp.dma(xt[:], src_rows, reads=[src_buf], writes=[xtb], slot=xtb)
        kb.act.op(lambda e: e.activation(out=xn[:], in_=xt[:], func=AF.Square, accum_out=ss[:]),
                  reads=[xtb], writes=[xnb, ssb])
        kb.act.op(lambda e: e.activation(out=ss[:], in_=ss[:], func=AF.Sqrt, scale=1.0 / D, bias=self.epsc[:]),
                  reads=[ssb, self.epsb], writes=[ssb])
        kb.dve.op(lambda e: e.reciprocal(out=ss[:], in_=ss[:]), reads=[ssb], writes=[ssb])
        kb.act.op(lambda e: e.activation(out=xn[:], in_=xt[:], func=AF.Copy, scale=ss[:]),
                  reads=[xtb, ssb], writes=[xnb])
        for half in range(2):
            pt, ptb = self.pt[half]
            for kk in range(8):
                k = half * 8 + kk
                kb.pe.op(lambda e: e.transpose(pt[:, kk, :], xn[:, k * 128:(k + 1) * 128], self.ident[:]),
                         reads=[xnb, self.identb], writes=[ptb], signal=(kk == 7))
            if half == 0:
                kb.dve.op(lambda e: e.tensor_copy(out=dstT[:, 0:8, col0:col0 + 128], in_=pt[:]),
                          reads=[ptb], writes=[dstTb])
            else:
                kb.act.op(lambda e: e.copy(out=dstT[:, 8:16, col0:col0 + 128], in_=pt[:]),
                          reads=[ptb], writes=[dstTb])


TG = 512
DFF = 5632
FC = DFF // 128
CCH = 1024
HALO = 32


def emit_p2(kb, ntok, a, ident, identb):
    nc = kb.nc
    NTG = ntok // TG
    PB = [(kb.psum(f"pb{i}", [128, 512], F32), Buf(f"pb{i}", psum=True)) for i in range(6)]
    PT = [(kb.psum(f"ptb{i}", [128, 8, 128], BF16), Buf(f"ptb{i}", psum=True)) for i in range(2)]
    wnames = ["w_uc", "w_gl", "w_ao", "w_go", "w_co", "w_out", "w_gu", "w_dn"]
    wb = {}
    for n in wnames:
        src = a[n]
        dst = kb.dscratch(n + "_bf", list(src.shape), BF16)
        b = Buf(n + "_bf")
        cast_weight(kb, dst, src, b)
        wb[n] = (dst, b)
    ws = WStream(kb, 3)
    modv = a["mod"]
    pv = kb.sbuf("pv", [128, 8, KC], F32)
    pvb = Buf("pv")
    load_pk(kb, pv[:, 0, :], modv[0, 0:D], pvb, KC)
    load_pk(kb, pv[:, 1, :], modv[0, D:2 * D], pvb, KC)
    load_pk(kb, pv[:, 2, :], a["g_mix"], pvb, KC)
    load_pk(kb, pv[:, 3, :], modv[1, 0:D], pvb, KC)
    load_pk(kb, pv[:, 4, :], modv[1, D:2 * D], pvb, KC)
    load_pk(kb, pv[:, 5, :], a["g_ffn"], pvb, KC)
    kb.dve.op(lambda e: e.scalar_tensor_tensor(out=pv[:, 6, :], in0=pv[:, 1, :], scalar=1.0, in1=pv[:, 2, :],
                                               op0=ALU.add, op1=ALU.mult), reads=[pvb], writes=[pvb])
    kb.dve.op(lambda e: e.scalar_tensor_tensor(out=pv[:, 7, :], in0=pv[:, 4, :], scalar=1.0, in1=pv[:, 5, :],
                                               op0=ALU.add, op1=ALU.mult), reads=[pvb], writes=[pvb])
    gbc = kb.sbuf("gbc", [128, 2, D], F32)
    gbcb = Buf("gbc")
    for i in range(2):
        kb.sp.dma(gbc[:, i, :], modv[i, 2 * D:3 * D].partition_broadcast(128), writes=[gbcb], slot=gbcb)
    cw = kb.sbuf("cw", [128, 8, 31], F32)
    cp = kb.sbuf("cp", [128, 3, 8], F32)
    cwb = Buf("cw")
    for c in range(8):
        kb.sp.dma(cw[:, c, :], a["conv_w"][:, c * 128:(c + 1) * 128].rearrange("k p -> p k"), writes=[cwb],
                  slot=cwb, allow_slow_non_contiguous=True)
    load_pk(kb, cp[:, 0, :], a["conv_b"], cwb, 8)
    load_pk(kb, cp[:, 1, :], a["ln_g"], cwb, 8)
    load_pk(kb, cp[:, 2, :], a["ln_b"], cwb, 8)
    flag = kb.sbuf("flag", [128, 1], F32)
    kb.sp.dma(flag[:], a["halo_flag"], writes=[cwb], slot=cwb)
    ones32 = kb.sbuf("ones32", [128, 128], F32)
    onesb = Buf("ones32")
    kb.pool.op(lambda e: e.memset(ones32[:], 1.0), writes=[onesb])
    epsc = kb.sbuf("epsc", [128, 1], F32)
    kb.pool.op(lambda e: e.memset(epsc[:], EPS), writes=[onesb])

    hT = kb.sbuf("hT", [128, KC, TG], BF16)
    hTb = Buf("hT")
    hh = kb.sbuf("hh", [128, KC, HALO], BF16)
    hhb = Buf("hh")
    ubuf = kb.sbuf("ubuf", [128, 8, HALO + TG], F32)
    ub = [Buf(f"ub{c}") for c in range(8)]
    acc = kb.sbuf("acc", [128, 8, TG], F32)
    accb = [Buf(f"acc{c}") for c in range(8)]
    sq = [kb.sbuf(f"sq{i}", [128, TG], F32) for i in range(2)]
    sqb = [Buf(f"sq{i}") for i in range(2)]
    sg = [kb.sbuf(f"sg{i}", [128, TG], F32) for i in range(2)]
    sgb = [Buf(f"sg{i}") for i in range(2)]
    sgh = kb.sbuf("sgh", [128, HALO], F32)
    sghb = Buf("sgh")
    mean = kb.sbuf("mean", [128, TG], F32)
    rstd = kb.sbuf("rstd", [128, TG], F32)
    msq = kb.sbuf("msq", [128, TG], F32)
    statb = Buf("stat")
    big = kb.sbuf("big", [128, FC, TG], BF16)
    bigb = Buf("big")
    cT = big[:, 0:8, :]
    oa = big[:, 8:12, :]
    od = big[:, 12:20, :]
    mg = big[:, 20:36, :]
    macc = acc[:, 0:4, :]
    maccb = accb[0:4]
    tmp = sq
    tmpb = sqb
    xp = [kb.sbuf(f"xp{i}", [128, 512], F32) for i in range(2)]
    xpb = [Buf(f"xp{i}") for i in range(2)]
    nt = NormT(kb, "nt", ident, identb, PT)
    xTo, xTob = hT, hTb

    x_rows, xb_in = a["x"], a["x_buf"]
    xout, xoutb = a["xout"], a["xout_buf"]
    xnT_h, xnTb_in = a["xnT_h"], a["xnT_h_buf"]
    xnT_o, xnTob = a["xnT_out"], a["xnT_out_buf"]
    xnT_v = xnT_h.rearrange("(k p) t -> p k t", p=128)
    xnTo_v = xnT_o.rearrange("(k p) t -> p k t", p=128)
    oaT_v = a["o_aT"].rearrange("(k p) t -> p k t", p=128)
    odT_v = a["o_dT"].rearrange("(k p) t -> p k t", p=128)
    cnt = {"pb": 0, "sg": 0, "tmp": 0, "xp": 0}

    def rot(name, n=2):
        v = cnt[name] % n
        cnt[name] += 1
        return v

    for g in range(NTG):
        t0 = g * TG
        kb.sp.dma(hT[:], xnT_v[:, :, HALO + t0:HALO + t0 + TG], reads=[xnTb_in], writes=[hTb], slot=hTb)
        for k in range(KC):
            kb.act.op(lambda e: e.activation(out=hT[:, k, :], in_=hT[:, k, :], func=AF.Identity,
                                             scale=pv[:, 6, k:k + 1], bias=pv[:, 0, k:k + 1]),
                      reads=[hTb, pvb], writes=[hTb])
        if g == 0:
            kb.sp.dma(hh[:], xnT_v[:, :, 0:HALO], reads=[xnTb_in], writes=[hhb], slot=hhb)
            for k in range(KC):
                kb.act.op(lambda e: e.activation(out=hh[:, k, :], in_=hh[:, k, :], func=AF.Identity,
                                                 scale=pv[:, 6, k:k + 1], bias=pv[:, 0, k:k + 1]),
                          reads=[hhb, pvb], writes=[hhb])
        else:
            for c in range(8):
                kb.pool.op(lambda e: e.tensor_copy(out=ubuf[:, c, 0:HALO], in_=ubuf[:, c, TG:TG + HALO]),
                           reads=[ub[c]], writes=[ub[c]])
        w_uc, w_ucb = wb["w_uc"]
        for cg in range(4):
            wt, wtb = ws.load(w_uc, w_ucb, 0, KC, cg * 512)
            for cc in range(4):
                c = cg * 4 + cc
                p_, pb_ = PB[rot("pb")]
                for k in range(KC):
                    kb.pe.op(lambda e: e.matmul(p_[:], lhsT=wt[:, k, cc * 128:(cc + 1) * 128], rhs=hT[:, k, :],
                                                start=(k == 0), stop=(k == KC - 1)),
                             reads=[wtb, hTb], writes=[pb_], signal=(k == KC - 1))
                if g == 0:
                    ph, phb = PB[4 + (c % 2)]
                    for k in range(KC):
                        kb.pe.op(lambda e: e.matmul(ph[:, 0:HALO], lhsT=wt[:, k, cc * 128:(cc + 1) * 128],
                                                    rhs=hh[:, k, :], start=(k == 0), stop=(k == KC - 1)),
                                 reads=[wtb, hhb], writes=[phb], signal=(k == KC - 1))
                if c < 8:
                    kb.act.op(lambda e: e.copy(out=ubuf[:, c, HALO:], in_=p_[:]), reads=[pb_], writes=[ub[c]])
                    if g == 0:
                        kb.act.op(lambda e: e.copy(out=ubuf[:, c, 0:HALO], in_=ph[:, 0:HALO]), reads=[phb],
                                  writes=[ub[c]])
                else:
                    s_ = rot("sg")
                    kb.act.op(lambda e: e.activation(out=sg[s_][:], in_=p_[:], func=AF.Sigmoid), reads=[pb_],
                              writes=[sgb[s_]])
                    kb.dve.op(lambda e: e.tensor_tensor(out=ubuf[:, c - 8, HALO:], in0=ubuf[:, c - 8, HALO:],
                                                        in1=sg[s_][:], op=ALU.mult),
                              reads=[sgb[s_], ub[c - 8]], writes=[ub[c - 8]])
                    if g == 0:
                        kb.act.op(lambda e: e.activation(out=sgh[:], in_=ph[:, 0:HALO], func=AF.Sigmoid),
                                  reads=[phb], writes=[sghb])
                        kb.dve.op(lambda e: e.scalar_tensor_tensor(out=ubuf[:, c - 8, 0:HALO], in0=sgh[:],
                                                                   scalar=flag[:, 0:1], in1=ubuf[:, c - 8, 0:HALO],
                                                                   op0=ALU.mult, op1=ALU.mult),
                                  reads=[sghb, ub[c - 8], cwb], writes=[ub[c - 8]])
        OFF = HALO - 30
        eng_of = [kb.dve] * 8
        for k in range(31):
            for c in range(8):
                eng = eng_of[c]
                src = ubuf[:, c, OFF + k:OFF + k + TG]
                if k == 0:
                    eng.op(lambda e: e.tensor_scalar(out=acc[:, c, :], in0=src, scalar1=cw[:, c, 0:1],
                                                     scalar2=cp[:, 0, c:c + 1], op0=ALU.mult, op1=ALU.add),
                           reads=[ub[c], cwb], writes=[accb[c]])
                else:
                    eng.op(lambda e: e.scalar_tensor_tensor(out=acc[:, c, :], in0=src, scalar=cw[:, c, k:k + 1],
                                                            in1=acc[:, c, :], op0=ALU.mult, op1=ALU.add),
                           reads=[ub[c], cwb, accb[c]], writes=[accb[c]])
        (p1, p1b), (p2, p2b) = PB[2], PB[3]
        for c in range(8):
            s_ = rot("tmp")
            kb.act.op(lambda e: e.activation(out=sq[s_][:], in_=acc[:, c, :], func=AF.Square), reads=[accb[c]],
                      writes=[sqb[s_]])
            kb.pe.op(lambda e: e.matmul(p1[:], lhsT=ones32[:], rhs=acc[:, c, :], start=(c == 0), stop=(c == 7)),
                     reads=[onesb, accb[c]], writes=[p1b])
            kb.pe.op(lambda e: e.matmul(p2[:], lhsT=ones32[:], rhs=sq[s_][:], start=(c == 0), stop=(c == 7)),
                     reads=[onesb, sqb[s_]], writes=[p2b])
        kb.act.op(lambda e: e.activation(out=mean[:], in_=p1[:], func=AF.Copy, scale=1.0 / CCH), reads=[p1b],
                  writes=[statb])
        kb.dve.op(lambda e: e.tensor_tensor(out=msq[:], in0=mean[:], in1=mean[:], op=ALU.mult), reads=[statb],
                  writes=[statb])
        kb.dve.op(lambda e: e.scalar_tensor_tensor(out=rstd[:], in0=p2[:], scalar=1.0 / CCH, in1=msq[:],
                                                   op0=ALU.mult, op1=ALU.subtract), reads=[p2b, statb],
                  writes=[statb])
        kb.act.op(lambda e: e.activation(out=rstd[:], in_=rstd[:], func=AF.Sqrt, bias=epsc[:]),
                  reads=[statb, onesb], writes=[statb])
        kb.dve.op(lambda e: e.reciprocal(out=rstd[:], in_=rstd[:]), reads=[statb], writes=[statb])
        for c in range(8):
            kb.dve.op(lambda e: e.tensor_tensor(out=acc[:, c, :], in0=acc[:, c, :], in1=mean[:], op=ALU.subtract),
                      reads=[accb[c], statb], writes=[accb[c]])
            kb.pool.op(lambda e: e.tensor_tensor(out=acc[:, c, :], in0=acc[:, c, :], in1=rstd[:], op=ALU.mult),
                       reads=[accb[c], statb], writes=[accb[c]])
            kb.act.op(lambda e: e.activation(out=cT[:, c, :], in_=acc[:, c, :], func=AF.Silu,
                                             scale=cp[:, 1, c:c + 1], bias=cp[:, 2, c:c + 1]),
                      reads=[accb[c], cwb], writes=[bigb])
        kb.sp.dma(oa, oaT_v[:, :, t0:t0 + TG], reads=[a["o_buf"]], writes=[bigb], slot=bigb)
        kb.sp.dma(od, odT_v[:, :, t0:t0 + TG], reads=[a["o_buf"]], writes=[bigb], slot=bigb)
        w_gl, w_glb = wb["w_gl"]
        ywl = [(wb["w_ao"], oa, 4), (wb["w_go"], od, 8), (wb["w_co"], cT, 8)]
        for og in range(4):
            for br in range(3):
                gw, gwb = ws.load(w_gl, w_glb, 0, KC, br * D + og * 512)
                (yw_d, yw_db), ysrc, ynk = ywl[br]
                yw, ywb = ws.load(yw_d, yw_db, 0, ynk, og * 512)
                for cc in range(4):
                    pg, pgb = PB[rot("pb")]
                    for k in range(KC):
                        kb.pe.op(lambda e: e.matmul(pg[:], lhsT=gw[:, k, cc * 128:(cc + 1) * 128], rhs=hT[:, k, :],
                                                    start=(k == 0), stop=(k == KC - 1)),
                                 reads=[gwb, hTb], writes=[pgb], signal=(k == KC - 1))
                    py, pyb = PB[2 + (cnt["pb"] % 2)]
                    for k in range(ynk):
                        kb.pe.op(lambda e: e.matmul(py[:], lhsT=yw[:, k, cc * 128:(cc + 1) * 128], rhs=ysrc[:, k, :],
                                                    start=(k == 0), stop=(k == ynk - 1)),
                                 reads=[ywb, bigb], writes=[pyb], signal=(k == ynk - 1))
                    s_ = rot("sg")
                    kb.act.op(lambda e: e.activation(out=sg[s_][:], in_=pg[:], func=AF.Sigmoid), reads=[pgb],
                              writes=[sgb[s_]])
                    if br == 0:
                        kb.dve.op(lambda e: e.tensor_tensor(out=macc[:, cc, :], in0=sg[s_][:], in1=py[:], op=ALU.mult),
                                  reads=[sgb[s_], pyb], writes=[maccb[cc]])
                    else:
                        t_ = rot("tmp")
                        kb.dve.op(lambda e: e.tensor_tensor(out=tmp[t_][:], in0=sg[s_][:], in1=py[:], op=ALU.mult),
                                  reads=[sgb[s_], pyb], writes=[tmpb[t_]])
                        if br == 1:
                            kb.pool.op(lambda e: e.tensor_tensor(out=macc[:, cc, :], in0=macc[:, cc, :], in1=tmp[t_][:],
                                                                 op=ALU.add), reads=[maccb[cc], tmpb[t_]],
                                       writes=[maccb[cc]])
                        else:
                            kb.pool.op(lambda e: e.tensor_tensor(out=mg[:, og * 4 + cc, :], in0=macc[:, cc, :],
                                                                 in1=tmp[t_][:], op=ALU.add),
                                       reads=[maccb[cc], tmpb[t_]], writes=[bigb])
        w_o, w_ob = wb["w_out"]
        for fg in range(4):
            wt, wtb = ws.load(w_o, w_ob, 0, KC, fg * 512)
            for j in range(4):
                r0 = t0 + j * 128
                p_, pb_ = PB[4 + rot("pb")]
                for k in range(KC):
                    kb.pe.op(lambda e: e.matmul(p_[:], lhsT=mg[:, k, j * 128:(j + 1) * 128], rhs=wt[:, k, :],
                                                start=(k == 0), stop=(k == KC - 1)),
                             reads=[wtb, bigb], writes=[pb_], signal=(k == KC - 1))
                x_ = rot("xp")
                kb.sp.dma(xp[x_][:], x_rows[r0:r0 + 128, fg * 512:(fg + 1) * 512], reads=[xb_in], writes=[xpb[x_]],
                          slot=xpb[x_])
                t_ = rot("tmp")
                kb.dve.op(lambda e: e.tensor_tensor(out=tmp[t_][:], in0=p_[:], in1=gbc[:, 0, fg * 512:(fg + 1) * 512],
                                                    op=ALU.mult), reads=[pb_, gbcb], writes=[tmpb[t_]])
                kb.pool.op(lambda e: e.tensor_tensor(out=xp[x_][:], in0=xp[x_][:], in1=tmp[t_][:], op=ALU.add),
                           reads=[xpb[x_], tmpb[t_]], writes=[xpb[x_]])
                kb.sp.dma(xout[r0:r0 + 128, fg * 512:(fg + 1) * 512], xp[x_][:], reads=[xpb[x_]], writes=[xoutb],
                          slot=xpb[x_])
        for j in range(4):
            r0 = t0 + j * 128
            nt.tile(xout[r0:r0 + 128, :], xoutb, hT, hTb, j * 128)
        for k in range(KC):
            kb.act.op(lambda e: e.activation(out=hT[:, k, :], in_=hT[:, k, :], func=AF.Identity,
                                             scale=pv[:, 7, k:k + 1], bias=pv[:, 3, k:k + 1]),
                      reads=[hTb, pvb], writes=[hTb])
        w_gu, w_gub = wb["w_gu"]
        for t in range(FC // 4):
            wa, wab = ws.load(w_gu, w_gub, 0, KC, t * 512)
            wv, wvb = ws.load(w_gu, w_gub, 0, KC, DFF + t * 512)
            for cc in range(4):
                f = t * 4 + cc
                pa, pab = PB[rot("pb")]
                for k in range(KC):
                    kb.pe.op(lambda e: e.matmul(pa[:], lhsT=wa[:, k, cc * 128:(cc + 1) * 128], rhs=hT[:, k, :],
                                                start=(k == 0), stop=(k == KC - 1)),
                             reads=[wab, hTb], writes=[pab], signal=(k == KC - 1))
                pv_, pvb_ = PB[2 + (cnt["pb"] % 2)]
                for k in range(KC):
                    kb.pe.op(lambda e: e.matmul(pv_[:], lhsT=wv[:, k, cc * 128:(cc + 1) * 128], rhs=hT[:, k, :],
                                                start=(k == 0), stop=(k == KC - 1)),
                             reads=[wvb, hTb], writes=[pvb_], signal=(k == KC - 1))
                s_ = rot("sg")
                kb.act.op(lambda e: e.activation(out=sg[s_][:], in_=pa[:], func=AF.Silu), reads=[pab],
                          writes=[sgb[s_]])
                kb.dve.op(lambda e: e.tensor_tensor(out=big[:, f, :], in0=sg[s_][:], in1=pv_[:], op=ALU.mult),
                          reads=[sgb[s_], pvb_], writes=[bigb])
        w_dn, w_dnb = wb["w_dn"]
        parts = [(0, 16), (16, 16), (32, 12)]
        for fg in range(4):
            for pi, (k0, nk) in enumerate(parts):
                wt, wtb = ws.load(w_dn, w_dnb, k0, nk, fg * 512)
                for j in range(4):
                    p_, pb_ = PB[j]
                    for kk in range(nk):
                        kb.pe.op(lambda e: e.matmul(p_[:], lhsT=big[:, k0 + kk, j * 128:(j + 1) * 128],
                                                    rhs=wt[:, kk, :], start=(pi == 0 and kk == 0),
                                                    stop=(pi == 2 and kk == nk - 1)),
                                 reads=[wtb, bigb], writes=[pb_], signal=(kk == nk - 1))
            for j in range(4):
                r0 = t0 + j * 128
                p_, pb_ = PB[j]
                x_ = rot("xp")
                kb.sp.dma(xp[x_][:], xout[r0:r0 + 128, fg * 512:(fg + 1) * 512], reads=[xoutb], writes=[xpb[x_]],
                          slot=xpb[x_])
                t_ = rot("tmp")
                kb.dve.op(lambda e: e.tensor_tensor(out=tmp[t_][:], in0=p_[:], in1=gbc[:, 1, fg * 512:(fg + 1) * 512],
                                                    op=ALU.mult), reads=[pb_, gbcb], writes=[tmpb[t_]])
                kb.pool.op(lambda e: e.tensor_tensor(out=xp[x_][:], in0=xp[x_][:], in1=tmp[t_][:], op=ALU.add),
                           reads=[xpb[x_], tmpb[t_]], writes=[xpb[x_]])
                kb.sp.dma(xout[r0:r0 + 128, fg * 512:(fg + 1) * 512], xp[x_][:], reads=[xpb[x_]], writes=[xoutb],
                          slot=xpb[x_])
        for j in range(4):
            r0 = t0 + j * 128
            nt.tile(xout[r0:r0 + 128, :], xoutb, xTo, xTob, j * 128)
        kb.sp.dma(xnTo_v[:, :, t0:t0 + TG], xTo[:], reads=[xTob], writes=[xnTob], slot=xTob)


def build_p2(ntok):
    kb = KB()
    a = {}
    a["x"] = kb.din("x", [ntok, D], F32)
    a["xnT_h"] = kb.din("xnT_h", [D, HALO + ntok], BF16)
    a["halo_flag"] = kb.din("halo_flag", [128, 1], F32)
    a["o_aT"] = kb.din("o_aT", [512, ntok], BF16)
    a["o_dT"] = kb.din("o_dT", [1024, ntok], BF16)
    a["mod"] = kb.din("mod", [2, 3 * D], F32)
    a["g_mix"] = kb.din("g_mix", [D], F32)
    a["g_ffn"] = kb.din("g_ffn", [D], F32)
    a["w_uc"] = kb.din("w_uc", [D, 2048], F32)
    a["w_gl"] = kb.din("w_gl", [D, 3 * D], F32)
    a["w_ao"] = kb.din("w_ao", [512, D], F32)
    a["w_go"] = kb.din("w_go", [1024, D], F32)
    a["w_co"] = kb.din("w_co", [1024, D], F32)
    a["w_out"] = kb.din("w_out", [D, D], F32)
    a["w_gu"] = kb.din("w_gu", [D, 2 * DFF], F32)
    a["w_dn"] = kb.din("w_dn", [DFF, D], F32)
    a["conv_w"] = kb.din("conv_w", [31, CCH], F32)
    a["conv_b"] = kb.din("conv_b", [CCH], F32)
    a["ln_g"] = kb.din("ln_g", [CCH], F32)
    a["ln_b"] = kb.din("ln_b", [CCH], F32)
    a["xout"] = kb.dout("xout", [ntok, D], F32)
    a["xnT_out"] = kb.dout("xnT_out", [D, ntok], BF16)
    for n in ["x_buf", "xnT_h_buf", "o_buf", "xout_buf", "xnT_out_buf"]:
        a[n] = Buf(n)
    ident, identb = make_ident(kb, BF16)
    emit_p2(kb, ntok, a, ident, identb)
    return kb.finish([a["xout_buf"], a["xnT_out_buf"]])


DBG = set()
NTOKC = 768 + 384 + 256 + 4
C_U2, C_SL2, C_L2, C_IA, C_IB, C_MP, C_MC, C_ONE, C_MISC = range(9)
TWO_PI = 6.283185307179586
CW1 = 6.28125
CW2 = TWO_PI - CW1


def host_consts():
    c = np.zeros((128, 9, 128), np.float32)
    m = np.arange(128)[:, None]
    i = np.arange(128)[None, :]
    same = (m // 64) == (i // 64)
    c[:, C_U2] = (m <= i) & same
    c[:, C_SL2] = (m > i) & same
    c[:, C_L2] = (m >= i) & same
    c[:, C_IA] = (m < 64) * np.ones((1, 128))
    c[:, C_IB] = (m >= 64) * np.ones((1, 128))
    c[:, C_MP] = (m >= i)
    c[:, C_MC] = (m <= i)
    c[:, C_ONE] = 1.0
    inv = (500000.0 ** (-np.arange(0, 32, 2, dtype=np.float32) / 32)).astype(np.float32)
    c[:, C_MISC, 0:16] = inv[None, :]
    return c


class PRegion:
    def __init__(self, kb):
        self.banks = [kb.psum(f"bank{i}", [128, 512], F32) for i in range(8)]

    def reg(self, bank, r0, nr=1):
        return self.banks[bank][:, r0 * 128:(r0 + nr) * 128]


def barrier(kb, extra_bufs=()):
    engs = [kb.pe, kb.act, kb.dve, kb.pool, kb.sp]
    for e in engs:
        for f in engs:
            if f is not e and f.n > 0:
                e._wait((f.sem, f.n, f))
        for b in kb.dma_slots:
            if b.dsem is not None and b.dcnt > 0:
                e._wait((b.dsem, b.dcnt, None))


def emit_p1(kb, S, a, phases="ABC"):
    nc = kb.nc
    NT = S // 128
    NG = S // 512
    NU = S // 2048
    PR = PRegion(kb)
    bankb = [Buf(f"bank{i}", psum=True) for i in range(8)]
    qk_tok = kb.dscratch("qk_tok", [S, 6, 128], BF16)
    v_tok = kb.dscratch("v_tok", [S, 3, 128], BF16)
    zs_tok = kb.dscratch("zs_tok", [S, 256], F32)
    gT = kb.dout("gT", [6, 128, S], F32) if 'gtout' in DBG else kb.dscratch("gT", [6, 128, S], F32)
    qkb, vtb, zsb, gTb = Buf("qk_tok"), Buf("v_tok"), Buf("zs_tok"), Buf("gT")
    cst = kb.sbuf("cst", [128, 9, 128], F32)
    cstb = Buf("cst")
    kb.sp.dma(cst[:], a["consts"], writes=[cstb], slot=cstb)
    ident32 = kb.sbuf("ident32", [128, 128], F32)
    identb32 = Buf("ident32b")
    identbf = kb.sbuf("identbf", [128, 128], BF16)
    kb.pool.op(lambda e: e.memset(ident32[:], 0.0), writes=[identb32])
    kb.pool.op(lambda e: e.affine_select(out=ident32[:], in_=ident32[:], pattern=[[-1, 128]],
                                         compare_op=ALU.not_equal, fill=1.0, base=0, channel_multiplier=1),
               reads=[identb32], writes=[identb32])
    kb.dve.op(lambda e: e.tensor_copy(out=identbf[:], in_=ident32[:]), reads=[identb32], writes=[identb32])
    GB = kb.sbuf("GB", [128, NT, 8], F32)
    GBb = Buf("GB")
    epsc = kb.sbuf("epsc", [128, 1], F32)
    onec = kb.sbuf("onec", [128, 1], F32)
    kb.pool.op(lambda e: e.memset(epsc[:], EPS), writes=[cstb])
    kb.pool.op(lambda e: e.memset(onec[:], 1.0), writes=[cstb])
    sl1 = kb.sbuf("sl1", [128, 129], F32)
    kb.dve.op(lambda e: e.tensor_copy(out=sl1[:, 0:128], in_=cst[:, C_SL2, :]), reads=[cstb], writes=[cstb])
    kb.dve.op(lambda e: e.tensor_copy(out=sl1[:, 128:129], in_=cst[:, C_ONE, 0:1]), reads=[cstb], writes=[cstb])
    maskbf = kb.sbuf("maskbf", [128, 256], BF16)
    kb.dve.op(lambda e: e.tensor_copy(out=maskbf[:, 0:128], in_=cst[:, C_MP, :]), reads=[cstb], writes=[cstb])
    kb.dve.op(lambda e: e.tensor_copy(out=maskbf[:, 128:256], in_=cst[:, C_MC, :]), reads=[cstb], writes=[cstb])
    onesbf = kb.sbuf("onesbf", [128, 128], BF16)
    kb.dve.op(lambda e: e.tensor_copy(out=onesbf[:], in_=cst[:, C_ONE, :]), reads=[cstb], writes=[cstb])

    kb.push()
    wtok = kb.sbuf("wtok", [128, KC, 1536], BF16)
    wfm = kb.sbuf("wfm", [128, KC, 768], BF16)
    wAb = Buf("wA")
    for k in range(KC):
        kb.pool.dma(wtok[:, k, 0:NTOKC], a["w_tok"][k * 128:(k + 1) * 128, :], writes=[wAb], slot=wAb)
        kb.pool.dma(wfm[:, k, :], a["w_fm"][k * 128:(k + 1) * 128, :], writes=[wAb], slot=wAb)
    pv = kb.sbuf("pv", [128, 4, KC], F32)
    pvb = Buf("pv")
    load_pk(kb, pv[:, 0, :], a["mod"][0, 0:D], pvb, KC)
    load_pk(kb, pv[:, 1, :], a["mod"][0, D:2 * D], pvb, KC)
    load_pk(kb, pv[:, 2, :], a["g_mix"], pvb, KC)
    kb.dve.op(lambda e: e.scalar_tensor_tensor(out=pv[:, 3, :], in0=pv[:, 1, :], scalar=1.0, in1=pv[:, 2, :],
                                               op0=ALU.add, op1=ALU.mult), reads=[pvb], writes=[pvb])
    gqk = kb.sbuf("gqk", [128, 6, 128], F32)
    for sl in range(6):
        src = a["q_norm_g"] if sl < 3 else a["k_norm_g"]
        kb.sp.dma(gqk[:, sl, :], src.partition_broadcast(128), writes=[pvb], slot=pvb)
    gcw = kb.sbuf("gcw", [128, 6, 4], F32)
    for ch in range(6):
        kb.sp.dma(gcw[:, ch, :], a["gconv_w"][:, ch * 128:(ch + 1) * 128].rearrange("k p -> p k"), writes=[pvb],
                  slot=pvb, allow_slow_non_contiguous=True)
    cs = kb.sbuf("cs", [128, 2, NT, 16], F32)
    kb.push()
    posi = kb.sbuf("posi", [128, NT], I32)
    posf = kb.sbuf("posf", [128, NT], F32)
    ang = kb.sbuf("ang", [128, 2, NT, 16], F32)
    kq = kb.sbuf("kq", [128, 2, NT, 16], F32)
    ki = kb.sbuf("ki", [128, 2, NT, 16], I32)
    rpb = Buf("rope")
    kb.sp.dma(posi[:], a["pos"].rearrange("(t p) -> p t", p=128), writes=[rpb], slot=rpb,
              allow_slow_non_contiguous=True)
    kb.dve.op(lambda e: e.tensor_copy(out=posf[:], in_=posi[:]), reads=[rpb], writes=[rpb])
    for f in range(16):
        kb.dve.op(lambda e: e.tensor_scalar(out=ang[:, 0, :, f], in0=posf[:], scalar1=cst[:, C_MISC, f:f + 1],
                                            scalar2=None, op0=ALU.mult), reads=[rpb, cstb], writes=[rpb])
    kb.dve.op(lambda e: e.tensor_scalar(out=ang[:, 1], in0=ang[:, 0], scalar1=float(np.pi / 2), scalar2=None,
                                        op0=ALU.add), reads=[rpb], writes=[rpb])
    kb.dve.op(lambda e: e.tensor_scalar(out=kq[:], in0=ang[:], scalar1=float(1.0 / TWO_PI), scalar2=None,
                                        op0=ALU.mult), reads=[rpb], writes=[rpb])
    kb.dve.op(lambda e: e.tensor_copy(out=ki[:], in_=kq[:]), reads=[rpb], writes=[rpb])
    kb.dve.op(lambda e: e.tensor_copy(out=kq[:], in_=ki[:]), reads=[rpb], writes=[rpb])
    kb.dve.op(lambda e: e.scalar_tensor_tensor(out=ang[:], in0=kq[:], scalar=-CW1, in1=ang[:], op0=ALU.mult,
                                               op1=ALU.add), reads=[rpb], writes=[rpb])
    kb.dve.op(lambda e: e.scalar_tensor_tensor(out=ang[:], in0=kq[:], scalar=-CW2, in1=ang[:], op0=ALU.mult,
                                               op1=ALU.add), reads=[rpb], writes=[rpb])
    kb.dve.op(lambda e: e.tensor_scalar(out=kq[:], in0=ang[:], scalar1=float(np.pi), scalar2=-TWO_PI, op0=ALU.is_gt,
                                        op1=ALU.mult), reads=[rpb], writes=[rpb])
    kb.dve.op(lambda e: e.tensor_tensor(out=ang[:], in0=ang[:], in1=kq[:], op=ALU.add), reads=[rpb], writes=[rpb])
    kb.dve.op(lambda e: e.tensor_scalar(out=kq[:], in0=ang[:], scalar1=float(-np.pi), scalar2=TWO_PI, op0=ALU.is_lt,
                                        op1=ALU.mult), reads=[rpb], writes=[rpb])
    kb.dve.op(lambda e: e.tensor_tensor(out=ang[:], in0=ang[:], in1=kq[:], op=ALU.add), reads=[rpb], writes=[rpb])
    kb.dve.op(lambda e: e.tensor_scalar(out=ang[:], in0=ang[:], scalar1=3.14159, scalar2=-3.14159,
                                        op0=ALU.min, op1=ALU.max), reads=[rpb], writes=[rpb])
    kb.act.op(lambda e: e.activation(out=cs[:], in_=ang[:], func=AF.Sin), reads=[rpb], writes=[rpb])
    barrier(kb)
    kb.pop()

    hT = [kb.sbuf(f"hTa{i}", [128, KC, 512], BF16) for i in range(2)]
    hTb = [Buf(f"hTa{i}") for i in range(2)]
    gx = kb.sbuf("gx", [128, 6, 3 + 512], F32)
    gxb = [Buf(f"gx{c}") for c in range(6)]
    gc = kb.sbuf("gc", [128, 6, 512], F32)
    gcb = [Buf(f"gc{c}") for c in range(6)]
    sqa = [kb.sbuf(f"sqa{i}", [128, 512], F32) for i in range(2)]
    sqab = [Buf(f"sqa{i}") for i in range(2)]
    rinv = [kb.sbuf(f"rinv{i}", [128, 512], F32) for i in range(2)]
    rinvb = [Buf(f"rinv{i}") for i in range(2)]
    tq = [kb.sbuf(f"tq{i}", [128, 6, 128], F32) for i in range(2)]
    tqb = [Buf(f"tq{i}") for i in range(2)]
    tsq = kb.sbuf("tsq", [128, 6, 128], F32)
    tsqb = Buf("tsq")
    rq = [kb.sbuf(f"rq{i}", [128, 6], F32) for i in range(2)]
    rqb = [Buf(f"rq{i}") for i in range(2)]
    rt = kb.sbuf("rt", [128, 4, 6, 16], F32)
    rtb = Buf("rt")
    qko = [kb.sbuf(f"qko{i}", [128, 6, 128], BF16) for i in range(2)]
    qkob = [Buf(f"qko{i}") for i in range(2)]
    vo = [kb.sbuf(f"vo{i}", [128, 3, 128], BF16) for i in range(2)]
    vob = [Buf(f"vo{i}") for i in range(2)]
    zo = [kb.sbuf(f"zo{i}", [128, 256], F32) for i in range(2)]
    zob = [Buf(f"zo{i}") for i in range(2)]
    xnT_v = a["xnT"].rearrange("(k p) t -> p k t", p=128)
    gT_v = gT.rearrange("c p t -> p c t")
    kb.pool.op(lambda e: e.memset(gx[:, :, 0:3], 0.0), writes=gxb)
    it = 0
    for g in range(NG if 'noloop' not in DBG else 0):
        t0 = g * 512
        hs = g % 2
        h_, hb_ = hT[hs], hTb[hs]
        kb.sp.dma(h_[:], xnT_v[:, :, t0:t0 + 512], reads=[a["xnT_buf"]], writes=[hb_], slot=hb_)
        for k in range(KC):
            kb.act.op(lambda e: e.activation(out=h_[:, k, :], in_=h_[:, k, :], func=AF.Identity,
                                             scale=pv[:, 3, k:k + 1], bias=pv[:, 0, k:k + 1]),
                      reads=[hb_, pvb], writes=[hb_])
        for ch in range(6 if 'nofm' not in DBG else 0):
            bk = ch % 2
            p_ = PR.banks[bk]
            for k in range(KC):
                kb.pe.op(lambda e: e.matmul(p_[:], lhsT=wfm[:, k, ch * 128:(ch + 1) * 128], rhs=h_[:, k, :],
                                            start=(k == 0), stop=(k == KC - 1)),
                         reads=[wAb, hb_], writes=[bankb[bk]], signal=(k == KC - 1))
            if g > 0:
                kb.pool.op(lambda e: e.tensor_copy(out=gx[:, ch, 0:3], in_=gx[:, ch, 512:515]), reads=[gxb[ch]],
                           writes=[gxb[ch]])
            kb.act.op(lambda e: e.copy(out=gx[:, ch, 3:515], in_=p_[:]), reads=[bankb[bk]], writes=[gxb[ch]])
            if 'fm1' in DBG:
                continue
            for tp in range(4):
                if tp == 0:
                    kb.dve.op(lambda e: e.tensor_scalar(out=gc[:, ch, :], in0=gx[:, ch, 0:512], scalar1=gcw[:, ch, 0:1],
                                                        scalar2=None, op0=ALU.mult), reads=[gxb[ch], pvb],
                              writes=[gcb[ch]])
                else:
                    kb.dve.op(lambda e: e.scalar_tensor_tensor(out=gc[:, ch, :], in0=gx[:, ch, tp:tp + 512],
                                                               scalar=gcw[:, ch, tp:tp + 1], in1=gc[:, ch, :],
                                                               op0=ALU.mult, op1=ALU.add),
                              reads=[gxb[ch], pvb, gcb[ch]], writes=[gcb[ch]])
            kb.act.op(lambda e: e.activation(out=gc[:, ch, :], in_=gc[:, ch, :], func=AF.Silu), reads=[gcb[ch]],
                      writes=[gcb[ch]])
            if 'fm2' in DBG:
                continue
            if ch < 4:
                s_ = ch % 2
                kb.act.op(lambda e: e.activation(out=sqa[s_][:], in_=gc[:, ch, :], func=AF.Square), reads=[gcb[ch]],
                          writes=[sqab[s_]])
                bk2 = 2 + (ch % 2)
                kb.pe.op(lambda e: e.matmul(PR.banks[bk2][:], lhsT=cst[:, C_ONE, :], rhs=sqa[s_][:], start=True, stop=True),
                         reads=[cstb, sqab[s_]], writes=[bankb[bk2]])
                kb.act.op(lambda e: e.activation(out=rinv[s_][:], in_=PR.banks[bk2][:], func=AF.Sqrt, bias=epsc[:]),
                          reads=[bankb[bk2], cstb], writes=[rinvb[s_]])
                kb.dve.op(lambda e: e.reciprocal(out=rinv[s_][:], in_=rinv[s_][:]), reads=[rinvb[s_]], writes=[rinvb[s_]])
                sc = float(128 ** -0.5) if ch < 2 else 1.0
                kb.dve.op(lambda e: e.scalar_tensor_tensor(out=gc[:, ch, :], in0=gc[:, ch, :], scalar=sc, in1=rinv[s_][:],
                                                           op0=ALU.mult, op1=ALU.mult),
                          reads=[gcb[ch], rinvb[s_]], writes=[gcb[ch]])
            if 'fm3' in DBG:
                continue
            kb.pool.dma(gT_v[:, ch, t0:t0 + 512], gc[:, ch, :], reads=[gcb[ch]], writes=[gTb], slot=gcb[ch])
        for j in range(4 if 'notok' not in DBG else 0):
            s_ = it % 2
            it += 1
            tile = g * 4 + j
            r0 = t0 + j * 128
            widths = [(0, 512, 4), (512, 512, 5), (1024, NTOKC - 1024, 6)]
            for (c0, wd, bk) in widths:
                for k in range(KC):
                    kb.pe.op(lambda e: e.matmul(PR.banks[bk][:, 0:wd], lhsT=h_[:, k, j * 128:(j + 1) * 128],
                                                rhs=wtok[:, k, c0:c0 + wd], start=(k == 0), stop=(k == KC - 1)),
                             reads=[wAb, hb_], writes=[bankb[bk]], signal=(k == KC - 1))
            if 'tok0' in DBG:
                continue
            if 'skip0' not in DBG:
                kb.act.op(lambda e: e.copy(out=tq[s_][:, 0:4, :], in_=PR.banks[4][:].rearrange('p (a d) -> p a d', a=4)), writes=[tqb[s_], bankb[4]])
            if 'skip1' not in DBG:
                kb.act.op(lambda e: e.copy(out=tq[s_][:, 4:6, :], in_=PR.banks[5][:, 0:256].rearrange('p (a d) -> p a d', a=2)), writes=[tqb[s_], bankb[5]])
            if 'skip2' not in DBG:
                kb.dve.op(lambda e: e.tensor_copy(out=vo[s_][:, 0:2, :], in_=PR.banks[5][:, 256:512].rearrange('p (a d) -> p a d', a=2)), writes=[vob[s_], bankb[5]])
            if 'skip3' not in DBG:
                kb.dve.op(lambda e: e.tensor_copy(out=vo[s_][:, 2, :], in_=PR.banks[6][:, 0:128]), writes=[vob[s_], bankb[6]])
            if 'nodma' not in DBG:
                kb.pool.dma(v_tok[r0:r0 + 128, :, :], vo[s_][:], reads=[vob[s_]], writes=[vtb], slot=vob[s_])
            if 'skip4' not in DBG:
                kb.act.op(lambda e: e.activation(out=zo[s_][:], in_=PR.banks[6][:, 128:384], func=AF.Silu), writes=[zob[s_], bankb[6]])
            if 'nodma' not in DBG:
                kb.pool.dma(zs_tok[r0:r0 + 128, :], zo[s_][:], reads=[zob[s_]], writes=[zsb], slot=zob[s_])
            if 'skip5' not in DBG:
                kb.act.op(lambda e: e.copy(out=GB[:, tile, :], in_=PR.banks[6][:, 384:392]), writes=[GBb, bankb[6]])
            if 'tok1' in DBG:
                continue
            kb.dve.op(lambda e: e.tensor_tensor(out=tsq[:], in0=tq[s_][:], in1=tq[s_][:], op=ALU.mult), reads=[tqb[s_]],
                      writes=[tsqb])
            kb.dve.op(lambda e: e.tensor_reduce(out=rq[s_][:], in_=tsq[:], axis=AX.X, op=ALU.add), reads=[tsqb],
                      writes=[rqb[s_]])
            kb.act.op(lambda e: e.activation(out=rq[s_][:], in_=rq[s_][:], func=AF.Sqrt, scale=1.0 / 128, bias=epsc[:]),
                      reads=[rqb[s_], cstb], writes=[rqb[s_]])
            kb.dve.op(lambda e: e.reciprocal(out=rq[s_][:], in_=rq[s_][:]), reads=[rqb[s_]], writes=[rqb[s_]])
            for sl in range(6):
                kb.dve.op(lambda e: e.scalar_tensor_tensor(out=tq[s_][:, sl, :], in0=tq[s_][:, sl, :],
                                                           scalar=rq[s_][:, sl:sl + 1], in1=gqk[:, sl, :],
                                                           op0=ALU.mult, op1=ALU.mult),
                          reads=[tqb[s_], rqb[s_], pvb], writes=[tqb[s_]])
            if 'tok2' in DBG:
                continue
            for sl in range(6):
                x1 = tq[s_][:, sl, 0:16]
                x2 = tq[s_][:, sl, 16:32]
                cos_ = cs[:, 1, tile, :]
                sin_ = cs[:, 0, tile, :]
                kb.pool.op(lambda e: e.tensor_tensor(out=rt[:, 0, sl, :], in0=x1, in1=cos_, op=ALU.mult),
                           reads=[tqb[s_], rpb], writes=[rtb])
                kb.pool.op(lambda e: e.tensor_tensor(out=rt[:, 1, sl, :], in0=x2, in1=sin_, op=ALU.mult),
                           reads=[tqb[s_], rpb], writes=[rtb])
                kb.pool.op(lambda e: e.tensor_tensor(out=rt[:, 2, sl, :], in0=x2, in1=cos_, op=ALU.mult),
                           reads=[tqb[s_], rpb], writes=[rtb])
                kb.pool.op(lambda e: e.tensor_tensor(out=rt[:, 3, sl, :], in0=x1, in1=sin_, op=ALU.mult),
                           reads=[tqb[s_], rpb], writes=[rtb])
            kb.dve.op(lambda e: e.tensor_tensor(out=tq[s_][:, :, 0:16], in0=rt[:, 0], in1=rt[:, 1], op=ALU.subtract),
                      reads=[rtb, tqb[s_]], writes=[tqb[s_]])
            kb.dve.op(lambda e: e.tensor_tensor(out=tq[s_][:, :, 16:32], in0=rt[:, 2], in1=rt[:, 3], op=ALU.add),
                      reads=[rtb, tqb[s_]], writes=[tqb[s_]])
            kb.act.op(lambda e: e.copy(out=qko[s_][:], in_=tq[s_][:]), reads=[tqb[s_]], writes=[qkob[s_]])
            if 'nodma' not in DBG:
                kb.pool.dma(qk_tok[r0:r0 + 128, :, :], qko[s_][:], reads=[qkob[s_]], writes=[qkb], slot=qkob[s_])
    if 'nogb' in DBG:
        barrier(kb)
        kb.pop()
        return
    ab = kb.sbuf("ab", [128, 2, 2], F32)
    abb = Buf("ab")
    kb.sp.dma(ab[:, 0, :], a["a_log"].partition_broadcast(128), writes=[abb], slot=abb)
    kb.sp.dma(ab[:, 1, :], a["dt_bias"].partition_broadcast(128), writes=[abb], slot=abb)
    kb.act.op(lambda e: e.activation(out=ab[:, 0, :], in_=ab[:, 0, :], func=AF.Exp), reads=[abb], writes=[abb])
    spt = kb.sbuf("spt", [128, 3, NT, 2], F32)
    sptb = Buf("spt")
    kb.act.op(lambda e: e.activation(out=GB[:, :, 0:2], in_=GB[:, :, 0:2], func=AF.Sigmoid), reads=[GBb], writes=[GBb])
    for h in range(2):
        kb.dve.op(lambda e: e.tensor_scalar(out=spt[:, 0, :, h], in0=GB[:, :, 2 + h], scalar1=ab[:, 1, h:h + 1],
                                            scalar2=None, op0=ALU.add), reads=[GBb, abb], writes=[sptb])
    kb.act.op(lambda e: e.activation(out=spt[:, 1], in_=spt[:, 0], func=AF.Abs), reads=[sptb], writes=[sptb])
    kb.act.op(lambda e: e.activation(out=spt[:, 1], in_=spt[:, 1], func=AF.Exp, scale=-1.0), reads=[sptb], writes=[sptb])
    kb.act.op(lambda e: e.activation(out=spt[:, 1], in_=spt[:, 1], func=AF.Ln, bias=onec[:]), reads=[sptb, cstb],
              writes=[sptb])
    kb.dve.op(lambda e: e.tensor_scalar(out=spt[:, 0], in0=spt[:, 0], scalar1=0.0, scalar2=None, op0=ALU.max),
              reads=[sptb], writes=[sptb])
    kb.dve.op(lambda e: e.tensor_tensor(out=spt[:, 0], in0=spt[:, 0], in1=spt[:, 1], op=ALU.add), reads=[sptb],
              writes=[sptb])
    for h in range(2):
        kb.dve.op(lambda e: e.tensor_scalar(out=GB[:, :, 2 + h], in0=spt[:, 0, :, h], scalar1=ab[:, 0, h:h + 1],
                                            scalar2=-1.0, op0=ALU.mult, op1=ALU.mult), reads=[sptb, abb, GBb],
                  writes=[GBb])
    barrier(kb)
    kb.pop()
    if "B" not in phases:
        return

    kb.push()
    SC = float(128 ** -0.5)
    qt_ = kb.sbuf("qtok", [128, 16, 128], BF16)
    kt_ = kb.sbuf("ktok", [128, 16, 128], BF16)
    qtb, ktb = Buf("qtok"), Buf("ktok")
    QT = kb.sbuf("QT", [128, 16, 128], BF16)
    QTb = Buf("QT")
    KT = [[kb.sbuf(f"KT{g}_{i}", [128, 16, 128], BF16) for i in range(2)] for g in range(3)]
    KTb = [[Buf(f"KT{g}_{i}") for i in range(2)] for g in range(3)]
    VV = [[kb.sbuf(f"VV{g}_{i}", [128, 16, 128], BF16) for i in range(2)] for g in range(3)]
    VVb = [[Buf(f"VV{g}_{i}") for i in range(2)] for g in range(3)]
    ND = kb.sbuf("ND", [128, 2, 2048], F32)
    NDb = Buf("ND")
    ex = [kb.sbuf(f"ex{i}", [128, 256], BF16) for i in range(2)]
    exb = [Buf(f"ex{i}") for i in range(2)]
    pT = [kb.sbuf(f"pT{i}", [128, 256], BF16) for i in range(2)]
    pTb = [Buf(f"pT{i}") for i in range(2)]
    oT = kb.sbuf("oT", [128, 2048], BF16)
    oTb = Buf("oT")
    ptr = [PR.banks[0][:].bitcast(BF16), PR.banks[1][:].bitcast(BF16)]
    blk = 0

    def load_blocks(dst, dstb, src3, n, g, slot):
        u0 = n * 2048
        if g == 0:
            kb.sp.dma(dst[:], src3[u0:u0 + 2048, slot, :].rearrange("(b i) d -> i b d", i=128), reads=[qkb, vtb],
                      writes=[dstb], slot=dstb)
        elif g == 1:
            for m in range(4):
                kb.sp.dma(dst[:, m * 4:(m + 1) * 4, :],
                          src3[u0 + m * 512:u0 + (m + 1) * 512, slot, :].rearrange("(i r) d -> i r d", r=4),
                          reads=[qkb, vtb], writes=[dstb], slot=dstb)
        else:
            kb.sp.dma(dst[:], src3[u0:u0 + 2048, slot, :].rearrange("(i r) d -> i r d", r=16), reads=[qkb, vtb],
                      writes=[dstb], slot=dstb)

    def nd_view(g, bi):
        if g == 0:
            return ND[:, :, bi * 128:(bi + 1) * 128]
        if g == 1:
            m, r = bi // 4, bi % 4
            return ND[:, :, m * 512:(m + 1) * 512].rearrange("p a (i r) -> p a i r", r=4)[:, :, :, r]
        return ND[:, :, :].rearrange("p a (i r) -> p a i r", r=16)[:, :, :, bi]

    for n in range(NU):
        cur = n % 2
        for g in range(3):
            load_blocks(qt_, qtb, qk_tok, n, g, g)
            load_blocks(kt_, ktb, qk_tok, n, g, 3 + g)
            load_blocks(VV[g][cur], VVb[g][cur], v_tok, n, g, g)
            for (src, srcb, dst, dstb) in ((qt_, qtb, QT, QTb), (kt_, ktb, KT[g][cur], KTb[g][cur])):
                for half in range(2):
                    for kk in range(8):
                        kb.pe.op(lambda e: e.transpose(ptr[half][:, kk * 128:(kk + 1) * 128], src[:, half * 8 + kk, :],
                                                       identbf[:]), reads=[srcb, identb32], writes=[bankb[half]],
                                 signal=(kk == 7))
                    if half == 0:
                        kb.dve.op(lambda e: e.tensor_copy(out=dst[:, 0:8, :], in_=ptr[half][:]), reads=[bankb[half]],
                                  writes=[dstb])
                    else:
                        kb.act.op(lambda e: e.copy(out=dst[:, 8:16, :], in_=ptr[half][:]), reads=[bankb[half]],
                                  writes=[dstb])
            for bi in range(16):
                if g == 0:
                    pb_, pu_ = (bi - 1, cur) if bi > 0 else (15, 1 - cur)
                    has_prev = not (n == 0 and bi == 0)
                elif g == 1:
                    pb_, pu_ = (bi - 4, cur) if bi >= 4 else (12 + bi, 1 - cur)
                    has_prev = not (n == 0 and bi < 4)
                else:
                    pb_, pu_ = bi, 1 - cur
                    has_prev = n > 0
                s_ = blk % 2
                blk += 1
                bs = 2 + s_
                bo = 4 + s_
                c0 = 0 if has_prev else 128
                if has_prev:
                    kb.pe.op(lambda e: e.matmul(PR.banks[bs][:, 0:128], lhsT=KT[g][pu_][:, pb_, :], rhs=QT[:, bi, :],
                                                start=True, stop=True), reads=[KTb[g][pu_], QTb], writes=[bankb[bs]],
                             signal=False)
                kb.pe.op(lambda e: e.matmul(PR.banks[bs][:, 128:256], lhsT=KT[g][cur][:, bi, :], rhs=QT[:, bi, :],
                                            start=True, stop=True), reads=[KTb[g][cur], QTb], writes=[bankb[bs]])
                kb.act.op(lambda e: e.activation(out=ex[s_][:, c0:256], in_=PR.banks[bs][:, c0:256], func=AF.Exp, scale=SC),
                          reads=[bankb[bs]], writes=[exb[s_]])
                kb.dve.op(lambda e: e.tensor_tensor(out=pT[s_][:, c0:256], in0=ex[s_][:, c0:256], in1=maskbf[:, c0:256],
                                                    op=ALU.mult), reads=[exb[s_], cstb], writes=[pTb[s_]])
                if has_prev:
                    kb.pe.op(lambda e: e.matmul(PR.banks[bo][:, 0:128], lhsT=VV[g][pu_][:, pb_, :], rhs=pT[s_][:, 0:128],
                                                start=True, stop=False), reads=[VVb[g][pu_], pTb[s_]],
                             writes=[bankb[bo]], signal=False)
                kb.pe.op(lambda e: e.matmul(PR.banks[bo][:, 0:128], lhsT=VV[g][cur][:, bi, :], rhs=pT[s_][:, 128:256],
                                            start=(not has_prev), stop=True), reads=[VVb[g][cur], pTb[s_]],
                         writes=[bankb[bo]], signal=False)
                if has_prev:
                    kb.pe.op(lambda e: e.matmul(PR.banks[bo][:, 128:256], lhsT=onesbf[:], rhs=pT[s_][:, 0:128],
                                                start=True, stop=False), reads=[cstb, pTb[s_]], writes=[bankb[bo]],
                             signal=False)
                kb.pe.op(lambda e: e.matmul(PR.banks[bo][:, 128:256], lhsT=onesbf[:], rhs=pT[s_][:, 128:256],
                                            start=(not has_prev), stop=True), reads=[cstb, pTb[s_]], writes=[bankb[bo]])
                src = PR.banks[bo][:, 0:256].rearrange("p (a q) -> p a q", a=2)
                if g == 0:
                    kb.act.op(lambda e: e.copy(out=nd_view(g, bi), in_=src), reads=[bankb[bo]], writes=[NDb])
                else:
                    kb.dve.op(lambda e: e.tensor_tensor(out=nd_view(g, bi), in0=nd_view(g, bi), in1=src, op=ALU.add),
                              reads=[bankb[bo], NDb], writes=[NDb])
        kb.dve.op(lambda e: e.reciprocal(out=ND[:, 1, :], in_=ND[:, 1, :]), reads=[NDb], writes=[NDb])
        kb.dve.op(lambda e: e.tensor_tensor(out=oT[:], in0=ND[:, 0, :], in1=ND[:, 1, :], op=ALU.mult), reads=[NDb],
                  writes=[oTb])
        kb.sp.dma(a["o_aT"][:, n * 2048:(n + 1) * 2048], oT[:], reads=[oTb], writes=[a["o_aT_buf"]], slot=oTb)
    barrier(kb)
    kb.pop()
    if "C" not in phases:
        return

    kb.push()
    gnb = kb.sbuf("gnb", [128, 128], F32)
    gnbb = Buf("gnb")
    kb.sp.dma(gnb[:], a["gdn_norm_g"].partition_broadcast(128), writes=[gnbb], slot=gnbb)
    St = [kb.sbuf(f"St{h}", [128, 128], F32) for h in range(2)]
    Stb = [Buf(f"St{h}") for h in range(2)]
    for h in range(2):
        kb.pool.op(lambda e: e.memset(St[h][:], 0.0), writes=[Stb[h]])

    class HS:
        pass

    H = []
    for h in range(2):
        o = HS()
        o.qkv = kb.sbuf(f"qkv{h}", [128, 3, 128], F32); o.qkvb = Buf(f"qkv{h}")
        o.gu = kb.sbuf(f"gu{h}", [128, 2, 128], F32); o.gub = Buf(f"gu{h}")
        o.ex = kb.sbuf(f"exc{h}", [128, 512], F32); o.exb = Buf(f"exc{h}")
        o.t1 = kb.sbuf(f"t1{h}", [128, 128], F32); o.t1b = Buf(f"t1{h}")
        o.dT = kb.sbuf(f"dT{h}", [128, 128], F32); o.dTb = Buf(f"dT{h}")
        o.qkm = kb.sbuf(f"qkm{h}", [128, 128], F32); o.qkmb = Buf(f"qkm{h}")
        o.bege = kb.sbuf(f"bege{h}", [128, 1], F32); o.begeb = Buf(f"bege{h}")
        o.y = kb.sbuf(f"y{h}", [128, 256], F32); o.yb = Buf(f"y{h}")
        o.kdec = kb.sbuf(f"kdec{h}", [128, 128], F32); o.kdecb = Buf(f"kdec{h}")
        o.qd = kb.sbuf(f"qd{h}", [128, 128], F32); o.qdb = Buf(f"qd{h}")
        o.pw = kb.sbuf(f"pw{h}", [128, 12, 128], F32); o.pwb = [Buf(f"pw{h}_{i}") for i in range(12)]
        o.wT = kb.sbuf(f"wT{h}", [128, 128], F32); o.wTb = Buf(f"wT{h}")
        o.vn = kb.sbuf(f"vn{h}", [128, 128], F32); o.vnb = Buf(f"vn{h}")
        o.O = kb.sbuf(f"O{h}", [128, 128], F32); o.Ob = Buf(f"O{h}")
        o.junk = kb.sbuf(f"junk{h}", [128, 128], F32)
        o.gcol = kb.sbuf(f"gcol{h}", [128, 8], F32); o.gcolb = Buf(f"gcol{h}")
        o.ss = kb.sbuf(f"ssn{h}", [128, 1], F32); o.ssb = Buf(f"ssn{h}")
        o.zt = kb.sbuf(f"zt{h}", [128, 128], F32); o.ztb = Buf(f"zt{h}")
        o.on = kb.sbuf(f"on{h}", [128, 128], BF16); o.onb = Buf(f"on{h}")
        o.od = kb.sbuf(f"od{h}", [128, 512], BF16); o.odb = Buf(f"od{h}")
        b0 = 4 * h
        o.R = {}
        names = [("a", b0, 0, 2), ("G1", b0, 2, 2), ("PD", b0 + 1, 0, 4), ("pwA", b0 + 2, 0, 1), ("pwB", b0 + 2, 1, 1),
                 ("Y", b0 + 2, 2, 2), ("P1", b0 + 3, 0, 1), ("PO", b0 + 3, 1, 1), ("PS2", b0 + 3, 2, 1), ("TR", b0 + 3, 3, 1)]
        for (nm, bk, r0, nr) in names:
            o.R[nm] = (PR.reg(bk, r0, nr), bankb[bk])
        H.append(o)

    def cmat(i):
        return cst[:, i, :]

    for t in range(NT):
        c0 = t * 128
        for h in range(2):
            o = H[h]
            bcol = GB[:, t, h:h + 1]
            kb.pool.op(lambda e: e.tensor_copy(out=o.gcol[:, 0:1], in_=GB[:, t, 2 + h:3 + h]), reads=[GBb], writes=[o.gcolb])
            gcol = o.gcol[:, 0:1]
            for i, ch in enumerate((h, 2 + h, 4 + h)):
                kb.sp.dma(o.qkv[:, i, :], gT[ch, :, c0:c0 + 128], reads=[gTb], writes=[o.qkvb], slot=o.qkvb)
            kb.sp.dma(o.zt[:], zs_tok[c0:c0 + 128, h * 128:(h + 1) * 128], reads=[zsb], writes=[o.ztb], slot=o.ztb)
            QTt, KTt, VTt = o.qkv[:, 0, :], o.qkv[:, 1, :], o.qkv[:, 2, :]
            (ra, rab), (rG, rGb), (rPD, rPDb) = o.R["a"], o.R["G1"], o.R["PD"]
            kb.pe.op(lambda e: e.transpose(ra[:, 0:128], KTt, ident32[:]), reads=[o.qkvb, identb32], writes=[rab], signal=False)
            kb.pe.op(lambda e: e.transpose(ra[:, 128:256], VTt, ident32[:]), reads=[o.qkvb, identb32], writes=[rab])
            kb.dve.op(lambda e: e.tensor_scalar(out=o.gu[:, 0, :], in0=cmat(C_U2), scalar1=gcol, scalar2=None, op0=ALU.mult),
                      reads=[cstb, GBb, o.gcolb], writes=[o.gub])
            kb.dve.op(lambda e: e.tensor_scalar(out=o.gu[:, 1, :], in0=cmat(C_SL2), scalar1=gcol, scalar2=None, op0=ALU.mult),
                      reads=[cstb, GBb, o.gcolb], writes=[o.gub])
            kb.pe.op(lambda e: e.matmul(rPD[:, 0:128], lhsT=o.gu[:, 0, :], rhs=cmat(C_SL2), start=True, stop=True),
                     reads=[o.gub, cstb], writes=[rPDb], signal=False)
            kb.pe.op(lambda e: e.matmul(rPD[:, 128:160], lhsT=o.gu[:, 0, :], rhs=cst[:, C_ONE, 0:32], start=True, stop=True),
                     reads=[o.gub, cstb], writes=[rPDb], signal=False)
            kb.pe.op(lambda e: e.matmul(rPD[:, 160:192], lhsT=o.gu[:, 1, :], rhs=cst[:, C_ONE, 0:32], start=True, stop=True),
                     reads=[o.gub, cstb], writes=[rPDb], signal=False)
            kb.pe.op(lambda e: e.matmul(rPD[:, 256:384], lhsT=cmat(C_SL2), rhs=o.gu[:, 0, :], start=True, stop=True),
                     reads=[o.gub, cstb], writes=[rPDb], signal=False)
            kb.pe.op(lambda e: e.matmul(rPD[:, 384:512], lhsT=cmat(C_ONE), rhs=o.gu[:, 0, :], start=True, stop=True),
                     reads=[o.gub, cstb], writes=[rPDb])
            if 'clamp' in DBG:
                kb.dve.op(lambda e: e.tensor_scalar(out=o.ex[:], in0=rPD[:], scalar1=-60.0, scalar2=None, op0=ALU.max),
                          reads=[rPDb], writes=[o.exb])
                kb.act.op(lambda e: e.activation(out=o.ex[:], in_=o.ex[:], func=AF.Exp), reads=[o.exb], writes=[o.exb])
            else:
                kb.act.op(lambda e: e.activation(out=o.ex[:], in_=rPD[:], func=AF.Exp), reads=[rPDb], writes=[o.exb])
            if 'c1' in DBG:
                continue
            kb.pe.op(lambda e: e.matmul(rG[:, 0:128], lhsT=KTt, rhs=KTt, start=True, stop=True), reads=[o.qkvb],
                     writes=[rGb], signal=False)
            kb.pe.op(lambda e: e.matmul(rG[:, 128:256], lhsT=KTt, rhs=QTt, start=True, stop=True), reads=[o.qkvb],
                     writes=[rGb])
            A0, B0 = o.pw[:, 0, :], o.pw[:, 1, :]
            kb.dve.op(lambda e: e.tensor_tensor(out=o.t1[:], in0=o.ex[:, 0:128], in1=cmat(C_SL2), op=ALU.mult),
                      reads=[o.exb, cstb], writes=[o.t1b])
            kb.dve.op(lambda e: e.scalar_tensor_tensor(out=A0, in0=rG[:, 0:128], scalar=bcol, in1=o.t1[:], op0=ALU.mult,
                                                       op1=ALU.mult), reads=[rGb, GBb, o.t1b], writes=[o.pwb[0]])
            kb.dve.op(lambda e: e.tensor_tensor(out=o.dT[:], in0=o.ex[:, 256:384], in1=cmat(C_U2), op=ALU.mult),
                      reads=[o.exb, cstb], writes=[o.dTb])
            kb.dve.op(lambda e: e.tensor_tensor(out=o.qkm[:], in0=rG[:, 128:256], in1=o.dT[:], op=ALU.mult),
                      reads=[rGb, o.dTb], writes=[o.qkmb])
            kb.pool.op(lambda e: e.tensor_tensor(out=o.bege[:], in0=bcol, in1=o.ex[:, 128:129], op=ALU.mult),
                       reads=[GBb, o.exb], writes=[o.begeb])
            kb.dve.op(lambda e: e.tensor_scalar(out=o.y[:, 0:128], in0=ra[:, 128:256], scalar1=bcol, scalar2=None,
                                                op0=ALU.mult), reads=[rab, GBb], writes=[o.yb])
            kb.dve.op(lambda e: e.tensor_scalar(out=o.y[:, 128:256], in0=ra[:, 0:128], scalar1=o.bege[:, 0:1], scalar2=None,
                                                op0=ALU.mult), reads=[rab, o.begeb], writes=[o.yb])
            kb.dve.op(lambda e: e.tensor_scalar(out=o.kdec[:], in0=ra[:, 0:128], scalar1=o.ex[:, 160:161], scalar2=None,
                                                op0=ALU.mult), reads=[rab, o.exb], writes=[o.kdecb])
            kb.pool.op(lambda e: e.tensor_tensor(out=o.qd[:], in0=QTt, in1=o.ex[:, 384:512], op=ALU.mult),
                       reads=[o.qkvb, o.exb], writes=[o.qdb])
            if 'dump' in DBG and t == 0:
                kb.sp.dma(a["dbg"][h, 0], A0, reads=[o.pwb[0]], writes=[a["dbg_buf"]], slot=o.pwb[0])
                kb.sp.dma(a["dbg"][h, 1], o.t1[:], reads=[o.t1b], writes=[a["dbg_buf"]], slot=o.t1b)
                kb.sp.dma(a["dbg"][h, 2], o.ex[:, 0:128], reads=[o.exb], writes=[a["dbg_buf"]], slot=o.exb)
                kb.sp.dma(a["dbg"][h, 3], o.gu[:, 0, :], reads=[o.gub], writes=[a["dbg_buf"]], slot=o.gub)
            if 'c2' in DBG:
                continue
            (rA, rAb), (rB, rBb), (rY, rYb) = o.R["pwA"], o.R["pwB"], o.R["Y"]
            (rT0, rT0b) = o.R["TR"]
            kb.pe.op(lambda e: e.transpose(rT0[:], A0, ident32[:]), reads=[o.pwb[0], identb32], writes=[rT0b])
            kb.act.op(lambda e: e.copy(out=B0, in_=rT0[:]), reads=[rT0b], writes=[o.pwb[1]])
            if 'altB' in DBG:
                rB, rBb = o.R['PS2']
            for l in range(1, (6 if 'nopow' not in DBG else 1) if ('pow1' not in DBG and 'pow1b' not in DBG) else (2 if 'pow2' not in DBG else 3)):
                Ap, Bp = o.pw[:, 2 * (l - 1), :], o.pw[:, 2 * (l - 1) + 1, :]
                Apb, Bpb = o.pwb[2 * (l - 1)], o.pwb[2 * (l - 1) + 1]
                if l < 5:
                    kb.pe.op(lambda e: e.matmul(rA[:], lhsT=Bp, rhs=Ap, start=True, stop=True), reads=[Apb, Bpb],
                             writes=[rAb])
                    kb.act.op(lambda e: e.copy(out=o.pw[:, 2 * l, :], in_=rA[:]), reads=[rAb], writes=[o.pwb[2 * l]])
                if 'pow1' in DBG:
                    continue
                kb.pe.op(lambda e: e.matmul(rB[:], lhsT=Ap, rhs=Bp, start=True, stop=True), reads=[Apb, Bpb],
                         writes=[rBb])
                kb.act.op(lambda e: e.copy(out=o.pw[:, 2 * l + 1, :], in_=rB[:]), reads=[rBb],
                          writes=[o.pwb[2 * l + 1]])
            if 'c3' in DBG:
                continue
            for l in (5, 4, 3, 2, 1, 0):
                kb.pe.op(lambda e: e.matmul(rY[:], lhsT=o.pw[:, 2 * l + 1, :], rhs=o.y[:], start=True, stop=True),
                         reads=[o.pwb[2 * l + 1], o.yb], writes=[rYb])
                kb.dve.op(lambda e: e.tensor_tensor(out=o.y[:], in0=o.y[:], in1=rY[:],
                                                    op=(ALU.add if l > 0 else ALU.subtract)), reads=[rYb, o.yb],
                          writes=[o.yb])
            (rT, rTb) = o.R["TR"]
            kb.pe.op(lambda e: e.transpose(rT[:], o.y[:, 128:256], ident32[:]), reads=[o.yb, identb32], writes=[rTb])
            kb.act.op(lambda e: e.copy(out=o.wT[:], in_=rT[:]), reads=[rTb], writes=[o.wTb])
            if 'c4' in DBG:
                continue
            (rP1, rP1b), (rPO, rPOb), (rPS, rPSb) = o.R["P1"], o.R["PO"], o.R["PS2"]
            for X, (lo, hi) in enumerate(((0, 64), (64, 128))):
                kb.pe.op(lambda e: e.matmul(rP1[:], lhsT=o.wT[:], rhs=St[h][:], start=True, stop=True),
                         reads=[o.wTb, Stb[h]], writes=[rP1b])
                kb.dve.op(lambda e: e.tensor_tensor(out=o.vn[lo:hi, :], in0=o.y[lo:hi, 0:128], in1=rP1[lo:hi, :],
                                                    op=ALU.subtract), reads=[rP1b, o.yb], writes=[o.vnb])
                kb.pe.op(lambda e: e.matmul(rPO[:], lhsT=o.qd[:], rhs=St[h][:], start=True, stop=False),
                         reads=[o.qdb, Stb[h]], writes=[rPOb], signal=False)
                kb.pe.op(lambda e: e.matmul(rPO[:], lhsT=o.qkm[lo:hi, :], rhs=o.vn[lo:hi, :], start=False, stop=True),
                         reads=[o.qkmb, o.vnb], writes=[rPOb])
                kb.act.op(lambda e: e.copy(out=o.O[lo:hi, :], in_=rPO[lo:hi, :]), reads=[rPOb], writes=[o.Ob])
                kb.pe.op(lambda e: e.matmul(rPS[:], lhsT=o.kdec[lo:hi, :], rhs=o.vn[lo:hi, :], start=True, stop=True),
                         reads=[o.kdecb, o.vnb], writes=[rPSb])
                kb.dve.op(lambda e: e.scalar_tensor_tensor(out=St[h][:], in0=St[h][:], scalar=o.ex[:, 447 + 64 * X:448 + 64 * X],
                                                           in1=rPS[:], op0=ALU.mult, op1=ALU.add),
                          reads=[rPSb, o.exb, Stb[h]], writes=[Stb[h]])
            if 'c5' in DBG:
                continue
            kb.act.op(lambda e: e.activation(out=o.junk[:], in_=o.O[:], func=AF.Square, accum_out=o.ss[:]), reads=[o.Ob],
                      writes=[o.ssb])
            kb.act.op(lambda e: e.activation(out=o.ss[:], in_=o.ss[:], func=AF.Sqrt, scale=1.0 / 128, bias=epsc[:]),
                      reads=[o.ssb, cstb], writes=[o.ssb])
            kb.dve.op(lambda e: e.reciprocal(out=o.ss[:], in_=o.ss[:]), reads=[o.ssb], writes=[o.ssb])
            kb.pool.op(lambda e: e.tensor_tensor(out=o.zt[:], in0=o.zt[:], in1=gnb[:], op=ALU.mult), reads=[o.ztb, gnbb],
                       writes=[o.ztb])
            kb.dve.op(lambda e: e.scalar_tensor_tensor(out=o.on[:], in0=o.O[:], scalar=o.ss[:, 0:1], in1=o.zt[:],
                                                       op0=ALU.mult, op1=ALU.mult), reads=[o.Ob, o.ssb, o.ztb],
                      writes=[o.onb])
            trb = PR.banks[4 * h + 3][:].bitcast(BF16)[:, 768:1024]
            kb.pe.op(lambda e: e.transpose(trb[:, 0:128], o.on[:], identbf[:]), reads=[o.onb, identb32], writes=[rTb])
            jj = t % 4
            kb.act.op(lambda e: e.copy(out=o.od[:, jj * 128:(jj + 1) * 128], in_=trb[:, 0:128]), reads=[rTb],
                      writes=[o.odb])
            if jj == 3:
                kb.sp.dma(a["o_dT"][h * 128:(h + 1) * 128, (t - 3) * 128:(t + 1) * 128], o.od[:], reads=[o.odb],
                          writes=[a["o_dT_buf"]], slot=o.odb)
    kb.pop()


def build_p1(S, phases="ABC"):
    kb = KB()
    a = {}
    a["xnT"] = kb.din("xnT", [D, S], BF16)
    a["mod"] = kb.din("mod", [2, 3 * D], F32)
    a["g_mix"] = kb.din("g_mix", [D], F32)
    a["pos"] = kb.din("pos", [S], I32)
    a["w_tok"] = kb.din("w_tok", [D, NTOKC], F32)
    a["w_fm"] = kb.din("w_fm", [D, 768], F32)
    a["q_norm_g"] = kb.din("q_norm_g", [128], F32)
    a["k_norm_g"] = kb.din("k_norm_g", [128], F32)
    a["gconv_w"] = kb.din("gconv_w", [4, 768], F32)
    a["a_log"] = kb.din("a_log", [2], F32)
    a["dt_bias"] = kb.din("dt_bias", [2], F32)
    a["gdn_norm_g"] = kb.din("gdn_norm_g", [128], F32)
    a["consts"] = kb.din("consts", [128, 9, 128], F32)
    a["o_aT"] = kb.dout("o_aT", [128, S], BF16)
    a["o_dT"] = kb.dout("o_dT", [256, S], BF16)
    for n in ["xnT_buf", "o_aT_buf", "o_dT_buf", "dbg_buf"]:
        a[n] = Buf(n)
    if 'dump' in DBG:
        a["dbg"] = kb.dout("dbg", [2, 4, 128, 128], F32)
    emit_p1(kb, S, a, phases)
    return kb.finish([a["o_aT_buf"], a["o_dT_buf"], a["dbg_buf"]])


SEQ = 16384
NB = 2
NCORE = 8
TPC = SEQ * NB // NCORE
O_Q, O_K, O_V, O_GQ, O_GK, O_GV, O_BETA, O_ALPHA, O_Z, O_UC, O_GL = (
    0, 1536, 3072, 4608, 5632, 6656, 7680, 7688, 7696, 8720, 10768)
_PROGS = {}


def _prog(name, builder, *args):
    key = (name,) + args
    if key not in _PROGS:
        _PROGS[key] = builder(*args)
    return _PROGS[key]


def _c(a):
    return np.ascontiguousarray(a)


def _run(nc, in_maps):
    res = run_bass_kernel_spmd(nc, in_maps, core_ids=list(range(NCORE)))
    return res.results


def kernel(x, c, positions, mix_mod_w, mix_mod_b, mix_norm_g, w_in, q_norm_g, k_norm_g, w_attn_o, gdn_conv_w,
           gdn_a_log, gdn_dt_bias, gdn_norm_g, w_gdn_o, conv_dw_w, conv_dw_b, conv_ln_g, conv_ln_b, w_conv_o, w_out,
           ffn_mod_w, ffn_mod_b, ffn_norm_g, w_gate_up, w_down):
    f32 = np.float32
    x = np.asarray(x, f32)
    c = np.asarray(c, f32)
    positions = np.asarray(positions, np.int32)
    cores = [(i // 4, i % 4) for i in range(NCORE)]
    consts = host_consts()

    p0 = _prog("p0", build_p0, TPC)
    mats = [mix_mod_w[0], ffn_mod_w[0], mix_mod_w[1], ffn_mod_w[1]]
    biases = [mix_mod_b[0], ffn_mod_b[0], mix_mod_b[1], ffn_mod_b[1]]
    in_maps = []
    for i, (b, j) in enumerate(cores):
        in_maps.append({
            "x": _c(x[b, j * TPC:(j + 1) * TPC]),
            "c": _c(c),
            "wm": _c(np.stack([np.asarray(m, f32)[:, 768 * i:768 * (i + 1)] for m in mats])),
            "bm": _c(np.stack([np.asarray(v, f32)[768 * i:768 * (i + 1)] for v in biases])),
        })
    r0 = _run(p0, in_maps)
    mod_all = np.concatenate([r0[i]["modo"] for i in range(NCORE)], axis=-1)
    xnT_full = [np.concatenate([r0[b * 4 + j]["xnT"] for j in range(4)], axis=1) for b in range(NB)]
    x_cur = [x[b] for b in range(NB)]

    p1 = _prog("p1", build_p1, SEQ)
    p2 = _prog("p2", build_p2, TPC)
    for l in range(2):
        win = np.asarray(w_in[l], f32)
        gcw = np.asarray(gdn_conv_w[l], f32)
        in_maps = []
        for i, (b, j) in enumerate(cores):
            heads_a = [(g * 4 + j) * 128 for g in range(3)]
            hd = [2 * j, 2 * j + 1]
            cols = ([O_Q + o for o in heads_a] + [O_K + o for o in heads_a] + [O_V + o for o in heads_a]
                    + [O_Z + h * 128 for h in hd])
            w_tok = np.concatenate([win[:, o:o + 128] for o in cols]
                                   + [win[:, O_BETA + hd[0]:O_BETA + hd[0] + 2], win[:, O_ALPHA + hd[0]:O_ALPHA + hd[0] + 2]],
                                   axis=1)
            fcols = [O_GQ + h * 128 for h in hd] + [O_GK + h * 128 for h in hd] + [O_GV + h * 128 for h in hd]
            w_fm = np.concatenate([win[:, o:o + 128] for o in fcols], axis=1)
            gconv = np.concatenate([gcw[:, o - O_GQ:o - O_GQ + 128] for o in fcols], axis=1)
            in_maps.append({
                "xnT": _c(xnT_full[b]),
                "mod": _c(np.stack([mod_all[2 * l, b], mod_all[2 * l + 1, b]])),
                "g_mix": _c(np.asarray(mix_norm_g[l], f32)),
                "pos": _c(positions[b]),
                "w_tok": _c(w_tok), "w_fm": _c(w_fm),
                "q_norm_g": _c(np.asarray(q_norm_g[l], f32)), "k_norm_g": _c(np.asarray(k_norm_g[l], f32)),
                "gconv_w": _c(gconv),
                "a_log": _c(np.asarray(gdn_a_log[l], f32)[hd[0]:hd[0] + 2]),
                "dt_bias": _c(np.asarray(gdn_dt_bias[l], f32)[hd[0]:hd[0] + 2]),
                "gdn_norm_g": _c(np.asarray(gdn_norm_g[l], f32)),
                "consts": consts,
            })
        r1 = _run(p1, in_maps)
        oaT = [np.concatenate([r1[b * 4 + j]["o_aT"] for j in range(4)], axis=0) for b in range(NB)]
        odT = [np.concatenate([r1[b * 4 + j]["o_dT"] for j in range(4)], axis=0) for b in range(NB)]
        in_maps = []
        shared = {
            "g_mix": _c(np.asarray(mix_norm_g[l], f32)), "g_ffn": _c(np.asarray(ffn_norm_g[l], f32)),
            "w_uc": _c(win[:, O_UC:O_UC + 2048]), "w_gl": _c(win[:, O_GL:O_GL + 3 * D]),
            "w_ao": _c(np.asarray(w_attn_o[l], f32)), "w_go": _c(np.asarray(w_gdn_o[l], f32)),
            "w_co": _c(np.asarray(w_conv_o[l], f32)), "w_out": _c(np.asarray(w_out[l], f32)),
            "w_gu": _c(np.asarray(w_gate_up[l], f32)), "w_dn": _c(np.asarray(w_down[l], f32)),
            "conv_w": _c(np.asarray(conv_dw_w[l], f32)), "conv_b": _c(np.asarray(conv_dw_b[l], f32)),
            "ln_g": _c(np.asarray(conv_ln_g[l], f32)), "ln_b": _c(np.asarray(conv_ln_b[l], f32)),
        }
        for i, (b, j) in enumerate(cores):
            t0 = j * TPC
            xh = np.zeros((D, HALO + TPC), dtype=xnT_full[b].dtype)
            if j > 0:
                xh[:, :] = xnT_full[b][:, t0 - HALO:t0 + TPC]
            else:
                xh[:, HALO:] = xnT_full[b][:, 0:TPC]
            m = dict(shared)
            m.update({
                "x": _c(x_cur[b][t0:t0 + TPC]),
                "xnT_h": xh,
                "halo_flag": np.full((128, 1), 1.0 if j > 0 else 0.0, f32),
                "o_aT": _c(oaT[b][:, t0:t0 + TPC]), "o_dT": _c(odT[b][:, t0:t0 + TPC]),
                "mod": _c(np.stack([mod_all[2 * l, b], mod_all[2 * l + 1, b]])),
            })
            in_maps.append(m)
        r2 = _run(p2, in_maps)
        x_cur = [np.concatenate([r2[b * 4 + j]["xout"] for j in range(4)], axis=0) for b in range(NB)]
        xnT_full = [np.concatenate([r2[b * 4 + j]["xnT_out"] for j in range(4)], axis=1) for b in range(NB)]
    return np.stack(x_cur).astype(f32)
```

```python
import contextlib
import numpy as np
import concourse.bass as bass
import concourse.mybir as mybir
from concourse.bass_utils import run_bass_kernel_spmd

F32 = mybir.dt.float32
BF16 = mybir.dt.bfloat16
I32 = mybir.dt.int32
AF = mybir.ActivationFunctionType
ALU = mybir.AluOpType
AX = mybir.AxisListType

D = 2048
KC = D // 128
EPS = 1e-6


class Buf:
    __slots__ = ("name", "w", "r", "dsem", "dcnt", "psum")

    def __init__(self, name="", psum=False):
        self.name = name
        self.psum = psum
        self.w = None
        self.r = []
        self.dsem = None
        self.dcnt = 0


class Eng:
    def __init__(self, kb, name, h, inorder_safe=False):
        self.kb = kb
        self.name = name
        self.h = h
        self.sem = kb.new_sem("e_" + name)
        self.n = 0
        self.seen = {}
        self.inorder_safe = inorder_safe

    def _wait(self, tok):
        sem, val, eng = tok
        if self.seen.get(id(sem), 0) >= val:
            return
        self.h.wait_ge(sem, val)
        self.seen[id(sem)] = val

    def _deps(self, reads, writes):
        for b in reads:
            if b.w is not None:
                if not (b.w[2] is self and self.inorder_safe):
                    self._wait(b.w)
        for b in writes:
            if b.w is not None:
                if not (b.w[2] is self and (self.inorder_safe or b.psum)):
                    self._wait(b.w)
            for t in b.r:
                if t[2] is not self:
                    self._wait(t)

    def op(self, fn, reads=(), writes=(), signal=True):
        if any(b.psum for b in reads):
            writes = list(writes) + [b for b in reads if b.psum]
            reads = [b for b in reads if not b.psum]
        self._deps(reads, writes)
        ins = fn(self.h)
        if signal:
            self.n += 1
            ins.then_inc(self.sem, 1)
            tok = (self.sem, self.n, self)
        else:
            tok = (self.sem, self.n + 1, self)
        for b in reads:
            b.r.append(tok)
            if len(b.r) > 12:
                b.r = _prune(b.r)
        for b in writes:
            b.w = tok
            b.r = []
        return ins

    def dma(self, out, in_, reads=(), writes=(), slot=None, **kw):
        self._deps(reads, writes)
        if slot.dsem is None:
            slot.dsem = self.kb.new_sem("d_" + slot.name)
            self.kb.dma_slots.append(slot)
        ins = self.h.dma_start(out=out, in_=in_, **kw)
        slot.dcnt += 16
        ins.then_inc(slot.dsem, 16)
        tok = (slot.dsem, slot.dcnt, None)
        for b in reads:
            b.r.append(tok)
            if len(b.r) > 12:
                b.r = _prune(b.r)
        for b in writes:
            b.w = tok
            b.r = []
        return ins


def _prune(toks):
    best = {}
    for t in toks:
        k = id(t[0])
        if k not in best or best[k][1] < t[1]:
            best[k] = t
    return list(best.values())


class KB:
    def __init__(self):
        self.nc = bass.Bass("TRN2", target_bir_lowering=False)
        self.es = contextlib.ExitStack()
        self.nsem = 0
        nc = self.nc
        self.pe = Eng(self, "pe", nc.tensor, inorder_safe=True)
        self.act = Eng(self, "act", nc.scalar)
        self.dve = Eng(self, "dve", nc.vector)
        self.pool = Eng(self, "pool", nc.gpsimd)
        self.sp = Eng(self, "sp", nc.sync)
        self.out_toks = []
        self.dma_slots = []
        self.scopes = []

    def new_sem(self, name):
        self.nsem += 1
        return self.es.enter_context(self.nc.semaphore(f"{name}_{self.nsem}"))

    def push(self):
        self.scopes.append(contextlib.ExitStack())

    def pop(self):
        self.scopes.pop().close()

    def sbuf(self, name, shape, dt):
        es = self.scopes[-1] if self.scopes else self.es
        return es.enter_context(self.nc.sbuf_tensor(name, list(shape), dt))

    def psum(self, name, shape, dt):
        return self.es.enter_context(self.nc.psum_tensor(name, list(shape), dt))

    def din(self, name, shape, dt):
        return self.nc.dram_tensor(name, list(shape), dt, kind="ExternalInput").ap()

    def dout(self, name, shape, dt):
        return self.nc.dram_tensor(name, list(shape), dt, kind="ExternalOutput").ap()

    def dscratch(self, name, shape, dt):
        return self.nc.dram_tensor(name, list(shape), dt, kind="Internal").ap()

    def finish(self, bufs):
        for b in bufs:
            if b.w is not None:
                self.sp._wait(b.w)
        self.es.close()
        return self.nc


def make_ident(kb, dt, name="ident"):
    t32 = kb.sbuf(name + "32", [128, 128], F32)
    b = Buf(name)
    kb.pool.op(lambda e: e.memset(t32[:], 0.0), writes=[b])
    kb.pool.op(lambda e: e.affine_select(out=t32[:], in_=t32[:], pattern=[[-1, 128]],
                                         compare_op=ALU.not_equal, fill=1.0, base=0,
                                         channel_multiplier=1), reads=[b], writes=[b])
    if dt == F32:
        return t32, b
    t = kb.sbuf(name, [128, 128], dt)
    b2 = Buf(name + "c")
    kb.dve.op(lambda e: e.tensor_copy(out=t[:], in_=t32[:]), reads=[b], writes=[b2])
    return t, b2


def emit_norm_transpose(kb, x_rows, xnT_out, ntok, ident, identb, x_buf, xnT_buf, tag,
                        pt_banks=None):
    nc = kb.nc
    NG = ntok // 512
    xt = [kb.sbuf(f"{tag}_x{i}", [128, D], F32) for i in range(2)]
    xtb = [Buf(f"{tag}_x{i}") for i in range(2)]
    sq = kb.sbuf(f"{tag}_sq", [128, D], BF16)
    sqb = Buf(f"{tag}_sq")
    ss = [kb.sbuf(f"{tag}_ss{i}", [128, 1], F32) for i in range(2)]
    ssb = [Buf(f"{tag}_ss{i}") for i in range(2)]
    xn = [kb.sbuf(f"{tag}_xn{i}", [128, D], BF16) for i in range(2)]
    xnb = [Buf(f"{tag}_xn{i}") for i in range(2)]
    xT = [kb.sbuf(f"{tag}_xT{i}", [128, KC, 512], BF16) for i in range(2)]
    xTb = [Buf(f"{tag}_xT{i}") for i in range(2)]
    epsc = kb.sbuf(f"{tag}_eps", [128, 1], F32)
    epsb = Buf(f"{tag}_eps")
    kb.pool.op(lambda e: e.memset(epsc[:], EPS), writes=[epsb])
    if pt_banks is None:
        pt_banks = [(kb.psum(f"{tag}_pt{i}", [128, 8, 128], BF16), Buf(f"{tag}_pt{i}", psum=True)) for i in range(2)]
    xo = xnT_out.rearrange("(k p) t -> p k t", p=128)
    it = 0
    for g in range(NG):
        gs = g % 2
        for j in range(4):
            s = it % 2
            it += 1
            r0 = g * 512 + j * 128
            kb.sp.dma(xt[s][:], x_rows[r0:r0 + 128, :], reads=[x_buf], writes=[xtb[s]], slot=xtb[s])
            kb.act.op(lambda e: e.activation(out=sq[:], in_=xt[s][:], func=AF.Square, accum_out=ss[s][:]),
                      reads=[xtb[s]], writes=[sqb, ssb[s]])
            kb.act.op(lambda e: e.activation(out=ss[s][:], in_=ss[s][:], func=AF.Sqrt, scale=1.0 / D, bias=epsc[:]),
                      reads=[ssb[s], epsb], writes=[ssb[s]])
            kb.dve.op(lambda e: e.reciprocal(out=ss[s][:], in_=ss[s][:]), reads=[ssb[s]], writes=[ssb[s]])
            kb.act.op(lambda e: e.activation(out=xn[s][:], in_=xt[s][:], func=AF.Copy, scale=ss[s][:]),
                      reads=[xtb[s], ssb[s]], writes=[xnb[s]])
            for half in range(2):
                pt, ptb = pt_banks[half]
                for kk in range(8):
                    k = half * 8 + kk
                    kb.pe.op(lambda e: e.transpose(pt[:, kk, :], xn[s][:, k * 128:(k + 1) * 128], ident[:]),
                             reads=[xnb[s], identb], writes=[ptb], signal=(kk == 7))
                eng = kb.dve if half == 0 else kb.act
                if half == 0:
                    kb.dve.op(lambda e: e.tensor_copy(out=xT[gs][:, 0:8, j * 128:(j + 1) * 128], in_=pt[:]),
                              reads=[ptb], writes=[xTb[gs]])
                else:
                    kb.act.op(lambda e: e.copy(out=xT[gs][:, 8:16, j * 128:(j + 1) * 128], in_=pt[:]),
                              reads=[ptb], writes=[xTb[gs]])
        kb.sp.dma(xo[:, :, g * 512:(g + 1) * 512], xT[gs][:], reads=[xTb[gs]], writes=[xnT_buf], slot=xTb[gs])


def build_p0(ntok):
    kb = KB()
    x = kb.din("x", [ntok, D], F32)
    c = kb.din("c", [2, D], F32)
    wm = kb.din("wm", [4, D, 768], F32)
    bm = kb.din("bm", [4, 768], F32)
    xnT = kb.dout("xnT", [D, ntok], BF16)
    modo = kb.dout("modo", [4, 2, 768], F32)
    xb, xnTb, modob = Buf("x"), Buf("xnT"), Buf("modo")
    ident, identb = make_ident(kb, BF16)

    cT = kb.sbuf("cT", [128, KC, 2], F32)
    cTb = Buf("cT")
    for b in range(2):
        kb.sp.dma(cT[:, :, b], c[b].rearrange("(k p) -> p k", p=128), writes=[cTb], slot=cTb,
                  allow_slow_non_contiguous=True)
    kb.act.op(lambda e: e.activation(out=cT[:], in_=cT[:], func=AF.Silu), reads=[cTb], writes=[cTb])
    wsl = [kb.sbuf(f"wm{i}", [128, KC, 768], F32) for i in range(2)]
    wslb = [Buf(f"wm{i}") for i in range(2)]
    bsl = kb.sbuf("bsl", [2, 4, 768], F32)
    bslb = Buf("bsl")
    for b in range(2):
        kb.sp.dma(bsl[b:b + 1, :, :], bm[None, :, :], writes=[bslb], slot=bslb)
    mo = kb.sbuf("mo", [2, 4, 768], F32)
    mob = Buf("mo")
    pm = [(kb.psum(f"pm{i}", [128, 512], F32), Buf(f"pm{i}", psum=True)) for i in range(2)]
    for m in range(4):
        s = m % 2
        kb.sp.dma(wsl[s][:], wm[m].rearrange("(k p) n -> p k n", p=128), writes=[wslb[s]], slot=wslb[s])
        for hf in range(2):
            p_, pb = pm[hf]
            for k in range(KC):
                kb.pe.op(lambda e: e.matmul(p_[0:2, 0:384], lhsT=cT[:, k, :], rhs=wsl[s][:, k, hf * 384:(hf + 1) * 384],
                                            start=(k == 0), stop=(k == KC - 1)),
                         reads=[cTb, wslb[s]], writes=[pb], signal=(k == KC - 1))
            kb.dve.op(lambda e: e.tensor_tensor(out=mo[:, m, hf * 384:(hf + 1) * 384], in0=p_[0:2, 0:384],
                                                in1=bsl[:, m, hf * 384:(hf + 1) * 384], op=ALU.add),
                      reads=[pb, bslb], writes=[mob])
    kb.sp.dma(modo.rearrange("m b n -> b m n"), mo[:], reads=[mob], writes=[modob], slot=mob)

    emit_norm_transpose(kb, x, xnT, ntok, ident, identb, xb, xnTb, "nt")
    return kb.finish([xnTb, modob])


class WStream:
    def __init__(self, kb, nslots=3, tag="ws"):
        self.kb = kb
        self.t = [kb.sbuf(f"{tag}{i}", [128, 16, 512], BF16) for i in range(nslots)]
        self.b = [Buf(f"{tag}{i}") for i in range(nslots)]
        self.i = 0

    def load(self, w, wbuf, k0, nk, c0, ncol=512):
        s = self.i % len(self.t)
        self.i += 1
        src = w[k0 * 128:(k0 + nk) * 128, c0:c0 + ncol].rearrange("(k p) n -> p k n", p=128)
        self.kb.sp.dma(self.t[s][:, 0:nk, 0:ncol], src, reads=[wbuf], writes=[self.b[s]], slot=self.b[s])
        return self.t[s], self.b[s]


def cast_weight(kb, dst, src, buf, rows_per=128):
    K = src.shape[0]
    for r in range(0, K, rows_per):
        kb.pool.dma(dst[r:r + rows_per, :], src[r:r + rows_per, :], writes=[buf], slot=buf)


def load_pk(kb, dst, src_vec, buf, nk):
    kb.sp.dma(dst, src_vec.rearrange("(k p) -> p k", p=128), writes=[buf], slot=buf,
              allow_slow_non_contiguous=True)


class NormT:
    def __init__(self, kb, tag, ident, identb, pt_banks):
        self.kb = kb
        self.ident, self.identb = ident, identb
        self.xt = [kb.sbuf(f"{tag}_x{i}", [128, D], F32) for i in range(2)]
        self.xtb = [Buf(f"{tag}_x{i}") for i in range(2)]
        self.ss = [kb.sbuf(f"{tag}_ss{i}", [128, 1], F32) for i in range(2)]
        self.ssb = [Buf(f"{tag}_ss{i}") for i in range(2)]
        self.xn = [kb.sbuf(f"{tag}_xn{i}", [128, D], BF16) for i in range(2)]
        self.xnb = [Buf(f"{tag}_xn{i}") for i in range(2)]
        self.epsc = kb.sbuf(f"{tag}_eps", [128, 1], F32)
        self.epsb = Buf(f"{tag}_eps")
        kb.pool.op(lambda e: e.memset(self.epsc[:], EPS), writes=[self.epsb])
        self.pt = pt_banks
        self.it = 0

    def tile(self, src_rows, src_buf, dstT, dstTb, col0):
        kb = self.kb
        s = self.it % 2
        self.it += 1
        xt, xtb, ss, ssb, xn, xnb = self.xt[s], self.xtb[s], self.ss[s], self.ssb[s], self.xn[s], self.xnb[s]
        kb.sp.dma(xt[:], src_rows, reads=[src_buf], writes=[xtb], slot=xtb)
        kb.act.op(lambda e: e.activation(out=xn[:], in_=xt[:], func=AF.Square, accum_out=ss[:]),
                  reads=[xtb], writes=[xnb, ssb])
        kb.act.op(lambda e: e.activation(out=ss[:], in_=ss[:], func=AF.Sqrt, scale=1.0 / D, bias=self.epsc[:]),
                  reads=[ssb, self.epsb], writes=[ssb])
        kb.dve.op(lambda e: e.reciprocal(out=ss[:], in_=ss[:]), reads=[ssb], writes=[ssb])
        kb.act.op(lambda e: e.activation(out=xn[:], in_=xt[:], func=AF.Copy, scale=ss[:]),
                  reads=[xtb, ssb], writes=[xnb])
        for half in range(2):
            pt, ptb = self.pt[half]
            for kk in range(8):
                k = half * 8 + kk
                kb.pe.op(lambda e: e.transpose(pt[:, kk, :], xn[:, k * 128:(k + 1) * 128], self.ident[:]),
                         reads=[xnb, self.identb], writes=[ptb], signal=(kk == 7))
            if half == 0:
                kb.dve.op(lambda e: e.tensor_copy(out=dstT[:, 0:8, col0:col0 + 128], in_=pt[:]),
                          reads=[ptb], writes=[dstTb])
            else:
                kb.act.op(lambda e: e.copy(out=dstT[:, 8:16, col0:col0 + 128], in_=pt[:]),
                          reads=[ptb], writes=[dstTb])


TG = 512
DFF = 5632
FC = DFF // 128
CCH = 1024
HALO = 32


def emit_p2(kb, ntok, a, ident, identb):
    nc = kb.nc
    NTG = ntok // TG
    PB = [(kb.psum(f"pb{i}", [128, 512], F32), Buf(f"pb{i}", psum=True)) for i in range(6)]
    PT = [(kb.psum(f"ptb{i}", [128, 8, 128], BF16), Buf(f"ptb{i}", psum=True)) for i in range(2)]
    wnames = ["w_uc", "w_gl", "w_ao", "w_go", "w_co", "w_out", "w_gu", "w_dn"]
    wb = {}
    for n in wnames:
        src = a[n]
        dst = kb.dscratch(n + "_bf", list(src.shape), BF16)
        b = Buf(n + "_bf")
        cast_weight(kb, dst, src, b)
        wb[n] = (dst, b)
    ws = WStream(kb, 3)
    modv = a["mod"]
    pv = kb.sbuf("pv", [128, 8, KC], F32)
    pvb = Buf("pv")
    load_pk(kb, pv[:, 0, :], modv[0, 0:D], pvb, KC)
    load_pk(kb, pv[:, 1, :], modv[0, D:2 * D], pvb, KC)
    load_pk(kb, pv[:, 2, :], a["g_mix"], pvb, KC)
    load_pk(kb, pv[:, 3, :], modv[1, 0:D], pvb, KC)
    load_pk(kb, pv[:, 4, :], modv[1, D:2 * D], pvb, KC)
    load_pk(kb, pv[:, 5, :], a["g_ffn"], pvb, KC)
    kb.dve.op(lambda e: e.scalar_tensor_tensor(out=pv[:, 6, :], in0=pv[:, 1, :], scalar=1.0, in1=pv[:, 2, :],
                                               op0=ALU.add, op1=ALU.mult), reads=[pvb], writes=[pvb])
    kb.dve.op(lambda e: e.scalar_tensor_tensor(out=pv[:, 7, :], in0=pv[:, 4, :], scalar=1.0, in1=pv[:, 5, :],
                                               op0=ALU.add, op1=ALU.mult), reads=[pvb], writes=[pvb])
    gbc = kb.sbuf("gbc", [128, 2, D], F32)
    gbcb = Buf("gbc")
    for i in range(2):
        kb.sp.dma(gbc[:, i, :], modv[i, 2 * D:3 * D].partition_broadcast(128), writes=[gbcb], slot=gbcb)
    cw = kb.sbuf("cw", [128, 8, 31], F32)
    cp = kb.sbuf("cp", [128, 3, 8], F32)
    cwb = Buf("cw")
    for c in range(8):
        kb.sp.dma(cw[:, c, :], a["conv_w"][:, c * 128:(c + 1) * 128].rearrange("k p -> p k"), writes=[cwb],
                  slot=cwb, allow_slow_non_contiguous=True)
    load_pk(kb, cp[:, 0, :], a["conv_b"], cwb, 8)
    load_pk(kb, cp[:, 1, :], a["ln_g"], cwb, 8)
    load_pk(kb, cp[:, 2, :], a["ln_b"], cwb, 8)
    flag = kb.sbuf("flag", [128, 1], F32)
    kb.sp.dma(flag[:], a["halo_flag"], writes=[cwb], slot=cwb)
    ones32 = kb.sbuf("ones32", [128, 128], F32)
    onesb = Buf("ones32")
    kb.pool.op(lambda e: e.memset(ones32[:], 1.0), writes=[onesb])
    epsc = kb.sbuf("epsc", [128, 1], F32)
    kb.pool.op(lambda e: e.memset(epsc[:], EPS), writes=[onesb])

    hT = kb.sbuf("hT", [128, KC, TG], BF16)
    hTb = Buf("hT")
    hh = kb.sbuf("hh", [128, KC, HALO], BF16)
    hhb = Buf("hh")
    ubuf = kb.sbuf("ubuf", [128, 8, HALO + TG], F32)
    ub = [Buf(f"ub{c}") for c in range(8)]
    acc = kb.sbuf("acc", [128, 8, TG], F32)
    accb = [Buf(f"acc{c}") for c in range(8)]
    sq = [kb.sbuf(f"sq{i}", [128, TG], F32) for i in range(2)]
    sqb = [Buf(f"sq{i}") for i in range(2)]
    sg = [kb.sbuf(f"sg{i}", [128, TG], F32) for i in range(2)]
    sgb = [Buf(f"sg{i}") for i in range(2)]
    sgh = kb.sbuf("sgh", [128, HALO], F32)
    sghb = Buf("sgh")
    mean = kb.sbuf("mean", [128, TG], F32)
    rstd = kb.sbuf("rstd", [128, TG], F32)
    msq = kb.sbuf("msq", [128, TG], F32)
    statb = Buf("stat")
    big = kb.sbuf("big", [128, FC, TG], BF16)
    bigb = Buf("big")
    cT = big[:, 0:8, :]
    oa = big[:, 8:12, :]
    od = big[:, 12:20, :]
    mg = big[:, 20:36, :]
    macc = acc[:, 0:4, :]
    maccb = accb[0:4]
    tmp = sq
    tmpb = sqb
    xp = [kb.sbuf(f"xp{i}", [128, 512], F32) for i in range(2)]
    xpb = [Buf(f"xp{i}") for i in range(2)]
    nt = NormT(kb, "nt", ident, identb, PT)
    xTo, xTob = hT, hTb

    x_rows, xb_in = a["x"], a["x_buf"]
    xout, xoutb = a["xout"], a["xout_buf"]
    xnT_h, xnTb_in = a["xnT_h"], a["xnT_h_buf"]
    xnT_o, xnTob = a["xnT_out"], a["xnT_out_buf"]
    xnT_v = xnT_h.rearrange("(k p) t -> p k t", p=128)
    xnTo_v = xnT_o.rearrange("(k p) t -> p k t", p=128)
    oaT_v = a["o_aT"].rearrange("(k p) t -> p k t", p=128)
    odT_v = a["o_dT"].rearrange("(k p) t -> p k t", p=128)
    cnt = {"pb": 0, "sg": 0, "tmp": 0, "xp": 0}

    def rot(name, n=2):
        v = cnt[name] % n
        cnt[name] += 1
        return v

    for g in range(NTG):
        t0 = g * TG
        kb.sp.dma(hT[:], xnT_v[:, :, HALO + t0:HALO + t0 + TG], reads=[xnTb_in], writes=[hTb], slot=hTb)
        for k in range(KC):
            kb.act.op(lambda e: e.activation(out=hT[:, k, :], in_=hT[:, k, :], func=AF.Identity,
                                             scale=pv[:, 6, k:k + 1], bias=pv[:, 0, k:k + 1]),
                      reads=[hTb, pvb], writes=[hTb])
        if g == 0:
            kb.sp.dma(hh[:], xnT_v[:, :, 0:HALO], reads=[xnTb_in], writes=[hhb], slot=hhb)
            for k in range(KC):
                kb.act.op(lambda e: e.activation(out=hh[:, k, :], in_=hh[:, k, :], func=AF.Identity,
                                                 scale=pv[:, 6, k:k + 1], bias=pv[:, 0, k:k + 1]),
                          reads=[hhb, pvb], writes=[hhb])
        else:
            for c in range(8):
                kb.pool.op(lambda e: e.tensor_copy(out=ubuf[:, c, 0:HALO], in_=ubuf[:, c, TG:TG + HALO]),
                           reads=[ub[c]], writes=[ub[c]])
        w_uc, w_ucb = wb["w_uc"]
        for cg in range(4):
            wt, wtb = ws.load(w_uc, w_ucb, 0, KC, cg * 512)
            for cc in range(4):
                c = cg * 4 + cc
                p_, pb_ = PB[rot("pb")]
                for k in range(KC):
                    kb.pe.op(lambda e: e.matmul(p_[:], lhsT=wt[:, k, cc * 128:(cc + 1) * 128], rhs=hT[:, k, :],
                                                start=(k == 0), stop=(k == KC - 1)),
                             reads=[wtb, hTb], writes=[pb_], signal=(k == KC - 1))
                if g == 0:
                    ph, phb = PB[4 + (c % 2)]
                    for k in range(KC):
                        kb.pe.op(lambda e: e.matmul(ph[:, 0:HALO], lhsT=wt[:, k, cc * 128:(cc + 1) * 128],
                                                    rhs=hh[:, k, :], start=(k == 0), stop=(k == KC - 1)),
                                 reads=[wtb, hhb], writes=[phb], signal=(k == KC - 1))
                if c < 8:
                    kb.act.op(lambda e: e.copy(out=ubuf[:, c, HALO:], in_=p_[:]), reads=[pb_], writes=[ub[c]])
                    if g == 0:
                        kb.act.op(lambda e: e.copy(out=ubuf[:, c, 0:HALO], in_=ph[:, 0:HALO]), reads=[phb],
                                  writes=[ub[c]])
                else:
                    s_ = rot("sg")
                    kb.act.op(lambda e: e.activation(out=sg[s_][:], in_=p_[:], func=AF.Sigmoid), reads=[pb_],
                              writes=[sgb[s_]])
                    kb.dve.op(lambda e: e.tensor_tensor(out=ubuf[:, c - 8, HALO:], in0=ubuf[:, c - 8, HALO:],
                                                        in1=sg[s_][:], op=ALU.mult),
                              reads=[sgb[s_], ub[c - 8]], writes=[ub[c - 8]])
                    if g == 0:
                        kb.act.op(lambda e: e.activation(out=sgh[:], in_=ph[:, 0:HALO], func=AF.Sigmoid),
                                  reads=[phb], writes=[sghb])
                        kb.dve.op(lambda e: e.scalar_tensor_tensor(out=ubuf[:, c - 8, 0:HALO], in0=sgh[:],
                                                                   scalar=flag[:, 0:1], in1=ubuf[:, c - 8, 0:HALO],
                                                                   op0=ALU.mult, op1=ALU.mult),
                                  reads=[sghb, ub[c - 8], cwb], writes=[ub[c - 8]])
        OFF = HALO - 30
        eng_of = [kb.dve] * 8
        for k in range(31):
            for c in range(8):
                eng = eng_of[c]
                src = ubuf[:, c, OFF + k:OFF + k + TG]
                if k == 0:
                    eng.op(lambda e: e.tensor_scalar(out=acc[:, c, :], in0=src, scalar1=cw[:, c, 0:1],
                                                     scalar2=cp[:, 0, c:c + 1], op0=ALU.mult, op1=ALU.add),
                           reads=[ub[c], cwb], writes=[accb[c]])
                else:
                    eng.op(lambda e: e.scalar_tensor_tensor(out=acc[:, c, :], in0=src, scalar=cw[:, c, k:k + 1],
                                                            in1=acc[:, c, :], op0=ALU.mult, op1=ALU.add),
                           reads=[ub[c], cwb, accb[c]], writes=[accb[c]])
        (p1, p1b), (p2, p2b) = PB[2], PB[3]
        for c in range(8):
            s_ = rot("tmp")
            kb.act.op(lambda e: e.activation(out=sq[s_][:], in_=acc[:, c, :], func=AF.Square), reads=[accb[c]],
                      writes=[sqb[s_]])
            kb.pe.op(lambda e: e.matmul(p1[:], lhsT=ones32[:], rhs=acc[:, c, :], start=(c == 0), stop=(c == 7)),
                     reads=[onesb, accb[c]], writes=[p1b])
            kb.pe.op(lambda e: e.matmul(p2[:], lhsT=ones32[:], rhs=sq[s_][:], start=(c == 0), stop=(c == 7)),
                     reads=[onesb, sqb[s_]], writes=[p2b])
        kb.act.op(lambda e: e.activation(out=mean[:], in_=p1[:], func=AF.Copy, scale=1.0 / CCH), reads=[p1b],
                  writes=[statb])
        kb.dve.op(lambda e: e.tensor_tensor(out=msq[:], in0=mean[:], in1=mean[:], op=ALU.mult), reads=[statb],
                  writes=[statb])
        kb.dve.op(lambda e: e.scalar_tensor_tensor(out=rstd[:], in0=p2[:], scalar=1.0 / CCH, in1=msq[:],
                                                   op0=ALU.mult, op1=ALU.subtract), reads=[p2b, statb],
                  writes=[statb])
        kb.act.op(lambda e: e.activation(out=rstd[:], in_=rstd[:], func=AF.Sqrt, bias=epsc[:]),
                  reads=[statb, onesb], writes=[statb])
        kb.dve.op(lambda e: e.reciprocal(out=rstd[:], in_=rstd[:]), reads=[statb], writes=[statb])
        for c in range(8):
            kb.dve.op(lambda e: e.tensor_tensor(out=acc[:, c, :], in0=acc[:, c, :], in1=mean[:], op=ALU.subtract),
                      reads=[accb[c], statb], writes=[accb[c]])
            kb.pool.op(lambda e: e.tensor_tensor(out=acc[:, c, :], in0=acc[:, c, :], in1=rstd[:], op=ALU.mult),
                       reads=[accb[c], statb], writes=[accb[c]])
            kb.act.op(lambda e: e.activation(out=cT[:, c, :], in_=acc[:, c, :], func=AF.Silu,
                                             scale=cp[:, 1, c:c + 1], bias=cp[:, 2, c:c + 1]),
                      reads=[accb[c], cwb], writes=[bigb])
        kb.sp.dma(oa, oaT_v[:, :, t0:t0 + TG], reads=[a["o_buf"]], writes=[bigb], slot=bigb)
        kb.sp.dma(od, odT_v[:, :, t0:t0 + TG], reads=[a["o_buf"]], writes=[bigb], slot=bigb)
        w_gl, w_glb = wb["w_gl"]
        ywl = [(wb["w_ao"], oa, 4), (wb["w_go"], od, 8), (wb["w_co"], cT, 8)]
        for og in range(4):
            for br in range(3):
                gw, gwb = ws.load(w_gl, w_glb, 0, KC, br * D + og * 512)
                (yw_d, yw_db), ysrc, ynk = ywl[br]
                yw, ywb = ws.load(yw_d, yw_db, 0, ynk, og * 512)
                for cc in range(4):
                    pg, pgb = PB[rot("pb")]
                    for k in range(KC):
                        kb.pe.op(lambda e: e.matmul(pg[:], lhsT=gw[:, k, cc * 128:(cc + 1) * 128], rhs=hT[:, k, :],
                                                    start=(k == 0), stop=(k == KC - 1)),
                                 reads=[gwb, hTb], writes=[pgb], signal=(k == KC - 1))
                    py, pyb = PB[2 + (cnt["pb"] % 2)]
                    for k in range(ynk):
                        kb.pe.op(lambda e: e.matmul(py[:], lhsT=yw[:, k, cc * 128:(cc + 1) * 128], rhs=ysrc[:, k, :],
                                                    start=(k == 0), stop=(k == ynk - 1)),
                                 reads=[ywb, bigb], writes=[pyb], signal=(k == ynk - 1))
                    s_ = rot("sg")
                    kb.act.op(lambda e: e.activation(out=sg[s_][:], in_=pg[:], func=AF.Sigmoid), reads=[pgb],
                              writes=[sgb[s_]])
                    if br == 0:
                        kb.dve.op(lambda e: e.tensor_tensor(out=macc[:, cc, :], in0=sg[s_][:], in1=py[:], op=ALU.mult),
                                  reads=[sgb[s_], pyb], writes=[maccb[cc]])
                    else:
                        t_ = rot("tmp")
                        kb.dve.op(lambda e: e.tensor_tensor(out=tmp[t_][:], in0=sg[s_][:], in1=py[:], op=ALU.mult),
                                  reads=[sgb[s_], pyb], writes=[tmpb[t_]])
                        if br == 1:
                            kb.pool.op(lambda e: e.tensor_tensor(out=macc[:, cc, :], in0=macc[:, cc, :], in1=tmp[t_][:],
                                                                 op=ALU.add), reads=[maccb[cc], tmpb[t_]],
                                       writes=[maccb[cc]])
                        else:
                            kb.pool.op(lambda e: e.tensor_tensor(out=mg[:, og * 4 + cc, :], in0=macc[:, cc, :],
                                                                 in1=tmp[t_][:], op=ALU.add),
                                       reads=[maccb[cc], tmpb[t_]], writes=[bigb])
        w_o, w_ob = wb["w_out"]
        for fg in range(4):
            wt, wtb = ws.load(w_o, w_ob, 0, KC, fg * 512)
            for j in range(4):
                r0 = t0 + j * 128
                p_, pb_ = PB[4 + rot("pb")]
                for k in range(KC):
                    kb.pe.op(lambda e: e.matmul(p_[:], lhsT=mg[:, k, j * 128:(j + 1) * 128], rhs=wt[:, k, :],
                                                start=(k == 0), stop=(k == KC - 1)),
                             reads=[wtb, bigb], writes=[pb_], signal=(k == KC - 1))
                x_ = rot("xp")
                kb.sp.dma(xp[x_][:], x_rows[r0:r0 + 128, fg * 512:(fg + 1) * 512], reads=[xb_in], writes=[xpb[x_]],
                          slot=xpb[x_])
                t_ = rot("tmp")
                kb.dve.op(lambda e: e.tensor_tensor(out=tmp[t_][:], in0=p_[:], in1=gbc[:, 0, fg * 512:(fg + 1) * 512],
                                                    op=ALU.mult), reads=[pb_, gbcb], writes=[tmpb[t_]])
                kb.pool.op(lambda e: e.tensor_tensor(out=xp[x_][:], in0=xp[x_][:], in1=tmp[t_][:], op=ALU.add),
                           reads=[xpb[x_], tmpb[t_]], writes=[xpb[x_]])
                kb.sp.dma(xout[r0:r0 + 128, fg * 512:(fg + 1) * 512], xp[x_][:], reads=[xpb[x_]], writes=[xoutb],
                          slot=xpb[x_])
        for j in range(4):
            r0 = t0 + j * 128
            nt.tile(xout[r0:r0 + 128, :], xoutb, hT, hTb, j * 128)
        for k in range(KC):
            kb.act.op(lambda e: e.activation(out=hT[:, k, :], in_=hT[:, k, :], func=AF.Identity,
                                             scale=pv[:, 7, k:k + 1], bias=pv[:, 3, k:k + 1]),
                      reads=[hTb, pvb], writes=[hTb])
        w_gu, w_gub = wb["w_gu"]
        for t in range(FC // 4):
            wa, wab = ws.load(w_gu, w_gub, 0, KC, t * 512)
            wv, wvb = ws.load(w_gu, w_gub, 0, KC, DFF + t * 512)
            for cc in range(4):
                f = t * 4 + cc
                pa, pab = PB[rot("pb")]
                for k in range(KC):
                    kb.pe.op(lambda e: e.matmul(pa[:], lhsT=wa[:, k, cc * 128:(cc + 1) * 128], rhs=hT[:, k, :],
                                                start=(k == 0), stop=(k == KC - 1)),
                             reads=[wab, hTb], writes=[pab], signal=(k == KC - 1))
                pv_, pvb_ = PB[2 + (cnt["pb"] % 2)]
                for k in range(KC):
                    kb.pe.op(lambda e: e.matmul(pv_[:], lhsT=wv[:, k, cc * 128:(cc + 1) * 128], rhs=hT[:, k, :],
                                                start=(k == 0), stop=(k == KC - 1)),
                             reads=[wvb, hTb], writes=[pvb_], signal=(k == KC - 1))
                s_ = rot("sg")
                kb.act.op(lambda e: e.activation(out=sg[s_][:], in_=pa[:], func=AF.Silu), reads=[pab],
                          writes=[sgb[s_]])
                kb.dve.op(lambda e: e.tensor_tensor(out=big[:, f, :], in0=sg[s_][:], in1=pv_[:], op=ALU.mult),
                          reads=[sgb[s_], pvb_], writes=[bigb])
        w_dn, w_dnb = wb["w_dn"]
        parts = [(0, 16), (16, 16), (32, 12)]
        for fg in range(4):
            for pi, (k0, nk) in enumerate(parts):
                wt, wtb = ws.load(w_dn, w_dnb, k0, nk, fg * 512)
                for j in range(4):
                    p_, pb_ = PB[j]
                    for kk in range(nk):
                        kb.pe.op(lambda e: e.matmul(p_[:], lhsT=big[:, k0 + kk, j * 128:(j + 1) * 128],
                                                    rhs=wt[:, kk, :], start=(pi == 0 and kk == 0),
                                                    stop=(pi == 2 and kk == nk - 1)),
                                 reads=[wtb, bigb], writes=[pb_], signal=(kk == nk - 1))
            for j in range(4):
                r0 = t0 + j * 128
                p_, pb_ = PB[j]
                x_ = rot("xp")
                kb.sp.dma(xp[x_][:], xout[r0:r0 + 128, fg * 512:(fg + 1) * 512], reads=[xoutb], writes=[xpb[x_]],
                          slot=xpb[x_])
                t_ = rot("tmp")
                kb.dve.op(lambda e: e.tensor_tensor(out=tmp[t_][:], in0=p_[:], in1=gbc[:, 1, fg * 512:(fg + 1) * 512],
                                                    op=ALU.mult), reads=[pb_, gbcb], writes=[tmpb[t_]])
                kb.pool.op(lambda e: e.tensor_tensor(out=xp[x_][:], in0=xp[x_][:], in1=tmp[t_][:], op=ALU.add),
                           reads=[xpb[x_], tmpb[t_]], writes=[xpb[x_]])
                kb.sp.dma(xout[r0:r0 + 128, fg * 512:(fg + 1) * 512], xp[x_][:], reads=[xpb[x_]], writes=[xoutb],
                          slot=xpb[x_])
        for j in range(4):
            r0 = t0 + j * 128
            nt.tile(xout[r0:r0 + 128, :], xoutb, xTo, xTob, j * 128)
        kb.sp.dma(xnTo_v[:, :, t0:t0 + TG], xTo[:], reads=[xTob], writes=[xnTob], slot=xTob)


def build_p2(ntok):
    kb = KB()
    a = {}
    a["x"] = kb.din("x", [ntok, D], F32)
    a["xnT_h"] = kb.din("xnT_h", [D, HALO + ntok], BF16)
    a["halo_flag"] = kb.din("halo_flag", [128, 1], F32)
    a["o_aT"] = kb.din("o_aT", [512, ntok], BF16)
    a["o_dT"] = kb.din("o_dT", [1024, ntok], BF16)
    a["mod"] = kb.din("mod", [2, 3 * D], F32)
    a["g_mix"] = kb.din("g_mix", [D], F32)
    a["g_ffn"] = kb.din("g_ffn", [D], F32)
    a["w_uc"] = kb.din("w_uc", [D, 2048], F32)
    a["w_gl"] = kb.din("w_gl", [D, 3 * D], F32)
    a["w_ao"] = kb.din("w_ao", [512, D], F32)
    a["w_go"] = kb.din("w_go", [1024, D], F32)
    a["w_co"] = kb.din("w_co", [1024, D], F32)
    a["w_out"] = kb.din("w_out", [D, D], F32)
    a["w_gu"] = kb.din("w_gu", [D, 2 * DFF], F32)
    a["w_dn"] = kb.din("w_dn", [DFF, D], F32)
    a["conv_w"] = kb.din("conv_w", [31, CCH], F32)
    a["conv_b"] = kb.din("conv_b", [CCH], F32)
    a["ln_g"] = kb.din("ln_g", [CCH], F32)
    a["ln_b"] = kb.din("ln_b", [CCH], F32)
    a["xout"] = kb.dout("xout", [ntok, D], F32)
    a["xnT_out"] = kb.dout("xnT_out", [D, ntok], BF16)
    for n in ["x_buf", "xnT_h_buf", "o_buf", "xout_buf", "xnT_out_buf"]:
        a[n] = Buf(n)
    ident, identb = make_ident(kb, BF16)
    emit_p2(kb, ntok, a, ident, identb)
    return kb.finish([a["xout_buf"], a["xnT_out_buf"]])


DBG = set()
NTOKC = 768 + 384 + 256 + 4
C_U2, C_SL2, C_L2, C_IA, C_IB, C_MP, C_MC, C_ONE, C_MISC = range(9)
TWO_PI = 6.283185307179586
CW1 = 6.28125
CW2 = TWO_PI - CW1


def host_consts():
    c = np.zeros((128, 9, 128), np.float32)
    m = np.arange(128)[:, None]
    i = np.arange(128)[None, :]
    same = (m // 64) == (i // 64)
    c[:, C_U2] = (m <= i) & same
    c[:, C_SL2] = (m > i) & same
    c[:, C_L2] = (m >= i) & same
    c[:, C_IA] = (m < 64) * np.ones((1, 128))
    c[:, C_IB] = (m >= 64) * np.ones((1, 128))
    c[:, C_MP] = (m >= i)
    c[:, C_MC] = (m <= i)
    c[:, C_ONE] = 1.0
    inv = (500000.0 ** (-np.arange(0, 32, 2, dtype=np.float32) / 32)).astype(np.float32)
    c[:, C_MISC, 0:16] = inv[None, :]
    return c


class PRegion:
    def __init__(self, kb):
        self.banks = [kb.psum(f"bank{i}", [128, 512], F32) for i in range(8)]

    def reg(self, bank, r0, nr=1):
        return self.banks[bank][:, r0 * 128:(r0 + nr) * 128]


def barrier(kb, extra_bufs=()):
    engs = [kb.pe, kb.act, kb.dve, kb.pool, kb.sp]
    for e in engs:
        for f in engs:
            if f is not e and f.n > 0:
                e._wait((f.sem, f.n, f))
        for b in kb.dma_slots:
            if b.dsem is not None and b.dcnt > 0:
                e._wait((b.dsem, b.dcnt, None))


def emit_p1(kb, S, a, phases="ABC"):
    nc = kb.nc
    NT = S // 128
    NG = S // 512
    NU = S // 2048
    PR = PRegion(kb)
    bankb = [Buf(f"bank{i}", psum=True) for i in range(8)]
    qk_tok = kb.dscratch("qk_tok", [S, 6, 128], BF16)
    v_tok = kb.dscratch("v_tok", [S, 3, 128], BF16)
    zs_tok = kb.dscratch("zs_tok", [S, 256], F32)
    gT = kb.dout("gT", [6, 128, S], F32) if 'gtout' in DBG else kb.dscratch("gT", [6, 128, S], F32)
    qkb, vtb, zsb, gTb = Buf("qk_tok"), Buf("v_tok"), Buf("zs_tok"), Buf("gT")
    cst = kb.sbuf("cst", [128, 9, 128], F32)
    cstb = Buf("cst")
    kb.sp.dma(cst[:], a["consts"], writes=[cstb], slot=cstb)
    ident32 = kb.sbuf("ident32", [128, 128], F32)
    identb32 = Buf("ident32b")
    identbf = kb.sbuf("identbf", [128, 128], BF16)
    kb.pool.op(lambda e: e.memset(ident32[:], 0.0), writes=[identb32])
    kb.pool.op(lambda e: e.affine_select(out=ident32[:], in_=ident32[:], pattern=[[-1, 128]],
                                         compare_op=ALU.not_equal, fill=1.0, base=0, channel_multiplier=1),
               reads=[identb32], writes=[identb32])
    kb.dve.op(lambda e: e.tensor_copy(out=identbf[:], in_=ident32[:]), reads=[identb32], writes=[identb32])
    GB = kb.sbuf("GB", [128, NT, 8], F32)
    GBb = Buf("GB")
    epsc = kb.sbuf("epsc", [128, 1], F32)
    onec = kb.sbuf("onec", [128, 1], F32)
    kb.pool.op(lambda e: e.memset(epsc[:], EPS), writes=[cstb])
    kb.pool.op(lambda e: e.memset(onec[:], 1.0), writes=[cstb])
    sl1 = kb.sbuf("sl1", [128, 129], F32)
    kb.dve.op(lambda e: e.tensor_copy(out=sl1[:, 0:128], in_=cst[:, C_SL2, :]), reads=[cstb], writes=[cstb])
    kb.dve.op(lambda e: e.tensor_copy(out=sl1[:, 128:129], in_=cst[:, C_ONE, 0:1]), reads=[cstb], writes=[cstb])
    maskbf = kb.sbuf("maskbf", [128, 256], BF16)
    kb.dve.op(lambda e: e.tensor_copy(out=maskbf[:, 0:128], in_=cst[:, C_MP, :]), reads=[cstb], writes=[cstb])
    kb.dve.op(lambda e: e.tensor_copy(out=maskbf[:, 128:256], in_=cst[:, C_MC, :]), reads=[cstb], writes=[cstb])
    onesbf = kb.sbuf("onesbf", [128, 128], BF16)
    kb.dve.op(lambda e: e.tensor_copy(out=onesbf[:], in_=cst[:, C_ONE, :]), reads=[cstb], writes=[cstb])

    kb.push()
    wtok = kb.sbuf("wtok", [128, KC, 1536], BF16)
    wfm = kb.sbuf("wfm", [128, KC, 768], BF16)
    wAb = Buf("wA")
    for k in range(KC):
        kb.pool.dma(wtok[:, k, 0:NTOKC], a["w_tok"][k * 128:(k + 1) * 128, :], writes=[wAb], slot=wAb)
        kb.pool.dma(wfm[:, k, :], a["w_fm"][k * 128:(k + 1) * 128, :], writes=[wAb], slot=wAb)
    pv = kb.sbuf("pv", [128, 4, KC], F32)
    pvb = Buf("pv")
    load_pk(kb, pv[:, 0, :], a["mod"][0, 0:D], pvb, KC)
    load_pk(kb, pv[:, 1, :], a["mod"][0, D:2 * D], pvb, KC)
    load_pk(kb, pv[:, 2, :], a["g_mix"], pvb, KC)
    kb.dve.op(lambda e: e.scalar_tensor_tensor(out=pv[:, 3, :], in0=pv[:, 1, :], scalar=1.0, in1=pv[:, 2, :],
                                               op0=ALU.add, op1=ALU.mult), reads=[pvb], writes=[pvb])
    gqk = kb.sbuf("gqk", [128, 6, 128], F32)
    for sl in range(6):
        src = a["q_norm_g"] if sl < 3 else a["k_norm_g"]
        kb.sp.dma(gqk[:, sl, :], src.partition_broadcast(128), writes=[pvb], slot=pvb)
    gcw = kb.sbuf("gcw", [128, 6, 4], F32)
    for ch in range(6):
        kb.sp.dma(gcw[:, ch, :], a["gconv_w"][:, ch * 128:(ch + 1) * 128].rearrange("k p -> p k"), writes=[pvb],
                  slot=pvb, allow_slow_non_contiguous=True)
    cs = kb.sbuf("cs", [128, 2, NT, 16], F32)
    kb.push()
    posi = kb.sbuf("posi", [128, NT], I32)
    posf = kb.sbuf("posf", [128, NT], F32)
    ang = kb.sbuf("ang", [128, 2, NT, 16], F32)
    kq = kb.sbuf("kq", [128, 2, NT, 16], F32)
    ki = kb.sbuf("ki", [128, 2, NT, 16], I32)
    rpb = Buf("rope")
    kb.sp.dma(posi[:], a["pos"].rearrange("(t p) -> p t", p=128), writes=[rpb], slot=rpb,
              allow_slow_non_contiguous=True)
    kb.dve.op(lambda e: e.tensor_copy(out=posf[:], in_=posi[:]), reads=[rpb], writes=[rpb])
    for f in range(16):
        kb.dve.op(lambda e: e.tensor_scalar(out=ang[:, 0, :, f], in0=posf[:], scalar1=cst[:, C_MISC, f:f + 1],
                                            scalar2=None, op0=ALU.mult), reads=[rpb, cstb], writes=[rpb])
    kb.dve.op(lambda e: e.tensor_scalar(out=ang[:, 1], in0=ang[:, 0], scalar1=float(np.pi / 2), scalar2=None,
                                        op0=ALU.add), reads=[rpb], writes=[rpb])
    kb.dve.op(lambda e: e.tensor_scalar(out=kq[:], in0=ang[:], scalar1=float(1.0 / TWO_PI), scalar2=None,
                                        op0=ALU.mult), reads=[rpb], writes=[rpb])
    kb.dve.op(lambda e: e.tensor_copy(out=ki[:], in_=kq[:]), reads=[rpb], writes=[rpb])
    kb.dve.op(lambda e: e.tensor_copy(out=kq[:], in_=ki[:]), reads=[rpb], writes=[rpb])
    kb.dve.op(lambda e: e.scalar_tensor_tensor(out=ang[:], in0=kq[:], scalar=-CW1, in1=ang[:], op0=ALU.mult,
                                               op1=ALU.add), reads=[rpb], writes=[rpb])
    kb.dve.op(lambda e: e.scalar_tensor_tensor(out=ang[:], in0=kq[:], scalar=-CW2, in1=ang[:], op0=ALU.mult,
                                               op1=ALU.add), reads=[rpb], writes=[rpb])
    kb.dve.op(lambda e: e.tensor_scalar(out=kq[:], in0=ang[:], scalar1=float(np.pi), scalar2=-TWO_PI, op0=ALU.is_gt,
                                        op1=ALU.mult), reads=[rpb], writes=[rpb])
    kb.dve.op(lambda e: e.tensor_tensor(out=ang[:], in0=ang[:], in1=kq[:], op=ALU.add), reads=[rpb], writes=[rpb])
    kb.dve.op(lambda e: e.tensor_scalar(out=kq[:], in0=ang[:], scalar1=float(-np.pi), scalar2=TWO_PI, op0=ALU.is_lt,
                                        op1=ALU.mult), reads=[rpb], writes=[rpb])
    kb.dve.op(lambda e: e.tensor_tensor(out=ang[:], in0=ang[:], in1=kq[:], op=ALU.add), reads=[rpb], writes=[rpb])
    kb.dve.op(lambda e: e.tensor_scalar(out=ang[:], in0=ang[:], scalar1=3.14159, scalar2=-3.14159,
                                        op0=ALU.min, op1=ALU.max), reads=[rpb], writes=[rpb])
    kb.act.op(lambda e: e.activation(out=cs[:], in_=ang[:], func=AF.Sin), reads=[rpb], writes=[rpb])
    barrier(kb)
    kb.pop()

    hT = [kb.sbuf(f"hTa{i}", [128, KC, 512], BF16) for i in range(2)]
    hTb = [Buf(f"hTa{i}") for i in range(2)]
    gx = kb.sbuf("gx", [128, 6, 3 + 512], F32)
    gxb = [Buf(f"gx{c}") for c in range(6)]
    gc = kb.sbuf("gc", [128, 6, 512], F32)
    gcb = [Buf(f"gc{c}") for c in range(6)]
    sqa = [kb.sbuf(f"sqa{i}", [128, 512], F32) for i in range(2)]
    sqab = [Buf(f"sqa{i}") for i in range(2)]
    rinv = [kb.sbuf(f"rinv{i}", [128, 512], F32) for i in range(2)]
    rinvb = [Buf(f"rinv{i}") for i in range(2)]
    tq = [kb.sbuf(f"tq{i}", [128, 6, 128], F32) for i in range(2)]
    tqb = [Buf(f"tq{i}") for i in range(2)]
    tsq = kb.sbuf("tsq", [128, 6, 128], F32)
    tsqb = Buf("tsq")
    rq = [kb.sbuf(f"rq{i}", [128, 6], F32) for i in range(2)]
    rqb = [Buf(f"rq{i}") for i in range(2)]
    rt = kb.sbuf("rt", [128, 4, 6, 16], F32)
    rtb = Buf("rt")
    qko = [kb.sbuf(f"qko{i}", [128, 6, 128], BF16) for i in range(2)]
    qkob = [Buf(f"qko{i}") for i in range(2)]
    vo = [kb.sbuf(f"vo{i}", [128, 3, 128], BF16) for i in range(2)]
    vob = [Buf(f"vo{i}") for i in range(2)]
    zo = [kb.sbuf(f"zo{i}", [128, 256], F32) for i in range(2)]
    zob = [Buf(f"zo{i}") for i in range(2)]
    xnT_v = a["xnT"].rearrange("(k p) t -> p k t", p=128)
    gT_v = gT.rearrange("c p t -> p c t")
    kb.pool.op(lambda e: e.memset(gx[:, :, 0:3], 0.0), writes=gxb)
    it = 0
    for g in range(NG if 'noloop' not in DBG else 0):
        t0 = g * 512
        hs = g % 2
        h_, hb_ = hT[hs], hTb[hs]
        kb.sp.dma(h_[:], xnT_v[:, :, t0:t0 + 512], reads=[a["xnT_buf"]], writes=[hb_], slot=hb_)
        for k in range(KC):
            kb.act.op(lambda e: e.activation(out=h_[:, k, :], in_=h_[:, k, :], func=AF.Identity,
                                             scale=pv[:, 3, k:k + 1], bias=pv[:, 0, k:k + 1]),
                      reads=[hb_, pvb], writes=[hb_])
        for ch in range(6 if 'nofm' not in DBG else 0):
            bk = ch % 2
            p_ = PR.banks[bk]
            for k in range(KC):
                kb.pe.op(lambda e: e.matmul(p_[:], lhsT=wfm[:, k, ch * 128:(ch + 1) * 128], rhs=h_[:, k, :],
                                            start=(k == 0), stop=(k == KC - 1)),
                         reads=[wAb, hb_], writes=[bankb[bk]], signal=(k == KC - 1))
            if g > 0:
                kb.pool.op(lambda e: e.tensor_copy(out=gx[:, ch, 0:3], in_=gx[:, ch, 512:515]), reads=[gxb[ch]],
                           writes=[gxb[ch]])
            kb.act.op(lambda e: e.copy(out=gx[:, ch, 3:515], in_=p_[:]), reads=[bankb[bk]], writes=[gxb[ch]])
            if 'fm1' in DBG:
                continue
            for tp in range(4):
                if tp == 0:
                    kb.dve.op(lambda e: e.tensor_scalar(out=gc[:, ch, :], in0=gx[:, ch, 0:512], scalar1=gcw[:, ch, 0:1],
                                                        scalar2=None, op0=ALU.mult), reads=[gxb[ch], pvb],
                              writes=[gcb[ch]])
                else:
                    kb.dve.op(lambda e: e.scalar_tensor_tensor(out=gc[:, ch, :], in0=gx[:, ch, tp:tp + 512],
                                                               scalar=gcw[:, ch, tp:tp + 1], in1=gc[:, ch, :],
                                                               op0=ALU.mult, op1=ALU.add),
                              reads=[gxb[ch], pvb, gcb[ch]], writes=[gcb[ch]])
            kb.act.op(lambda e: e.activation(out=gc[:, ch, :], in_=gc[:, ch, :], func=AF.Silu), reads=[gcb[ch]],
                      writes=[gcb[ch]])
            if 'fm2' in DBG:
                continue
            if ch < 4:
                s_ = ch % 2
                kb.act.op(lambda e: e.activation(out=sqa[s_][:], in_=gc[:, ch, :], func=AF.Square), reads=[gcb[ch]],
                          writes=[sqab[s_]])
                bk2 = 2 + (ch % 2)
                kb.pe.op(lambda e: e.matmul(PR.banks[bk2][:], lhsT=cst[:, C_ONE, :], rhs=sqa[s_][:], start=True, stop=True),
                         reads=[cstb, sqab[s_]], writes=[bankb[bk2]])
                kb.act.op(lambda e: e.activation(out=rinv[s_][:], in_=PR.banks[bk2][:], func=AF.Sqrt, bias=epsc[:]),
                          reads=[bankb[bk2], cstb], writes=[rinvb[s_]])
                kb.dve.op(lambda e: e.reciprocal(out=rinv[s_][:], in_=rinv[s_][:]), reads=[rinvb[s_]], writes=[rinvb[s_]])
                sc = float(128 ** -0.5) if ch < 2 else 1.0
                kb.dve.op(lambda e: e.scalar_tensor_tensor(out=gc[:, ch, :], in0=gc[:, ch, :], scalar=sc, in1=rinv[s_][:],
                                                           op0=ALU.mult, op1=ALU.mult),
                          reads=[gcb[ch], rinvb[s_]], writes=[gcb[ch]])
            if 'fm3' in DBG:
                continue
            kb.pool.dma(gT_v[:, ch, t0:t0 + 512], gc[:, ch, :], reads=[gcb[ch]], writes=[gTb], slot=gcb[ch])
        for j in range(4 if 'notok' not in DBG else 0):
            s_ = it % 2
            it += 1
            tile = g * 4 + j
            r0 = t0 + j * 128
            widths = [(0, 512, 4), (512, 512, 5), (1024, NTOKC - 1024, 6)]
            for (c0, wd, bk) in widths:
                for k in range(KC):
                    kb.pe.op(lambda e: e.matmul(PR.banks[bk][:, 0:wd], lhsT=h_[:, k, j * 128:(j + 1) * 128],
                                                rhs=wtok[:, k, c0:c0 + wd], start=(k == 0), stop=(k == KC - 1)),
                             reads=[wAb, hb_], writes=[bankb[bk]], signal=(k == KC - 1))
            if 'tok0' in DBG:
                continue
            if 'skip0' not in DBG:
                kb.act.op(lambda e: e.copy(out=tq[s_][:, 0:4, :], in_=PR.banks[4][:].rearrange('p (a d) -> p a d', a=4)), writes=[tqb[s_], bankb[4]])
            if 'skip1' not in DBG:
                kb.act.op(lambda e: e.copy(out=tq[s_][:, 4:6, :], in_=PR.banks[5][:, 0:256].rearrange('p (a d) -> p a d', a=2)), writes=[tqb[s_], bankb[5]])
            if 'skip2' not in DBG:
                kb.dve.op(lambda e: e.tensor_copy(out=vo[s_][:, 0:2, :], in_=PR.banks[5][:, 256:512].rearrange('p (a d) -> p a d', a=2)), writes=[vob[s_], bankb[5]])
            if 'skip3' not in DBG:
                kb.dve.op(lambda e: e.tensor_copy(out=vo[s_][:, 2, :], in_=PR.banks[6][:, 0:128]), writes=[vob[s_], bankb[6]])
            if 'nodma' not in DBG:
                kb.pool.dma(v_tok[r0:r0 + 128, :, :], vo[s_][:], reads=[vob[s_]], writes=[vtb], slot=vob[s_])
            if 'skip4' not in DBG:
                kb.act.op(lambda e: e.activation(out=zo[s_][:], in_=PR.banks[6][:, 128:384], func=AF.Silu), writes=[zob[s_], bankb[6]])
            if 'nodma' not in DBG:
                kb.pool.dma(zs_tok[r0:r0 + 128, :], zo[s_][:], reads=[zob[s_]], writes=[zsb], slot=zob[s_])
            if 'skip5' not in DBG:
                kb.act.op(lambda e: e.copy(out=GB[:, tile, :], in_=PR.banks[6][:, 384:392]), writes=[GBb, bankb[6]])
            if 'tok1' in DBG:
                continue
            kb.dve.op(lambda e: e.tensor_tensor(out=tsq[:], in0=tq[s_][:], in1=tq[s_][:], op=ALU.mult), reads=[tqb[s_]],
                      writes=[tsqb])
            kb.dve.op(lambda e: e.tensor_reduce(out=rq[s_][:], in_=tsq[:], axis=AX.X, op=ALU.add), reads=[tsqb],
                      writes=[rqb[s_]])
            kb.act.op(lambda e: e.activation(out=rq[s_][:], in_=rq[s_][:], func=AF.Sqrt, scale=1.0 / 128, bias=epsc[:]),
                      reads=[rqb[s_], cstb], writes=[rqb[s_]])
            kb.dve.op(lambda e: e.reciprocal(out=rq[s_][:], in_=rq[s_][:]), reads=[rqb[s_]], writes=[rqb[s_]])
            for sl in range(6):
                kb.dve.op(lambda e: e.scalar_tensor_tensor(out=tq[s_][:, sl, :], in0=tq[s_][:, sl, :],
                                                           scalar=rq[s_][:, sl:sl + 1], in1=gqk[:, sl, :],
                                                           op0=ALU.mult, op1=ALU.mult),
                          reads=[tqb[s_], rqb[s_], pvb], writes=[tqb[s_]])
            if 'tok2' in DBG:
                continue
            for sl in range(6):
                x1 = tq[s_][:, sl, 0:16]
                x2 = tq[s_][:, sl, 16:32]
                cos_ = cs[:, 1, tile, :]
                sin_ = cs[:, 0, tile, :]
                kb.pool.op(lambda e: e.tensor_tensor(out=rt[:, 0, sl, :], in0=x1, in1=cos_, op=ALU.mult),
                           reads=[tqb[s_], rpb], writes=[rtb])
                kb.pool.op(lambda e: e.tensor_tensor(out=rt[:, 1, sl, :], in0=x2, in1=sin_, op=ALU.mult),
                           reads=[tqb[s_], rpb], writes=[rtb])
                kb.pool.op(lambda e: e.tensor_tensor(out=rt[:, 2, sl, :], in0=x2, in1=cos_, op=ALU.mult),
                           reads=[tqb[s_], rpb], writes=[rtb])
                kb.pool.op(lambda e: e.tensor_tensor(out=rt[:, 3, sl, :], in0=x1, in1=sin_, op=ALU.mult),
                           reads=[tqb[s_], rpb], writes=[rtb])
            kb.dve.op(lambda e: e.tensor_tensor(out=tq[s_][:, :, 0:16], in0=rt[:, 0], in1=rt[:, 1], op=ALU.subtract),
                      reads=[rtb, tqb[s_]], writes=[tqb[s_]])
            kb.dve.op(lambda e: e.tensor_tensor(out=tq[s_][:, :, 16:32], in0=rt[:, 2], in1=rt[:, 3], op=ALU.add),
                      reads=[rtb, tqb[s_]], writes=[tqb[s_]])
            kb.act.op(lambda e: e.copy(out=qko[s_][:], in_=tq[s_][:]), reads=[tqb[s_]], writes=[qkob[s_]])
            if 'nodma' not in DBG:
                kb.pool.dma(qk_tok[r0:r0 + 128, :, :], qko[s_][:], reads=[qkob[s_]], writes=[qkb], slot=qkob[s_])
    if 'nogb' in DBG:
        barrier(kb)
        kb.pop()
        return
    ab = kb.sbuf("ab", [128, 2, 2], F32)
    abb = Buf("ab")
    kb.sp.dma(ab[:, 0, :], a["a_log"].partition_broadcast(128), writes=[abb], slot=abb)
    kb.sp.dma(ab[:, 1, :], a["dt_bias"].partition_broadcast(128), writes=[abb], slot=abb)
    kb.act.op(lambda e: e.activation(out=ab[:, 0, :], in_=ab[:, 0, :], func=AF.Exp), reads=[abb], writes=[abb])
    spt = kb.sbuf("spt", [128, 3, NT, 2], F32)
    sptb = Buf("spt")
    kb.act.op(lambda e: e.activation(out=GB[:, :, 0:2], in_=GB[:, :, 0:2], func=AF.Sigmoid), reads=[GBb], writes=[GBb])
    for h in range(2):
        kb.dve.op(lambda e: e.tensor_scalar(out=spt[:, 0, :, h], in0=GB[:, :, 2 + h], scalar1=ab[:, 1, h:h + 1],
                                            scalar2=None, op0=ALU.add), reads=[GBb, abb], writes=[sptb])
    kb.act.op(lambda e: e.activation(out=spt[:, 1], in_=spt[:, 0], func=AF.Abs), reads=[sptb], writes=[sptb])
    kb.act.op(lambda e: e.activation(out=spt[:, 1], in_=spt[:, 1], func=AF.Exp, scale=-1.0), reads=[sptb], writes=[sptb])
    kb.act.op(lambda e: e.activation(out=spt[:, 1], in_=spt[:, 1], func=AF.Ln, bias=onec[:]), reads=[sptb, cstb],
              writes=[sptb])
    kb.dve.op(lambda e: e.tensor_scalar(out=spt[:, 0], in0=spt[:, 0], scalar1=0.0, scalar2=None, op0=ALU.max),
              reads=[sptb], writes=[sptb])
    kb.dve.op(lambda e: e.tensor_tensor(out=spt[:, 0], in0=spt[:, 0], in1=spt[:, 1], op=ALU.add), reads=[sptb],
              writes=[sptb])
    for h in range(2):
        kb.dve.op(lambda e: e.tensor_scalar(out=GB[:, :, 2 + h], in0=spt[:, 0, :, h], scalar1=ab[:, 0, h:h + 1],
                                            scalar2=-1.0, op0=ALU.mult, op1=ALU.mult), reads=[sptb, abb, GBb],
                  writes=[GBb])
    barrier(kb)
    kb.pop()
    if "B" not in phases:
        return

    kb.push()
    SC = float(128 ** -0.5)
    qt_ = kb.sbuf("qtok", [128, 16, 128], BF16)
    kt_ = kb.sbuf("ktok", [128, 16, 128], BF16)
    qtb, ktb = Buf("qtok"), Buf("ktok")
    QT = kb.sbuf("QT", [128, 16, 128], BF16)
    QTb = Buf("QT")
    KT = [[kb.sbuf(f"KT{g}_{i}", [128, 16, 128], BF16) for i in range(2)] for g in range(3)]
    KTb = [[Buf(f"KT{g}_{i}") for i in range(2)] for g in range(3)]
    VV = [[kb.sbuf(f"VV{g}_{i}", [128, 16, 128], BF16) for i in range(2)] for g in range(3)]
    VVb = [[Buf(f"VV{g}_{i}") for i in range(2)] for g in range(3)]
    ND = kb.sbuf("ND", [128, 2, 2048], F32)
    NDb = Buf("ND")
    ex = [kb.sbuf(f"ex{i}", [128, 256], BF16) for i in range(2)]
    exb = [Buf(f"ex{i}") for i in range(2)]
    pT = [kb.sbuf(f"pT{i}", [128, 256], BF16) for i in range(2)]
    pTb = [Buf(f"pT{i}") for i in range(2)]
    oT = kb.sbuf("oT", [128, 2048], BF16)
    oTb = Buf("oT")
    ptr = [PR.banks[0][:].bitcast(BF16), PR.banks[1][:].bitcast(BF16)]
    blk = 0

    def load_blocks(dst, dstb, src3, n, g, slot):
        u0 = n * 2048
        if g == 0:
            kb.sp.dma(dst[:], src3[u0:u0 + 2048, slot, :].rearrange("(b i) d -> i b d", i=128), reads=[qkb, vtb],
                      writes=[dstb], slot=dstb)
        elif g == 1:
            for m in range(4):
                kb.sp.dma(dst[:, m * 4:(m + 1) * 4, :],
                          src3[u0 + m * 512:u0 + (m + 1) * 512, slot, :].rearrange("(i r) d -> i r d", r=4),
                          reads=[qkb, vtb], writes=[dstb], slot=dstb)
        else:
            kb.sp.dma(dst[:], src3[u0:u0 + 2048, slot, :].rearrange("(i r) d -> i r d", r=16), reads=[qkb, vtb],
                      writes=[dstb], slot=dstb)

    def nd_view(g, bi):
        if g == 0:
            return ND[:, :, bi * 128:(bi + 1) * 128]
        if g == 1:
            m, r = bi // 4, bi % 4
            return ND[:, :, m * 512:(m + 1) * 512].rearrange("p a (i r) -> p a i r", r=4)[:, :, :, r]
        return ND[:, :, :].rearrange("p a (i r) -> p a i r", r=16)[:, :, :, bi]

    for n in range(NU):
        cur = n % 2
        for g in range(3):
            load_blocks(qt_, qtb, qk_tok, n, g, g)
            load_blocks(kt_, ktb, qk_tok, n, g, 3 + g)
            load_blocks(VV[g][cur], VVb[g][cur], v_tok, n, g, g)
            for (src, srcb, dst, dstb) in ((qt_, qtb, QT, QTb), (kt_, ktb, KT[g][cur], KTb[g][cur])):
                for half in range(2):
                    for kk in range(8):
                        kb.pe.op(lambda e: e.transpose(ptr[half][:, kk * 128:(kk + 1) * 128], src[:, half * 8 + kk, :],
                                                       identbf[:]), reads=[srcb, identb32], writes=[bankb[half]],
                                 signal=(kk == 7))
                    if half == 0:
                        kb.dve.op(lambda e: e.tensor_copy(out=dst[:, 0:8, :], in_=ptr[half][:]), reads=[bankb[half]],
                                  writes=[dstb])
                    else:
                        kb.act.op(lambda e: e.copy(out=dst[:, 8:16, :], in_=ptr[half][:]), reads=[bankb[half]],
                                  writes=[dstb])
            for bi in range(16):
                if g == 0:
                    pb_, pu_ = (bi - 1, cur) if bi > 0 else (15, 1 - cur)
                    has_prev = not (n == 0 and bi == 0)
                elif g == 1:
                    pb_, pu_ = (bi - 4, cur) if bi >= 4 else (12 + bi, 1 - cur)
                    has_prev = not (n == 0 and bi < 4)
                else:
                    pb_, pu_ = bi, 1 - cur
                    has_prev = n > 0
                s_ = blk % 2
                blk += 1
                bs = 2 + s_
                bo = 4 + s_
                c0 = 0 if has_prev else 128
                if has_prev:
                    kb.pe.op(lambda e: e.matmul(PR.banks[bs][:, 0:128], lhsT=KT[g][pu_][:, pb_, :], rhs=QT[:, bi, :],
                                                start=True, stop=True), reads=[KTb[g][pu_], QTb], writes=[bankb[bs]],
                             signal=False)
                kb.pe.op(lambda e: e.matmul(PR.banks[bs][:, 128:256], lhsT=KT[g][cur][:, bi, :], rhs=QT[:, bi, :],
                                            start=True, stop=True), reads=[KTb[g][cur], QTb], writes=[bankb[bs]])
                kb.act.op(lambda e: e.activation(out=ex[s_][:, c0:256], in_=PR.banks[bs][:, c0:256], func=AF.Exp, scale=SC),
                          reads=[bankb[bs]], writes=[exb[s_]])
                kb.dve.op(lambda e: e.tensor_tensor(out=pT[s_][:, c0:256], in0=ex[s_][:, c0:256], in1=maskbf[:, c0:256],
                                                    op=ALU.mult), reads=[exb[s_], cstb], writes=[pTb[s_]])
                if has_prev:
                    kb.pe.op(lambda e: e.matmul(PR.banks[bo][:, 0:128], lhsT=VV[g][pu_][:, pb_, :], rhs=pT[s_][:, 0:128],
                                                start=True, stop=False), reads=[VVb[g][pu_], pTb[s_]],
                             writes=[bankb[bo]], signal=False)
                kb.pe.op(lambda e: e.matmul(PR.banks[bo][:, 0:128], lhsT=VV[g][cur][:, bi, :], rhs=pT[s_][:, 128:256],
                                            start=(not has_prev), stop=True), reads=[VVb[g][cur], pTb[s_]],
                         writes=[bankb[bo]], signal=False)
                if has_prev:
                    kb.pe.op(lambda e: e.matmul(PR.banks[bo][:, 128:256], lhsT=onesbf[:], rhs=pT[s_][:, 0:128],
                                                start=True, stop=False), reads=[cstb, pTb[s_]], writes=[bankb[bo]],
                             signal=False)
                kb.pe.op(lambda e: e.matmul(PR.banks[bo][:, 128:256], lhsT=onesbf[:], rhs=pT[s_][:, 128:256],
                                            start=(not has_prev), stop=True), reads=[cstb, pTb[s_]], writes=[bankb[bo]])
                src = PR.banks[bo][:, 0:256].rearrange("p (a q) -> p a q", a=2)
                if g == 0:
                    kb.act.op(lambda e: e.copy(out=nd_view(g, bi), in_=src), reads=[bankb[bo]], writes=[NDb])
                else:
                    kb.dve.op(lambda e: e.tensor_tensor(out=nd_view(g, bi), in0=nd_view(g, bi), in1=src, op=ALU.add),
                              reads=[bankb[bo], NDb], writes=[NDb])
        kb.dve.op(lambda e: e.reciprocal(out=ND[:, 1, :], in_=ND[:, 1, :]), reads=[NDb], writes=[NDb])
        kb.dve.op(lambda e: e.tensor_tensor(out=oT[:], in0=ND[:, 0, :], in1=ND[:, 1, :], op=ALU.mult), reads=[NDb],
                  writes=[oTb])
        kb.sp.dma(a["o_aT"][:, n * 2048:(n + 1) * 2048], oT[:], reads=[oTb], writes=[a["o_aT_buf"]], slot=oTb)
    barrier(kb)
    kb.pop()
    if "C" not in phases:
        return

    kb.push()
    gnb = kb.sbuf("gnb", [128, 128], F32)
    gnbb = Buf("gnb")
    kb.sp.dma(gnb[:], a["gdn_norm_g"].partition_broadcast(128), writes=[gnbb], slot=gnbb)
    St = [kb.sbuf(f"St{h}", [128, 128], F32) for h in range(2)]
    Stb = [Buf(f"St{h}") for h in range(2)]
    for h in range(2):
        kb.pool.op(lambda e: e.memset(St[h][:], 0.0), writes=[Stb[h]])

    class HS:
        pass

    def cmat(i):
        return cst[:, i, :]

    PSET = {}
    for h in range(2):
        for par in range(2):
            o = HS()
            n = f"{h}{par}"
            o.qkv = kb.sbuf(f"qkv{n}", [128, 3, 128], F32); o.qkvb = Buf(f"qkv{n}")
            o.gu = kb.sbuf(f"gu{n}", [128, 2, 128], F32); o.gub = Buf(f"gu{n}")
            o.ex = kb.sbuf(f"exc{n}", [128, 512], F32); o.exb = Buf(f"exc{n}")
            o.t1 = kb.sbuf(f"t1{n}", [128, 128], F32); o.t1b = Buf(f"t1{n}")
            o.dT = kb.sbuf(f"dT{n}", [128, 128], F32); o.dTb = Buf(f"dT{n}")
            o.qkm = kb.sbuf(f"qkm{n}", [128, 128], F32); o.qkmb = Buf(f"qkm{n}")
            o.bege = kb.sbuf(f"bege{n}", [128, 8], F32); o.begeb = Buf(f"bege{n}")
            o.y = kb.sbuf(f"y{n}", [128, 256], F32); o.yb = Buf(f"y{n}")
            o.kdec = kb.sbuf(f"kdec{n}", [128, 128], F32); o.kdecb = Buf(f"kdec{n}")
            o.qd = kb.sbuf(f"qd{n}", [128, 128], F32); o.qdb = Buf(f"qd{n}")
            o.pw = kb.sbuf(f"pw{n}", [128, 12, 128], F32); o.pwb = [Buf(f"pw{n}_{i}") for i in range(12)]
            o.wT = kb.sbuf(f"wT{n}", [128, 128], F32); o.wTb = Buf(f"wT{n}")
            o.gcol = kb.sbuf(f"gcol{n}", [128, 8], F32); o.gcolb = Buf(f"gcol{n}")
            o.zt = kb.sbuf(f"zt{n}", [128, 128], F32); o.ztb = Buf(f"zt{n}")
            bx, by = 2 * h, 2 * h + 1
            o.ra, o.rG = PR.reg(bx, 0, 2), PR.reg(bx, 2, 2)
            o.rTa, o.rTw = PR.reg(bx, 0, 1), PR.reg(bx, 1, 1)
            o.bxb = bankb[bx]
            o.rPD = PR.reg(by, 0, 4)
            o.rA, o.rB, o.rY = PR.reg(by, 0, 1), PR.reg(by, 1, 1), PR.reg(by, 2, 2)
            o.byb = bankb[by]
            PSET[(h, par)] = o
    SSET = []
    for h in range(2):
        o = HS()
        o.vn = kb.sbuf(f"vn{h}", [128, 128], F32); o.vnb = Buf(f"vn{h}")
        o.O = kb.sbuf(f"O{h}", [128, 128], F32); o.Ob = Buf(f"O{h}")
        o.junk = kb.sbuf(f"junk{h}", [128, 128], F32)
        o.ss = kb.sbuf(f"ssn{h}", [128, 8], F32); o.ssb = Buf(f"ssn{h}")
        o.on = kb.sbuf(f"on{h}", [128, 128], BF16); o.onb = Buf(f"on{h}")
        o.od = [kb.sbuf(f"od{h}_{i}", [128, 512], BF16) for i in range(2)]
        o.odb = [Buf(f"od{h}_{i}") for i in range(2)]
        bk = 4 + h
        o.rP1, o.rPO, o.rPS = PR.reg(bk, 0, 1), PR.reg(bk, 1, 1), PR.reg(bk, 2, 1)
        o.trb = PR.banks[bk][:].bitcast(BF16)[:, 768:1024]
        o.bb = bankb[bk]
        SSET.append(o)

    def prep(t, h):
        o = PSET[(h, t % 2)]
        c0 = t * 128
        bcol = GB[:, t, h:h + 1]
        for i, ch in enumerate((h, 2 + h, 4 + h)):
            kb.sp.dma(o.qkv[:, i, :], gT[ch, :, c0:c0 + 128], reads=[gTb], writes=[o.qkvb], slot=o.qkvb)
        kb.sp.dma(o.zt[:], zs_tok[c0:c0 + 128, h * 128:(h + 1) * 128], reads=[zsb], writes=[o.ztb], slot=o.ztb)
        QTt, KTt, VTt = o.qkv[:, 0, :], o.qkv[:, 1, :], o.qkv[:, 2, :]
        kb.pool.op(lambda e: e.tensor_copy(out=o.gcol[:, 0:1], in_=GB[:, t, 2 + h:3 + h]), reads=[GBb], writes=[o.gcolb])
        gcol = o.gcol[:, 0:1]
        kb.pe.op(lambda e: e.transpose(o.ra[:, 0:128], KTt, ident32[:]), reads=[o.qkvb, identb32], writes=[o.bxb], signal=False)
        kb.pe.op(lambda e: e.transpose(o.ra[:, 128:256], VTt, ident32[:]), reads=[o.qkvb, identb32], writes=[o.bxb])
        kb.dve.op(lambda e: e.tensor_scalar(out=o.gu[:, 0, :], in0=cmat(C_U2), scalar1=gcol, scalar2=None, op0=ALU.mult),
                  reads=[cstb, o.gcolb], writes=[o.gub])
        kb.dve.op(lambda e: e.tensor_scalar(out=o.gu[:, 1, :], in0=cmat(C_SL2), scalar1=gcol, scalar2=None, op0=ALU.mult),
                  reads=[cstb, o.gcolb], writes=[o.gub])
        yield
        rPD = o.rPD
        kb.pe.op(lambda e: e.matmul(rPD[:, 0:128], lhsT=o.gu[:, 0, :], rhs=cmat(C_SL2), start=True, stop=True),
                 reads=[o.gub, cstb], writes=[o.byb], signal=False)
        kb.pe.op(lambda e: e.matmul(rPD[:, 128:160], lhsT=o.gu[:, 0, :], rhs=cst[:, C_ONE, 0:32], start=True, stop=True),
                 reads=[o.gub, cstb], writes=[o.byb], signal=False)
        kb.pe.op(lambda e: e.matmul(rPD[:, 160:192], lhsT=o.gu[:, 1, :], rhs=cst[:, C_ONE, 0:32], start=True, stop=True),
                 reads=[o.gub, cstb], writes=[o.byb], signal=False)
        kb.pe.op(lambda e: e.matmul(rPD[:, 256:384], lhsT=cmat(C_SL2), rhs=o.gu[:, 0, :], start=True, stop=True),
                 reads=[o.gub, cstb], writes=[o.byb], signal=False)
        kb.pe.op(lambda e: e.matmul(rPD[:, 384:512], lhsT=cmat(C_ONE), rhs=o.gu[:, 0, :], start=True, stop=True),
                 reads=[o.gub, cstb], writes=[o.byb])
        kb.pe.op(lambda e: e.matmul(o.rG[:, 0:128], lhsT=KTt, rhs=KTt, start=True, stop=True), reads=[o.qkvb],
                 writes=[o.bxb], signal=False)
        kb.pe.op(lambda e: e.matmul(o.rG[:, 128:256], lhsT=KTt, rhs=QTt, start=True, stop=True), reads=[o.qkvb],
                 writes=[o.bxb])
        yield
        kb.act.op(lambda e: e.activation(out=o.ex[:, 0:192], in_=rPD[:, 0:192], func=AF.Exp), reads=[o.byb], writes=[o.exb])
        kb.act.op(lambda e: e.activation(out=o.ex[:, 256:512], in_=rPD[:, 256:512], func=AF.Exp), reads=[o.byb], writes=[o.exb])
        yield
        A0, B0 = o.pw[:, 0, :], o.pw[:, 1, :]
        kb.dve.op(lambda e: e.tensor_tensor(out=o.t1[:], in0=o.ex[:, 0:128], in1=cmat(C_SL2), op=ALU.mult),
                  reads=[o.exb, cstb], writes=[o.t1b])
        kb.dve.op(lambda e: e.scalar_tensor_tensor(out=A0, in0=o.rG[:, 0:128], scalar=bcol, in1=o.t1[:], op0=ALU.mult,
                                                   op1=ALU.mult), reads=[o.bxb, GBb, o.t1b], writes=[o.pwb[0]])
        kb.pool.op(lambda e: e.tensor_tensor(out=o.dT[:], in0=o.ex[:, 256:384], in1=cmat(C_U2), op=ALU.mult),
                   reads=[o.exb, cstb], writes=[o.dTb])
        kb.pool.op(lambda e: e.tensor_tensor(out=o.bege[:, 0:1], in0=bcol, in1=o.ex[:, 128:129], op=ALU.mult),
                   reads=[GBb, o.exb], writes=[o.begeb])
        kb.pool.op(lambda e: e.tensor_tensor(out=o.qd[:], in0=QTt, in1=o.ex[:, 384:512], op=ALU.mult),
                   reads=[o.qkvb, o.exb], writes=[o.qdb])
        yield
        kb.dve.op(lambda e: e.tensor_tensor(out=o.qkm[:], in0=o.rG[:, 128:256], in1=o.dT[:], op=ALU.mult),
                  reads=[o.bxb, o.dTb], writes=[o.qkmb])
        kb.dve.op(lambda e: e.tensor_scalar(out=o.y[:, 0:128], in0=o.ra[:, 128:256], scalar1=bcol, scalar2=None,
                                            op0=ALU.mult), reads=[o.bxb, GBb], writes=[o.yb])
        kb.dve.op(lambda e: e.tensor_scalar(out=o.y[:, 128:256], in0=o.ra[:, 0:128], scalar1=o.bege[:, 0:1], scalar2=None,
                                            op0=ALU.mult), reads=[o.bxb, o.begeb], writes=[o.yb])
        kb.dve.op(lambda e: e.tensor_scalar(out=o.kdec[:], in0=o.ra[:, 0:128], scalar1=o.ex[:, 160:161], scalar2=None,
                                            op0=ALU.mult), reads=[o.bxb, o.exb], writes=[o.kdecb])
        yield
        kb.pe.op(lambda e: e.transpose(o.rTa[:], A0, ident32[:]), reads=[o.pwb[0], identb32], writes=[o.bxb])
        yield
        kb.act.op(lambda e: e.copy(out=B0, in_=o.rTa[:]), reads=[o.bxb], writes=[o.pwb[1]])
        yield
        for l in range(1, 6):
            Ap, Bp = o.pw[:, 2 * (l - 1), :], o.pw[:, 2 * (l - 1) + 1, :]
            Apb, Bpb = o.pwb[2 * (l - 1)], o.pwb[2 * (l - 1) + 1]
            if l < 5:
                kb.pe.op(lambda e: e.matmul(o.rA[:], lhsT=Bp, rhs=Ap, start=True, stop=True), reads=[Apb, Bpb],
                         writes=[o.byb], signal=False)
            kb.pe.op(lambda e: e.matmul(o.rB[:], lhsT=Ap, rhs=Bp, start=True, stop=True), reads=[Apb, Bpb],
                     writes=[o.byb])
            yield
            if l < 5:
                kb.act.op(lambda e: e.copy(out=o.pw[:, 2 * l, :], in_=o.rA[:]), reads=[o.byb], writes=[o.pwb[2 * l]])
            kb.act.op(lambda e: e.copy(out=o.pw[:, 2 * l + 1, :], in_=o.rB[:]), reads=[o.byb], writes=[o.pwb[2 * l + 1]])
            yield
        for l in (5, 4, 3, 2, 1, 0):
            kb.pe.op(lambda e: e.matmul(o.rY[:], lhsT=o.pw[:, 2 * l + 1, :], rhs=o.y[:], start=True, stop=True),
                     reads=[o.pwb[2 * l + 1], o.yb], writes=[o.byb])
            yield
            kb.dve.op(lambda e: e.tensor_tensor(out=o.y[:], in0=o.y[:], in1=o.rY[:],
                                                op=(ALU.add if l > 0 else ALU.subtract)), reads=[o.byb, o.yb],
                      writes=[o.yb])
            yield
        kb.pe.op(lambda e: e.transpose(o.rTw[:], o.y[:, 128:256], ident32[:]), reads=[o.yb, identb32], writes=[o.bxb])
        yield
        kb.act.op(lambda e: e.copy(out=o.wT[:], in_=o.rTw[:]), reads=[o.bxb], writes=[o.wTb])
        kb.pool.op(lambda e: e.tensor_tensor(out=o.zt[:], in0=o.zt[:], in1=gnb[:], op=ALU.mult), reads=[o.ztb, gnbb],
                   writes=[o.ztb])
        yield

    def scan(t, h):
        o = PSET[(h, t % 2)]
        sc = SSET[h]
        for X, (lo, hi) in enumerate(((0, 64), (64, 128))):
            kb.pe.op(lambda e: e.matmul(sc.rP1[:], lhsT=o.wT[:], rhs=St[h][:], start=True, stop=True),
                     reads=[o.wTb, Stb[h]], writes=[sc.bb])
            yield
            kb.dve.op(lambda e: e.tensor_tensor(out=sc.vn[lo:hi, :], in0=o.y[lo:hi, 0:128], in1=sc.rP1[lo:hi, :],
                                                op=ALU.subtract), reads=[sc.bb, o.yb], writes=[sc.vnb])
            yield
            kb.pe.op(lambda e: e.matmul(sc.rPS[:], lhsT=o.kdec[lo:hi, :], rhs=sc.vn[lo:hi, :], start=True, stop=True),
                     reads=[o.kdecb, sc.vnb], writes=[sc.bb], signal=False)
            kb.pe.op(lambda e: e.matmul(sc.rPO[:], lhsT=o.qd[:], rhs=St[h][:], start=True, stop=False),
                     reads=[o.qdb, Stb[h]], writes=[sc.bb], signal=False)
            kb.pe.op(lambda e: e.matmul(sc.rPO[:], lhsT=o.qkm[lo:hi, :], rhs=sc.vn[lo:hi, :], start=False, stop=True),
                     reads=[o.qkmb, sc.vnb], writes=[sc.bb])
            yield
            kb.dve.op(lambda e: e.scalar_tensor_tensor(out=St[h][:], in0=St[h][:], scalar=o.ex[:, 447 + 64 * X:448 + 64 * X],
                                                       in1=sc.rPS[:], op0=ALU.mult, op1=ALU.add),
                      reads=[sc.bb, o.exb, Stb[h]], writes=[Stb[h]])
            kb.act.op(lambda e: e.copy(out=sc.O[lo:hi, :], in_=sc.rPO[lo:hi, :]), reads=[sc.bb], writes=[sc.Ob])
            yield
        kb.act.op(lambda e: e.activation(out=sc.junk[:], in_=sc.O[:], func=AF.Square, accum_out=sc.ss[:, 0:1]), reads=[sc.Ob],
                  writes=[sc.ssb])
        kb.act.op(lambda e: e.activation(out=sc.ss[:, 0:1], in_=sc.ss[:, 0:1], func=AF.Sqrt, scale=1.0 / 128, bias=epsc[:]),
                  reads=[sc.ssb, cstb], writes=[sc.ssb])
        yield
        kb.dve.op(lambda e: e.reciprocal(out=sc.ss[:, 0:1], in_=sc.ss[:, 0:1]), reads=[sc.ssb], writes=[sc.ssb])
        kb.dve.op(lambda e: e.scalar_tensor_tensor(out=sc.on[:], in0=sc.O[:], scalar=sc.ss[:, 0:1], in1=o.zt[:],
                                                   op0=ALU.mult, op1=ALU.mult), reads=[sc.Ob, sc.ssb, o.ztb],
                  writes=[sc.onb])
        yield
        kb.pe.op(lambda e: e.transpose(sc.trb[:, 0:128], sc.on[:], identbf[:]), reads=[sc.onb, identb32], writes=[sc.bb])
        yield
        jj = t % 4
        q4 = (t // 4) % 2
        kb.act.op(lambda e: e.copy(out=sc.od[q4][:, jj * 128:(jj + 1) * 128], in_=sc.trb[:, 0:128]), reads=[sc.bb],
                  writes=[sc.odb[q4]])
        if jj == 3:
            kb.sp.dma(a["o_dT"][h * 128:(h + 1) * 128, (t - 3) * 128:(t + 1) * 128], sc.od[q4][:], reads=[sc.odb[q4]],
                      writes=[a["o_dT_buf"]], slot=sc.odb[q4])
        yield

    for step in range(NT + 1):
        gens = []
        if step >= 1:
            gens += [scan(step - 1, 0), scan(step - 1, 1)]
        if step < NT:
            gens += [prep(step, 0), prep(step, 1)]
        while gens:
            nxt = []
            for gnr in gens:
                try:
                    next(gnr)
                    nxt.append(gnr)
                except StopIteration:
                    pass
            gens = nxt
    kb.pop()


def build_p1(S, phases="ABC"):
    kb = KB()
    a = {}
    a["xnT"] = kb.din("xnT", [D, S], BF16)
    a["mod"] = kb.din("mod", [2, 3 * D], F32)
    a["g_mix"] = kb.din("g_mix", [D], F32)
    a["pos"] = kb.din("pos", [S], I32)
    a["w_tok"] = kb.din("w_tok", [D, NTOKC], F32)
    a["w_fm"] = kb.din("w_fm", [D, 768], F32)
    a["q_norm_g"] = kb.din("q_norm_g", [128], F32)
    a["k_norm_g"] = kb.din("k_norm_g", [128], F32)
    a["gconv_w"] = kb.din("gconv_w", [4, 768], F32)
    a["a_log"] = kb.din("a_log", [2], F32)
    a["dt_bias"] = kb.din("dt_bias", [2], F32)
    a["gdn_norm_g"] = kb.din("gdn_norm_g", [128], F32)
    a["consts"] = kb.din("consts", [128, 9, 128], F32)
    a["o_aT"] = kb.dout("o_aT", [128, S], BF16)
    a["o_dT"] = kb.dout("o_dT", [256, S], BF16)
    for n in ["xnT_buf", "o_aT_buf", "o_dT_buf", "dbg_buf"]:
        a[n] = Buf(n)
    if 'dump' in DBG:
        a["dbg"] = kb.dout("dbg", [2, 4, 128, 128], F32)
    emit_p1(kb, S, a, phases)
    return kb.finish([a["o_aT_buf"], a["o_dT_buf"], a["dbg_buf"]])


SEQ = 16384
NB = 2
NCORE = 8
TPC = SEQ * NB // NCORE
O_Q, O_K, O_V, O_GQ, O_GK, O_GV, O_BETA, O_ALPHA, O_Z, O_UC, O_GL = (
    0, 1536, 3072, 4608, 5632, 6656, 7680, 7688, 7696, 8720, 10768)
_PROGS = {}


def _prog(name, builder, *args):
    key = (name,) + args
    if key not in _PROGS:
        _PROGS[key] = builder(*args)
    return _PROGS[key]


def _c(a):
    return np.ascontiguousarray(a)


def _run(nc, in_maps):
    res = run_bass_kernel_spmd(nc, in_maps, core_ids=list(range(NCORE)))
    return res.results


def kernel(x, c, positions, mix_mod_w, mix_mod_b, mix_norm_g, w_in, q_norm_g, k_norm_g, w_attn_o, gdn_conv_w,
           gdn_a_log, gdn_dt_bias, gdn_norm_g, w_gdn_o, conv_dw_w, conv_dw_b, conv_ln_g, conv_ln_b, w_conv_o, w_out,
           ffn_mod_w, ffn_mod_b, ffn_norm_g, w_gate_up, w_down):
    f32 = np.float32
    x = np.asarray(x, f32)
    c = np.asarray(c, f32)
    positions = np.asarray(positions, np.int32)
    cores = [(i // 4, i % 4) for i in range(NCORE)]
    consts = host_consts()

    p0 = _prog("p0", build_p0, TPC)
    mats = [mix_mod_w[0], ffn_mod_w[0], mix_mod_w[1], ffn_mod_w[1]]
    biases = [mix_mod_b[0], ffn_mod_b[0], mix_mod_b[1], ffn_mod_b[1]]
    in_maps = []
    for i, (b, j) in enumerate(cores):
        in_maps.append({
            "x": _c(x[b, j * TPC:(j + 1) * TPC]),
            "c": _c(c),
            "wm": _c(np.stack([np.asarray(m, f32)[:, 768 * i:768 * (i + 1)] for m in mats])),
            "bm": _c(np.stack([np.asarray(v, f32)[768 * i:768 * (i + 1)] for v in biases])),
        })
    r0 = _run(p0, in_maps)
    mod_all = np.concatenate([r0[i]["modo"] for i in range(NCORE)], axis=-1)
    xnT_full = [np.concatenate([r0[b * 4 + j]["xnT"] for j in range(4)], axis=1) for b in range(NB)]
    x_cur = [x[b] for b in range(NB)]

    p1 = _prog("p1", build_p1, SEQ)
    p2 = _prog("p2", build_p2, TPC)
    for l in range(2):
        win = np.asarray(w_in[l], f32)
        gcw = np.asarray(gdn_conv_w[l], f32)
        in_maps = []
        for i, (b, j) in enumerate(cores):
            heads_a = [(g * 4 + j) * 128 for g in range(3)]
            hd = [2 * j, 2 * j + 1]
            cols = ([O_Q + o for o in heads_a] + [O_K + o for o in heads_a] + [O_V + o for o in heads_a]
                    + [O_Z + h * 128 for h in hd])
            w_tok = np.concatenate([win[:, o:o + 128] for o in cols]
                                   + [win[:, O_BETA + hd[0]:O_BETA + hd[0] + 2], win[:, O_ALPHA + hd[0]:O_ALPHA + hd[0] + 2]],
                                   axis=1)
            fcols = [O_GQ + h * 128 for h in hd] + [O_GK + h * 128 for h in hd] + [O_GV + h * 128 for h in hd]
            w_fm = np.concatenate([win[:, o:o + 128] for o in fcols], axis=1)
            gconv = np.concatenate([gcw[:, o - O_GQ:o - O_GQ + 128] for o in fcols], axis=1)
            in_maps.append({
                "xnT": _c(xnT_full[b]),
                "mod": _c(np.stack([mod_all[2 * l, b], mod_all[2 * l + 1, b]])),
                "g_mix": _c(np.asarray(mix_norm_g[l], f32)),
                "pos": _c(positions[b]),
                "w_tok": _c(w_tok), "w_fm": _c(w_fm),
                "q_norm_g": _c(np.asarray(q_norm_g[l], f32)), "k_norm_g": _c(np.asarray(k_norm_g[l], f32)),
                "gconv_w": _c(gconv),
                "a_log": _c(np.asarray(gdn_a_log[l], f32)[hd[0]:hd[0] + 2]),
                "dt_bias": _c(np.asarray(gdn_dt_bias[l], f32)[hd[0]:hd[0] + 2]),
                "gdn_norm_g": _c(np.asarray(gdn_norm_g[l], f32)),
                "consts": consts,
            })
        r1 = _run(p1, in_maps)
        oaT = [np.concatenate([r1[b * 4 + j]["o_aT"] for j in range(4)], axis=0) for b in range(NB)]
        odT = [np.concatenate([r1[b * 4 + j]["o_dT"] for j in range(4)], axis=0) for b in range(NB)]
        in_maps = []
        shared = {
            "g_mix": _c(np.asarray(mix_norm_g[l], f32)), "g_ffn": _c(np.asarray(ffn_norm_g[l], f32)),
            "w_uc": _c(win[:, O_UC:O_UC + 2048]), "w_gl": _c(win[:, O_GL:O_GL + 3 * D]),
            "w_ao": _c(np.asarray(w_attn_o[l], f32)), "w_go": _c(np.asarray(w_gdn_o[l], f32)),
            "w_co": _c(np.asarray(w_conv_o[l], f32)), "w_out": _c(np.asarray(w_out[l], f32)),
            "w_gu": _c(np.asarray(w_gate_up[l], f32)), "w_dn": _c(np.asarray(w_down[l], f32)),
            "conv_w": _c(np.asarray(conv_dw_w[l], f32)), "conv_b": _c(np.asarray(conv_dw_b[l], f32)),
            "ln_g": _c(np.asarray(conv_ln_g[l], f32)), "ln_b": _c(np.asarray(conv_ln_b[l], f32)),
        }
        for i, (b, j) in enumerate(cores):
            t0 = j * TPC
            xh = np.zeros((D, HALO + TPC), dtype=xnT_full[b].dtype)
            if j > 0:
                xh[:, :] = xnT_full[b][:, t0 - HALO:t0 + TPC]
            else:
                xh[:, HALO:] = xnT_full[b][:, 0:TPC]
            m = dict(shared)
            m.update({
                "x": _c(x_cur[b][t0:t0 + TPC]),
                "xnT_h": xh,
                "halo_flag": np.full((128, 1), 1.0 if j > 0 else 0.0, f32),
                "o_aT": _c(oaT[b][:, t0:t0 + TPC]), "o_dT": _c(odT[b][:, t0:t0 + TPC]),
                "mod": _c(np.stack([mod_all[2 * l, b], mod_all[2 * l + 1, b]])),
            })
            in_maps.append(m)
        r2 = _run(p2, in_maps)
        x_cur = [np.concatenate([r2[b * 4 + j]["xout"] for j in range(4)], axis=0) for b in range(NB)]
        xnT_full = [np.concatenate([r2[b * 4 + j]["xnT_out"] for j in range(4)], axis=1) for b in range(NB)]
    return np.stack(x_cur).astype(f32)
```

```python
import contextlib
import numpy as np
import concourse.bass as bass
import concourse.mybir as mybir
from concourse.bass_utils import run_bass_kernel_spmd

F32 = mybir.dt.float32
BF16 = mybir.dt.bfloat16
I32 = mybir.dt.int32
AF = mybir.ActivationFunctionType
ALU = mybir.AluOpType
AX = mybir.AxisListType

D = 2048
KC = D // 128
EPS = 1e-6


class Buf:
    __slots__ = ("name", "w", "r", "dsem", "dcnt", "psum")

    def __init__(self, name="", psum=False):
        self.name = name
        self.psum = psum
        self.w = None
        self.r = []
        self.dsem = None
        self.dcnt = 0


class Eng:
    def __init__(self, kb, name, h, inorder_safe=False):
        self.kb = kb
        self.name = name
        self.h = h
        self.sem = kb.new_sem("e_" + name)
        self.n = 0
        self.seen = {}
        self.inorder_safe = inorder_safe

    def _wait(self, tok):
        sem, val, eng = tok
        if self.seen.get(id(sem), 0) >= val:
            return
        self.h.wait_ge(sem, val)
        self.seen[id(sem)] = val

    def _deps(self, reads, writes):
        for b in reads:
            if b.w is not None:
                if not (b.w[2] is self and self.inorder_safe):
                    self._wait(b.w)
        for b in writes:
            if b.w is not None:
                if not (b.w[2] is self and (self.inorder_safe or b.psum)):
                    self._wait(b.w)
            for t in b.r:
                if t[2] is not self:
                    self._wait(t)

    def op(self, fn, reads=(), writes=(), signal=True):
        if any(b.psum for b in reads):
            writes = list(writes) + [b for b in reads if b.psum]
            reads = [b for b in reads if not b.psum]
        self._deps(reads, writes)
        ins = fn(self.h)
        if signal:
            self.n += 1
            ins.then_inc(self.sem, 1)
            tok = (self.sem, self.n, self)
        else:
            tok = (self.sem, self.n + 1, self)
        for b in reads:
            b.r.append(tok)
            if len(b.r) > 12:
                b.r = _prune(b.r)
        for b in writes:
            b.w = tok
            b.r = []
        return ins

    def dma(self, out, in_, reads=(), writes=(), slot=None, **kw):
        self._deps(reads, writes)
        if slot.dsem is None:
            slot.dsem = self.kb.new_sem("d_" + slot.name)
            self.kb.dma_slots.append(slot)
        ins = self.h.dma_start(out=out, in_=in_, **kw)
        slot.dcnt += 16
        ins.then_inc(slot.dsem, 16)
        tok = (slot.dsem, slot.dcnt, None)
        for b in reads:
            b.r.append(tok)
            if len(b.r) > 12:
                b.r = _prune(b.r)
        for b in writes:
            b.w = tok
            b.r = []
        return ins


def _prune(toks):
    best = {}
    for t in toks:
        k = id(t[0])
        if k not in best or best[k][1] < t[1]:
            best[k] = t
    return list(best.values())


class KB:
    def __init__(self):
        self.nc = bass.Bass("TRN2", target_bir_lowering=False)
        self.es = contextlib.ExitStack()
        self.nsem = 0
        nc = self.nc
        self.pe = Eng(self, "pe", nc.tensor, inorder_safe=True)
        self.act = Eng(self, "act", nc.scalar)
        self.dve = Eng(self, "dve", nc.vector)
        self.pool = Eng(self, "pool", nc.gpsimd)
        self.sp = Eng(self, "sp", nc.sync)
        self.out_toks = []
        self.dma_slots = []
        self.scopes = []

    def new_sem(self, name):
        self.nsem += 1
        return self.es.enter_context(self.nc.semaphore(f"{name}_{self.nsem}"))

    def push(self):
        self.scopes.append(contextlib.ExitStack())

    def pop(self):
        self.scopes.pop().close()

    def sbuf(self, name, shape, dt):
        es = self.scopes[-1] if self.scopes else self.es
        return es.enter_context(self.nc.sbuf_tensor(name, list(shape), dt))

    def psum(self, name, shape, dt):
        return self.es.enter_context(self.nc.psum_tensor(name, list(shape), dt))

    def din(self, name, shape, dt):
        return self.nc.dram_tensor(name, list(shape), dt, kind="ExternalInput").ap()

    def dout(self, name, shape, dt):
        return self.nc.dram_tensor(name, list(shape), dt, kind="ExternalOutput").ap()

    def dscratch(self, name, shape, dt):
        return self.nc.dram_tensor(name, list(shape), dt, kind="Internal").ap()

    def finish(self, bufs):
        for b in bufs:
            if b.w is not None:
                self.sp._wait(b.w)
        self.es.close()
        return self.nc


def make_ident(kb, dt, name="ident"):
    t32 = kb.sbuf(name + "32", [128, 128], F32)
    b = Buf(name)
    kb.pool.op(lambda e: e.memset(t32[:], 0.0), writes=[b])
    kb.pool.op(lambda e: e.affine_select(out=t32[:], in_=t32[:], pattern=[[-1, 128]],
                                         compare_op=ALU.not_equal, fill=1.0, base=0,
                                         channel_multiplier=1), reads=[b], writes=[b])
    if dt == F32:
        return t32, b
    t = kb.sbuf(name, [128, 128], dt)
    b2 = Buf(name + "c")
    kb.dve.op(lambda e: e.tensor_copy(out=t[:], in_=t32[:]), reads=[b], writes=[b2])
    return t, b2


def emit_norm_transpose(kb, x_rows, xnT_out, ntok, ident, identb, x_buf, xnT_buf, tag,
                        pt_banks=None):
    nc = kb.nc
    NG = ntok // 512
    xt = [kb.sbuf(f"{tag}_x{i}", [128, D], F32) for i in range(2)]
    xtb = [Buf(f"{tag}_x{i}") for i in range(2)]
    sq = kb.sbuf(f"{tag}_sq", [128, D], BF16)
    sqb = Buf(f"{tag}_sq")
    ss = [kb.sbuf(f"{tag}_ss{i}", [128, 1], F32) for i in range(2)]
    ssb = [Buf(f"{tag}_ss{i}") for i in range(2)]
    xn = [kb.sbuf(f"{tag}_xn{i}", [128, D], BF16) for i in range(2)]
    xnb = [Buf(f"{tag}_xn{i}") for i in range(2)]
    xT = [kb.sbuf(f"{tag}_xT{i}", [128, KC, 512], BF16) for i in range(2)]
    xTb = [Buf(f"{tag}_xT{i}") for i in range(2)]
    epsc = kb.sbuf(f"{tag}_eps", [128, 1], F32)
    epsb = Buf(f"{tag}_eps")
    kb.pool.op(lambda e: e.memset(epsc[:], EPS), writes=[epsb])
    if pt_banks is None:
        pt_banks = [(kb.psum(f"{tag}_pt{i}", [128, 8, 128], BF16), Buf(f"{tag}_pt{i}", psum=True)) for i in range(2)]
    xo = xnT_out.rearrange("(k p) t -> p k t", p=128)
    it = 0
    for g in range(NG):
        gs = g % 2
        for j in range(4):
            s = it % 2
            it += 1
            r0 = g * 512 + j * 128
            kb.sp.dma(xt[s][:], x_rows[r0:r0 + 128, :], reads=[x_buf], writes=[xtb[s]], slot=xtb[s])
            kb.act.op(lambda e: e.activation(out=sq[:], in_=xt[s][:], func=AF.Square, accum_out=ss[s][:]),
                      reads=[xtb[s]], writes=[sqb, ssb[s]])
            kb.act.op(lambda e: e.activation(out=ss[s][:], in_=ss[s][:], func=AF.Sqrt, scale=1.0 / D, bias=epsc[:]),
                      reads=[ssb[s], epsb], writes=[ssb[s]])
            kb.dve.op(lambda e: e.reciprocal(out=ss[s][:], in_=ss[s][:]), reads=[ssb[s]], writes=[ssb[s]])
            kb.act.op(lambda e: e.activation(out=xn[s][:], in_=xt[s][:], func=AF.Copy, scale=ss[s][:]),
                      reads=[xtb[s], ssb[s]], writes=[xnb[s]])
            for half in range(2):
                pt, ptb = pt_banks[half]
                for kk in range(8):
                    k = half * 8 + kk
                    kb.pe.op(lambda e: e.transpose(pt[:, kk, :], xn[s][:, k * 128:(k + 1) * 128], ident[:]),
                             reads=[xnb[s], identb], writes=[ptb], signal=(kk == 7))
                eng = kb.dve if half == 0 else kb.act
                if half == 0:
                    kb.dve.op(lambda e: e.tensor_copy(out=xT[gs][:, 0:8, j * 128:(j + 1) * 128], in_=pt[:]),
                              reads=[ptb], writes=[xTb[gs]])
                else:
                    kb.act.op(lambda e: e.copy(out=xT[gs][:, 8:16, j * 128:(j + 1) * 128], in_=pt[:]),
                              reads=[ptb], writes=[xTb[gs]])
        kb.sp.dma(xo[:, :, g * 512:(g + 1) * 512], xT[gs][:], reads=[xTb[gs]], writes=[xnT_buf], slot=xTb[gs])


def build_p0(ntok):
    kb = KB()
    x = kb.din("x", [ntok, D], F32)
    c = kb.din("c", [2, D], F32)
    wm = kb.din("wm", [4, D, 768], F32)
    bm = kb.din("bm", [4, 768], F32)
    xnT = kb.dout("xnT", [D, ntok], BF16)
    modo = kb.dout("modo", [4, 2, 768], F32)
    xb, xnTb, modob = Buf("x"), Buf("xnT"), Buf("modo")
    ident, identb = make_ident(kb, BF16)

    cT = kb.sbuf("cT", [128, KC, 2], F32)
    cTb = Buf("cT")
    for b in range(2):
        kb.sp.dma(cT[:, :, b], c[b].rearrange("(k p) -> p k", p=128), writes=[cTb], slot=cTb,
                  allow_slow_non_contiguous=True)
    kb.act.op(lambda e: e.activation(out=cT[:], in_=cT[:], func=AF.Silu), reads=[cTb], writes=[cTb])
    wsl = [kb.sbuf(f"wm{i}", [128, KC, 768], F32) for i in range(2)]
    wslb = [Buf(f"wm{i}") for i in range(2)]
    bsl = kb.sbuf("bsl", [2, 4, 768], F32)
    bslb = Buf("bsl")
    for b in range(2):
        kb.sp.dma(bsl[b:b + 1, :, :], bm[None, :, :], writes=[bslb], slot=bslb)
    mo = kb.sbuf("mo", [2, 4, 768], F32)
    mob = Buf("mo")
    pm = [(kb.psum(f"pm{i}", [128, 512], F32), Buf(f"pm{i}", psum=True)) for i in range(2)]
    for m in range(4):
        s = m % 2
        kb.sp.dma(wsl[s][:], wm[m].rearrange("(k p) n -> p k n", p=128), writes=[wslb[s]], slot=wslb[s])
        for hf in range(2):
            p_, pb = pm[hf]
            for k in range(KC):
                kb.pe.op(lambda e: e.matmul(p_[0:2, 0:384], lhsT=cT[:, k, :], rhs=wsl[s][:, k, hf * 384:(hf + 1) * 384],
                                            start=(k == 0), stop=(k == KC - 1)),
                         reads=[cTb, wslb[s]], writes=[pb], signal=(k == KC - 1))
            kb.dve.op(lambda e: e.tensor_tensor(out=mo[:, m, hf * 384:(hf + 1) * 384], in0=p_[0:2, 0:384],
                                                in1=bsl[:, m, hf * 384:(hf + 1) * 384], op=ALU.add),
                      reads=[pb, bslb], writes=[mob])
    kb.sp.dma(modo.rearrange("m b n -> b m n"), mo[:], reads=[mob], writes=[modob], slot=mob)

    emit_norm_transpose(kb, x, xnT, ntok, ident, identb, xb, xnTb, "nt")
    return kb.finish([xnTb, modob])


class WStream:
    def __init__(self, kb, nslots=3, tag="ws"):
        self.kb = kb
        self.t = [kb.sbuf(f"{tag}{i}", [128, 16, 512], BF16) for i in range(nslots)]
        self.b = [Buf(f"{tag}{i}") for i in range(nslots)]
        self.i = 0

    def load(self, w, wbuf, k0, nk, c0, ncol=512):
        s = self.i % len(self.t)
        self.i += 1
        src = w[k0 * 128:(k0 + nk) * 128, c0:c0 + ncol].rearrange("(k p) n -> p k n", p=128)
        self.kb.sp.dma(self.t[s][:, 0:nk, 0:ncol], src, reads=[wbuf], writes=[self.b[s]], slot=self.b[s])
        return self.t[s], self.b[s]


def cast_weight(kb, dst, src, buf, rows_per=128):
    K = src.shape[0]
    for r in range(0, K, rows_per):
        kb.pool.dma(dst[r:r + rows_per, :], src[r:r + rows_per, :], writes=[buf], slot=buf)


def load_pk(kb, dst, src_vec, buf, nk):
    kb.sp.dma(dst, src_vec.rearrange("(k p) -> p k", p=128), writes=[buf], slot=buf,
              allow_slow_non_contiguous=True)


class NormT:
    def __init__(self, kb, tag, ident, identb, pt_banks):
        self.kb = kb
        self.ident, self.identb = ident, identb
        self.xt = [kb.sbuf(f"{tag}_x{i}", [128, D], F32) for i in range(2)]
        self.xtb = [Buf(f"{tag}_x{i}") for i in range(2)]
        self.ss = [kb.sbuf(f"{tag}_ss{i}", [128, 1], F32) for i in range(2)]
        self.ssb = [Buf(f"{tag}_ss{i}") for i in range(2)]
        self.xn = [kb.sbuf(f"{tag}_xn{i}", [128, D], BF16) for i in range(2)]
        self.xnb = [Buf(f"{tag}_xn{i}") for i in range(2)]
        self.epsc = kb.sbuf(f"{tag}_eps", [128, 1], F32)
        self.epsb = Buf(f"{tag}_eps")
        kb.pool.op(lambda e: e.memset(self.epsc[:], EPS), writes=[self.epsb])
        self.pt = pt_banks
        self.it = 0

    def tile(self, src_rows, src_buf, dstT, dstTb, col0):
        kb = self.kb
        s = self.it % 2
        self.it += 1
        xt, xtb, ss, ssb, xn, xnb = self.xt[s], self.xtb[s], self.ss[s], self.ssb[s], self.xn[s], self.xnb[s]
        kb.sp.dma(xt[:], src_rows, reads=[src_buf], writes=[xtb], slot=xtb)
        kb.act.op(lambda e: e.activation(out=xn[:], in_=xt[:], func=AF.Square, accum_out=ss[:]),
                  reads=[xtb], writes=[xnb, ssb])
        kb.act.op(lambda e: e.activation(out=ss[:], in_=ss[:], func=AF.Sqrt, scale=1.0 / D, bias=self.epsc[:]),
                  reads=[ssb, self.epsb], writes=[ssb])
        kb.dve.op(lambda e: e.reciprocal(out=ss[:], in_=ss[:]), reads=[ssb], writes=[ssb])
        kb.act.op(lambda e: e.activation(out=xn[:], in_=xt[:], func=AF.Copy, scale=ss[:]),
                  reads=[xtb, ssb], writes=[xnb])
        for half in range(2):
            pt, ptb = self.pt[half]
            for kk in range(8):
                k = half * 8 + kk
                kb.pe.op(lambda e: e.transpose(pt[:, kk, :], xn[:, k * 128:(k + 1) * 128], self.ident[:]),
                         reads=[xnb, self.identb], writes=[ptb], signal=(kk == 7))
            if half == 0:
                kb.dve.op(lambda e: e.tensor_copy(out=dstT[:, 0:8, col0:col0 + 128], in_=pt[:]),
                          reads=[ptb], writes=[dstTb])
            else:
                kb.act.op(lambda e: e.copy(out=dstT[:, 8:16, col0:col0 + 128], in_=pt[:]),
                          reads=[ptb], writes=[dstTb])


TG = 512
DFF = 5632
FC = DFF // 128
CCH = 1024
HALO = 32


def emit_p2(kb, ntok, a, ident, identb):
    nc = kb.nc
    NTG = ntok // TG
    PB = [(kb.psum(f"pb{i}", [128, 512], F32), Buf(f"pb{i}", psum=True)) for i in range(6)]
    PT = [(kb.psum(f"ptb{i}", [128, 8, 128], BF16), Buf(f"ptb{i}", psum=True)) for i in range(2)]
    wnames = ["w_uc", "w_gl", "w_ao", "w_go", "w_co", "w_out", "w_gu", "w_dn"]
    wb = {}
    for n in wnames:
        src = a[n]
        dst = kb.dscratch(n + "_bf", list(src.shape), BF16)
        b = Buf(n + "_bf")
        cast_weight(kb, dst, src, b)
        wb[n] = (dst, b)
    ws = WStream(kb, 3)
    modv = a["mod"]
    pv = kb.sbuf("pv", [128, 8, KC], F32)
    pvb = Buf("pv")
    load_pk(kb, pv[:, 0, :], modv[0, 0:D], pvb, KC)
    load_pk(kb, pv[:, 1, :], modv[0, D:2 * D], pvb, KC)
    load_pk(kb, pv[:, 2, :], a["g_mix"], pvb, KC)
    load_pk(kb, pv[:, 3, :], modv[1, 0:D], pvb, KC)
    load_pk(kb, pv[:, 4, :], modv[1, D:2 * D], pvb, KC)
    load_pk(kb, pv[:, 5, :], a["g_ffn"], pvb, KC)
    kb.dve.op(lambda e: e.scalar_tensor_tensor(out=pv[:, 6, :], in0=pv[:, 1, :], scalar=1.0, in1=pv[:, 2, :],
                                               op0=ALU.add, op1=ALU.mult), reads=[pvb], writes=[pvb])
    kb.dve.op(lambda e: e.scalar_tensor_tensor(out=pv[:, 7, :], in0=pv[:, 4, :], scalar=1.0, in1=pv[:, 5, :],
                                               op0=ALU.add, op1=ALU.mult), reads=[pvb], writes=[pvb])
    gbc = kb.sbuf("gbc", [128, 2, D], F32)
    gbcb = Buf("gbc")
    for i in range(2):
        kb.sp.dma(gbc[:, i, :], modv[i, 2 * D:3 * D].partition_broadcast(128), writes=[gbcb], slot=gbcb)
    cw = kb.sbuf("cw", [128, 8, 31], F32)
    cp = kb.sbuf("cp", [128, 3, 8], F32)
    cwb = Buf("cw")
    for c in range(8):
        kb.sp.dma(cw[:, c, :], a["conv_w"][:, c * 128:(c + 1) * 128].rearrange("k p -> p k"), writes=[cwb],
                  slot=cwb, allow_slow_non_contiguous=True)
    load_pk(kb, cp[:, 0, :], a["conv_b"], cwb, 8)
    load_pk(kb, cp[:, 1, :], a["ln_g"], cwb, 8)
    load_pk(kb, cp[:, 2, :], a["ln_b"], cwb, 8)
    flag = kb.sbuf("flag", [128, 1], F32)
    kb.sp.dma(flag[:], a["halo_flag"], writes=[cwb], slot=cwb)
    ones32 = kb.sbuf("ones32", [128, 128], F32)
    onesb = Buf("ones32")
    kb.pool.op(lambda e: e.memset(ones32[:], 1.0), writes=[onesb])
    epsc = kb.sbuf("epsc", [128, 1], F32)
    kb.pool.op(lambda e: e.memset(epsc[:], EPS), writes=[onesb])

    hT = kb.sbuf("hT", [128, KC, TG], BF16)
    hTb = Buf("hT")
    hh = kb.sbuf("hh", [128, KC, HALO], BF16)
    hhb = Buf("hh")
    ubuf = kb.sbuf("ubuf", [128, 8, HALO + TG], F32)
    ub = [Buf(f"ub{c}") for c in range(8)]
    acc = kb.sbuf("acc", [128, 8, TG], F32)
    accb = [Buf(f"acc{c}") for c in range(8)]
    sq = [kb.sbuf(f"sq{i}", [128, TG], F32) for i in range(2)]
    sqb = [Buf(f"sq{i}") for i in range(2)]
    sg = [kb.sbuf(f"sg{i}", [128, TG], F32) for i in range(2)]
    sgb = [Buf(f"sg{i}") for i in range(2)]
    sgh = kb.sbuf("sgh", [128, HALO], F32)
    sghb = Buf("sgh")
    mean = kb.sbuf("mean", [128, TG], F32)
    rstd = kb.sbuf("rstd", [128, TG], F32)
    msq = kb.sbuf("msq", [128, TG], F32)
    statb = Buf("stat")
    big = kb.sbuf("big", [128, FC, TG], BF16)
    bigb = Buf("big")
    cT = big[:, 0:8, :]
    oa = big[:, 8:12, :]
    od = big[:, 12:20, :]
    mg = big[:, 20:36, :]
    macc = acc[:, 0:4, :]
    maccb = accb[0:4]
    tmp = sq
    tmpb = sqb
    xp = [kb.sbuf(f"xp{i}", [128, 512], F32) for i in range(2)]
    xpb = [Buf(f"xp{i}") for i in range(2)]
    nt = NormT(kb, "nt", ident, identb, PT)
    xTo, xTob = hT, hTb

    x_rows, xb_in = a["x"], a["x_buf"]
    xout, xoutb = a["xout"], a["xout_buf"]
    xnT_h, xnTb_in = a["xnT_h"], a["xnT_h_buf"]
    xnT_o, xnTob = a["xnT_out"], a["xnT_out_buf"]
    xnT_v = xnT_h.rearrange("(k p) t -> p k t", p=128)
    xnTo_v = xnT_o.rearrange("(k p) t -> p k t", p=128)
    oaT_v = a["o_aT"].rearrange("(k p) t -> p k t", p=128)
    odT_v = a["o_dT"].rearrange("(k p) t -> p k t", p=128)
    cnt = {"pb": 0, "sg": 0, "tmp": 0, "xp": 0}

    def rot(name, n=2):
        v = cnt[name] % n
        cnt[name] += 1
        return v

    for g in range(NTG):
        t0 = g * TG
        kb.sp.dma(hT[:], xnT_v[:, :, HALO + t0:HALO + t0 + TG], reads=[xnTb_in], writes=[hTb], slot=hTb)
        for k in range(KC):
            kb.act.op(lambda e: e.activation(out=hT[:, k, :], in_=hT[:, k, :], func=AF.Identity,
                                             scale=pv[:, 6, k:k + 1], bias=pv[:, 0, k:k + 1]),
                      reads=[hTb, pvb], writes=[hTb])
        if g == 0:
            kb.sp.dma(hh[:], xnT_v[:, :, 0:HALO], reads=[xnTb_in], writes=[hhb], slot=hhb)
            for k in range(KC):
                kb.act.op(lambda e: e.activation(out=hh[:, k, :], in_=hh[:, k, :], func=AF.Identity,
                                                 scale=pv[:, 6, k:k + 1], bias=pv[:, 0, k:k + 1]),
                          reads=[hhb, pvb], writes=[hhb])
        else:
            for c in range(8):
                kb.pool.op(lambda e: e.tensor_copy(out=ubuf[:, c, 0:HALO], in_=ubuf[:, c, TG:TG + HALO]),
                           reads=[ub[c]], writes=[ub[c]])
        w_uc, w_ucb = wb["w_uc"]
        for cg in range(4):
            wt, wtb = ws.load(w_uc, w_ucb, 0, KC, cg * 512)
            for cc in range(4):
                c = cg * 4 + cc
                p_, pb_ = PB[rot("pb")]
                for k in range(KC):
                    kb.pe.op(lambda e: e.matmul(p_[:], lhsT=wt[:, k, cc * 128:(cc + 1) * 128], rhs=hT[:, k, :],
                                                start=(k == 0), stop=(k == KC - 1)),
                             reads=[wtb, hTb], writes=[pb_], signal=(k == KC - 1))
                if g == 0:
                    ph, phb = PB[4 + (c % 2)]
                    for k in range(KC):
                        kb.pe.op(lambda e: e.matmul(ph[:, 0:HALO], lhsT=wt[:, k, cc * 128:(cc + 1) * 128],
                                                    rhs=hh[:, k, :], start=(k == 0), stop=(k == KC - 1)),
                                 reads=[wtb, hhb], writes=[phb], signal=(k == KC - 1))
                if c < 8:
                    kb.act.op(lambda e: e.copy(out=ubuf[:, c, HALO:], in_=p_[:]), reads=[pb_], writes=[ub[c]])
                    if g == 0:
                        kb.act.op(lambda e: e.copy(out=ubuf[:, c, 0:HALO], in_=ph[:, 0:HALO]), reads=[phb],
                                  writes=[ub[c]])
                else:
                    s_ = rot("sg")
                    kb.act.op(lambda e: e.activation(out=sg[s_][:], in_=p_[:], func=AF.Sigmoid), reads=[pb_],
                              writes=[sgb[s_]])
                    kb.dve.op(lambda e: e.tensor_tensor(out=ubuf[:, c - 8, HALO:], in0=ubuf[:, c - 8, HALO:],
                                                        in1=sg[s_][:], op=ALU.mult),
                              reads=[sgb[s_], ub[c - 8]], writes=[ub[c - 8]])
                    if g == 0:
                        kb.act.op(lambda e: e.activation(out=sgh[:], in_=ph[:, 0:HALO], func=AF.Sigmoid),
                                  reads=[phb], writes=[sghb])
                        kb.dve.op(lambda e: e.scalar_tensor_tensor(out=ubuf[:, c - 8, 0:HALO], in0=sgh[:],
                                                                   scalar=flag[:, 0:1], in1=ubuf[:, c - 8, 0:HALO],
                                                                   op0=ALU.mult, op1=ALU.mult),
                                  reads=[sghb, ub[c - 8], cwb], writes=[ub[c - 8]])
        OFF = HALO - 30
        eng_of = [kb.dve] * 8
        for k in range(31):
            for c in range(8):
                eng = eng_of[c]
                src = ubuf[:, c, OFF + k:OFF + k + TG]
                if k == 0:
                    eng.op(lambda e: e.tensor_scalar(out=acc[:, c, :], in0=src, scalar1=cw[:, c, 0:1],
                                                     scalar2=cp[:, 0, c:c + 1], op0=ALU.mult, op1=ALU.add),
                           reads=[ub[c], cwb], writes=[accb[c]])
                else:
                    eng.op(lambda e: e.scalar_tensor_tensor(out=acc[:, c, :], in0=src, scalar=cw[:, c, k:k + 1],
                                                            in1=acc[:, c, :], op0=ALU.mult, op1=ALU.add),
                           reads=[ub[c], cwb, accb[c]], writes=[accb[c]])
        (p1, p1b), (p2, p2b) = PB[2], PB[3]
        for c in range(8):
            s_ = rot("tmp")
            kb.act.op(lambda e: e.activation(out=sq[s_][:], in_=acc[:, c, :], func=AF.Square), reads=[accb[c]],
                      writes=[sqb[s_]])
            kb.pe.op(lambda e: e.matmul(p1[:], lhsT=ones32[:], rhs=acc[:, c, :], start=(c == 0), stop=(c == 7)),
                     reads=[onesb, accb[c]], writes=[p1b])
            kb.pe.op(lambda e: e.matmul(p2[:], lhsT=ones32[:], rhs=sq[s_][:], start=(c == 0), stop=(c == 7)),
                     reads=[onesb, sqb[s_]], writes=[p2b])
        kb.act.op(lambda e: e.activation(out=mean[:], in_=p1[:], func=AF.Copy, scale=1.0 / CCH), reads=[p1b],
                  writes=[statb])
        kb.dve.op(lambda e: e.tensor_tensor(out=msq[:], in0=mean[:], in1=mean[:], op=ALU.mult), reads=[statb],
                  writes=[statb])
        kb.dve.op(lambda e: e.scalar_tensor_tensor(out=rstd[:], in0=p2[:], scalar=1.0 / CCH, in1=msq[:],
                                                   op0=ALU.mult, op1=ALU.subtract), reads=[p2b, statb],
                  writes=[statb])
        kb.act.op(lambda e: e.activation(out=rstd[:], in_=rstd[:], func=AF.Sqrt, bias=epsc[:]),
                  reads=[statb, onesb], writes=[statb])
        kb.dve.op(lambda e: e.reciprocal(out=rstd[:], in_=rstd[:]), reads=[statb], writes=[statb])
        for c in range(8):
            kb.dve.op(lambda e: e.tensor_tensor(out=acc[:, c, :], in0=acc[:, c, :], in1=mean[:], op=ALU.subtract),
                      reads=[accb[c], statb], writes=[accb[c]])
            kb.pool.op(lambda e: e.tensor_tensor(out=acc[:, c, :], in0=acc[:, c, :], in1=rstd[:], op=ALU.mult),
                       reads=[accb[c], statb], writes=[accb[c]])
            kb.act.op(lambda e: e.activation(out=cT[:, c, :], in_=acc[:, c, :], func=AF.Silu,
                                             scale=cp[:, 1, c:c + 1], bias=cp[:, 2, c:c + 1]),
                      reads=[accb[c], cwb], writes=[bigb])
        kb.sp.dma(oa, oaT_v[:, :, t0:t0 + TG], reads=[a["o_buf"]], writes=[bigb], slot=bigb)
        kb.sp.dma(od, odT_v[:, :, t0:t0 + TG], reads=[a["o_buf"]], writes=[bigb], slot=bigb)
        w_gl, w_glb = wb["w_gl"]
        ywl = [(wb["w_ao"], oa, 4), (wb["w_go"], od, 8), (wb["w_co"], cT, 8)]
        for og in range(4):
            for br in range(3):
                gw, gwb = ws.load(w_gl, w_glb, 0, KC, br * D + og * 512)
                (yw_d, yw_db), ysrc, ynk = ywl[br]
                yw, ywb = ws.load(yw_d, yw_db, 0, ynk, og * 512)
                for cc in range(4):
                    pg, pgb = PB[rot("pb")]
                    for k in range(KC):
                        kb.pe.op(lambda e: e.matmul(pg[:], lhsT=gw[:, k, cc * 128:(cc + 1) * 128], rhs=hT[:, k, :],
                                                    start=(k == 0), stop=(k == KC - 1)),
                                 reads=[gwb, hTb], writes=[pgb], signal=(k == KC - 1))
                    py, pyb = PB[2 + (cnt["pb"] % 2)]
                    for k in range(ynk):
                        kb.pe.op(lambda e: e.matmul(py[:], lhsT=yw[:, k, cc * 128:(cc + 1) * 128], rhs=ysrc[:, k, :],
                                                    start=(k == 0), stop=(k == ynk - 1)),
                                 reads=[ywb, bigb], writes=[pyb], signal=(k == ynk - 1))
                    s_ = rot("sg")
                    kb.act.op(lambda e: e.activation(out=sg[s_][:], in_=pg[:], func=AF.Sigmoid), reads=[pgb],
                              writes=[sgb[s_]])
                    if br == 0:
                        kb.dve.op(lambda e: e.tensor_tensor(out=macc[:, cc, :], in0=sg[s_][:], in1=py[:], op=ALU.mult),
                                  reads=[sgb[s_], pyb], writes=[maccb[cc]])
                    else:
                        t_ = rot("tmp")
                        kb.dve.op(lambda e: e.tensor_tensor(out=tmp[t_][:], in0=sg[s_][:], in1=py[:], op=ALU.mult),
                                  reads=[sgb[s_], pyb], writes=[tmpb[t_]])
                        if br == 1:
                            kb.pool.op(lambda e: e.tensor_tensor(out=macc[:, cc, :], in0=macc[:, cc, :], in1=tmp[t_][:],
                                                                 op=ALU.add), reads=[maccb[cc], tmpb[t_]],
                                       writes=[maccb[cc]])
                        else:
                            kb.pool.op(lambda e: e.tensor_tensor(out=mg[:, og * 4 + cc, :], in0=macc[:, cc, :],
                                                                 in1=tmp[t_][:], op=ALU.add),
                                       reads=[maccb[cc], tmpb[t_]], writes=[bigb])
        w_o, w_ob = wb["w_out"]
        for fg in range(4):
            wt, wtb = ws.load(w_o, w_ob, 0, KC, fg * 512)
            for j in range(4):
                r0 = t0 + j * 128
                p_, pb_ = PB[4 + rot("pb")]
                for k in range(KC):
                    kb.pe.op(lambda e: e.matmul(p_[:], lhsT=mg[:, k, j * 128:(j + 1) * 128], rhs=wt[:, k, :],
                                                start=(k == 0), stop=(k == KC - 1)),
                             reads=[wtb, bigb], writes=[pb_], signal=(k == KC - 1))
                x_ = rot("xp")
                kb.pool.dma(xp[x_][:], x_rows[r0:r0 + 128, fg * 512:(fg + 1) * 512], reads=[xb_in], writes=[xpb[x_]],
                          slot=xpb[x_])
                t_ = rot("tmp")
                kb.dve.op(lambda e: e.tensor_tensor(out=tmp[t_][:], in0=p_[:], in1=gbc[:, 0, fg * 512:(fg + 1) * 512],
                                                    op=ALU.mult), reads=[pb_, gbcb], writes=[tmpb[t_]])
                kb.pool.op(lambda e: e.tensor_tensor(out=xp[x_][:], in0=xp[x_][:], in1=tmp[t_][:], op=ALU.add),
                           reads=[xpb[x_], tmpb[t_]], writes=[xpb[x_]])
                kb.pool.dma(xout[r0:r0 + 128, fg * 512:(fg + 1) * 512], xp[x_][:], reads=[xpb[x_]], writes=[xoutb],
                          slot=xpb[x_])
        for j in range(4):
            r0 = t0 + j * 128
            nt.tile(xout[r0:r0 + 128, :], xoutb, hT, hTb, j * 128)
        for k in range(KC):
            kb.act.op(lambda e: e.activation(out=hT[:, k, :], in_=hT[:, k, :], func=AF.Identity,
                                             scale=pv[:, 7, k:k + 1], bias=pv[:, 3, k:k + 1]),
                      reads=[hTb, pvb], writes=[hTb])
        w_gu, w_gub = wb["w_gu"]
        for t in range(FC // 4):
            wa, wab = ws.load(w_gu, w_gub, 0, KC, t * 512)
            wv, wvb = ws.load(w_gu, w_gub, 0, KC, DFF + t * 512)
            for cc in range(4):
                f = t * 4 + cc
                pa, pab = PB[rot("pb")]
                for k in range(KC):
                    kb.pe.op(lambda e: e.matmul(pa[:], lhsT=wa[:, k, cc * 128:(cc + 1) * 128], rhs=hT[:, k, :],
                                                start=(k == 0), stop=(k == KC - 1)),
                             reads=[wab, hTb], writes=[pab], signal=(k == KC - 1))
                pv_, pvb_ = PB[2 + (cnt["pb"] % 2)]
                for k in range(KC):
                    kb.pe.op(lambda e: e.matmul(pv_[:], lhsT=wv[:, k, cc * 128:(cc + 1) * 128], rhs=hT[:, k, :],
                                                start=(k == 0), stop=(k == KC - 1)),
                             reads=[wvb, hTb], writes=[pvb_], signal=(k == KC - 1))
                s_ = rot("sg")
                kb.act.op(lambda e: e.activation(out=sg[s_][:], in_=pa[:], func=AF.Silu), reads=[pab],
                          writes=[sgb[s_]])
                kb.dve.op(lambda e: e.tensor_tensor(out=big[:, f, :], in0=sg[s_][:], in1=pv_[:], op=ALU.mult),
                          reads=[sgb[s_], pvb_], writes=[bigb])
        w_dn, w_dnb = wb["w_dn"]
        parts = [(0, 16), (16, 16), (32, 12)]
        for fg in range(4):
            for pi, (k0, nk) in enumerate(parts):
                wt, wtb = ws.load(w_dn, w_dnb, k0, nk, fg * 512)
                for j in range(4):
                    p_, pb_ = PB[j]
                    for kk in range(nk):
                        kb.pe.op(lambda e: e.matmul(p_[:], lhsT=big[:, k0 + kk, j * 128:(j + 1) * 128],
                                                    rhs=wt[:, kk, :], start=(pi == 0 and kk == 0),
                                                    stop=(pi == 2 and kk == nk - 1)),
                                 reads=[wtb, bigb], writes=[pb_], signal=(kk == nk - 1))
            for j in range(4):
                r0 = t0 + j * 128
                p_, pb_ = PB[j]
                x_ = rot("xp")
                kb.pool.dma(xp[x_][:], xout[r0:r0 + 128, fg * 512:(fg + 1) * 512], reads=[xoutb], writes=[xpb[x_]],
                          slot=xpb[x_])
                t_ = rot("tmp")
                kb.dve.op(lambda e: e.tensor_tensor(out=tmp[t_][:], in0=p_[:], in1=gbc[:, 1, fg * 512:(fg + 1) * 512],
                                                    op=ALU.mult), reads=[pb_, gbcb], writes=[tmpb[t_]])
                kb.pool.op(lambda e: e.tensor_tensor(out=xp[x_][:], in0=xp[x_][:], in1=tmp[t_][:], op=ALU.add),
                           reads=[xpb[x_], tmpb[t_]], writes=[xpb[x_]])
                kb.pool.dma(xout[r0:r0 + 128, fg * 512:(fg + 1) * 512], xp[x_][:], reads=[xpb[x_]], writes=[xoutb],
                          slot=xpb[x_])
        for j in range(4):
            r0 = t0 + j * 128
            nt.tile(xout[r0:r0 + 128, :], xoutb, xTo, xTob, j * 128)
        kb.sp.dma(xnTo_v[:, :, t0:t0 + TG], xTo[:], reads=[xTob], writes=[xnTob], slot=xTob)


def build_p2(ntok):
    kb = KB()
    a = {}
    a["x"] = kb.din("x", [ntok, D], F32)
    a["xnT_h"] = kb.din("xnT_h", [D, HALO + ntok], BF16)
    a["halo_flag"] = kb.din("halo_flag", [128, 1], F32)
    a["o_aT"] = kb.din("o_aT", [512, ntok], BF16)
    a["o_dT"] = kb.din("o_dT", [1024, ntok], BF16)
    a["mod"] = kb.din("mod", [2, 3 * D], F32)
    a["g_mix"] = kb.din("g_mix", [D], F32)
    a["g_ffn"] = kb.din("g_ffn", [D], F32)
    a["w_uc"] = kb.din("w_uc", [D, 2048], F32)
    a["w_gl"] = kb.din("w_gl", [D, 3 * D], F32)
    a["w_ao"] = kb.din("w_ao", [512, D], F32)
    a["w_go"] = kb.din("w_go", [1024, D], F32)
    a["w_co"] = kb.din("w_co", [1024, D], F32)
    a["w_out"] = kb.din("w_out", [D, D], F32)
    a["w_gu"] = kb.din("w_gu", [D, 2 * DFF], F32)
    a["w_dn"] = kb.din("w_dn", [DFF, D], F32)
    a["conv_w"] = kb.din("conv_w", [31, CCH], F32)
    a["conv_b"] = kb.din("conv_b", [CCH], F32)
    a["ln_g"] = kb.din("ln_g", [CCH], F32)
    a["ln_b"] = kb.din("ln_b", [CCH], F32)
    a["xout"] = kb.dout("xout", [ntok, D], F32)
    a["xnT_out"] = kb.dout("xnT_out", [D, ntok], BF16)
    for n in ["x_buf", "xnT_h_buf", "o_buf", "xout_buf", "xnT_out_buf"]:
        a[n] = Buf(n)
    ident, identb = make_ident(kb, BF16)
    emit_p2(kb, ntok, a, ident, identb)
    return kb.finish([a["xout_buf"], a["xnT_out_buf"]])


DBG = set()
NTOKC = 768 + 384 + 256 + 4
C_U2, C_SL2, C_L2, C_IA, C_IB, C_MP, C_MC, C_ONE, C_MISC = range(9)
TWO_PI = 6.283185307179586
CW1 = 6.28125
CW2 = TWO_PI - CW1


def host_consts():
    c = np.zeros((128, 9, 128), np.float32)
    m = np.arange(128)[:, None]
    i = np.arange(128)[None, :]
    same = (m // 64) == (i // 64)
    c[:, C_U2] = (m <= i) & same
    c[:, C_SL2] = (m > i) & same
    c[:, C_L2] = (m >= i) & same
    c[:, C_IA] = (m < 64) * np.ones((1, 128))
    c[:, C_IB] = (m >= 64) * np.ones((1, 128))
    c[:, C_MP] = (m >= i)
    c[:, C_MC] = (m <= i)
    c[:, C_ONE] = 1.0
    inv = (500000.0 ** (-np.arange(0, 32, 2, dtype=np.float32) / 32)).astype(np.float32)
    c[:, C_MISC, 0:16] = inv[None, :]
    return c


class PRegion:
    def __init__(self, kb):
        self.banks = [kb.psum(f"bank{i}", [128, 512], F32) for i in range(8)]

    def reg(self, bank, r0, nr=1):
        return self.banks[bank][:, r0 * 128:(r0 + nr) * 128]


def barrier(kb, extra_bufs=()):
    engs = [kb.pe, kb.act, kb.dve, kb.pool, kb.sp]
    for e in engs:
        for f in engs:
            if f is not e and f.n > 0:
                e._wait((f.sem, f.n, f))
        for b in kb.dma_slots:
            if b.dsem is not None and b.dcnt > 0:
                e._wait((b.dsem, b.dcnt, None))


def emit_p1(kb, S, a, phases="ABC"):
    nc = kb.nc
    NT = S // 128
    NG = S // 512
    NU = S // 2048
    PR = PRegion(kb)
    bankb = [Buf(f"bank{i}", psum=True) for i in range(8)]
    qk_tok = kb.dscratch("qk_tok", [S, 6, 128], BF16)
    v_tok = kb.dscratch("v_tok", [S, 3, 128], BF16)
    zs_tok = kb.dscratch("zs_tok", [S, 256], F32)
    gT = kb.dout("gT", [6, 128, S], F32) if 'gtout' in DBG else kb.dscratch("gT", [6, 128, S], F32)
    qkb, vtb, zsb, gTb = Buf("qk_tok"), Buf("v_tok"), Buf("zs_tok"), Buf("gT")
    cst = kb.sbuf("cst", [128, 9, 128], F32)
    cstb = Buf("cst")
    kb.sp.dma(cst[:], a["consts"], writes=[cstb], slot=cstb)
    ident32 = kb.sbuf("ident32", [128, 128], F32)
    identb32 = Buf("ident32b")
    identbf = kb.sbuf("identbf", [128, 128], BF16)
    kb.pool.op(lambda e: e.memset(ident32[:], 0.0), writes=[identb32])
    kb.pool.op(lambda e: e.affine_select(out=ident32[:], in_=ident32[:], pattern=[[-1, 128]],
                                         compare_op=ALU.not_equal, fill=1.0, base=0, channel_multiplier=1),
               reads=[identb32], writes=[identb32])
    kb.dve.op(lambda e: e.tensor_copy(out=identbf[:], in_=ident32[:]), reads=[identb32], writes=[identb32])
    GB = kb.sbuf("GB", [128, NT, 8], F32)
    GBb = Buf("GB")
    epsc = kb.sbuf("epsc", [128, 1], F32)
    onec = kb.sbuf("onec", [128, 1], F32)
    kb.pool.op(lambda e: e.memset(epsc[:], EPS), writes=[cstb])
    kb.pool.op(lambda e: e.memset(onec[:], 1.0), writes=[cstb])
    sl1 = kb.sbuf("sl1", [128, 129], F32)
    kb.dve.op(lambda e: e.tensor_copy(out=sl1[:, 0:128], in_=cst[:, C_SL2, :]), reads=[cstb], writes=[cstb])
    kb.dve.op(lambda e: e.tensor_copy(out=sl1[:, 128:129], in_=cst[:, C_ONE, 0:1]), reads=[cstb], writes=[cstb])
    maskbf = kb.sbuf("maskbf", [128, 256], BF16)
    kb.dve.op(lambda e: e.tensor_copy(out=maskbf[:, 0:128], in_=cst[:, C_MP, :]), reads=[cstb], writes=[cstb])
    kb.dve.op(lambda e: e.tensor_copy(out=maskbf[:, 128:256], in_=cst[:, C_MC, :]), reads=[cstb], writes=[cstb])
    onesbf = kb.sbuf("onesbf", [128, 128], BF16)
    kb.dve.op(lambda e: e.tensor_copy(out=onesbf[:], in_=cst[:, C_ONE, :]), reads=[cstb], writes=[cstb])

    kb.push()
    wtok = kb.sbuf("wtok", [128, KC, 1536], BF16)
    wfm = kb.sbuf("wfm", [128, KC, 768], BF16)
    wAb = Buf("wA")
    for k in range(KC):
        kb.pool.dma(wtok[:, k, 0:NTOKC], a["w_tok"][k * 128:(k + 1) * 128, :], writes=[wAb], slot=wAb)
        kb.pool.dma(wfm[:, k, :], a["w_fm"][k * 128:(k + 1) * 128, :], writes=[wAb], slot=wAb)
    pv = kb.sbuf("pv", [128, 4, KC], F32)
    pvb = Buf("pv")
    load_pk(kb, pv[:, 0, :], a["mod"][0, 0:D], pvb, KC)
    load_pk(kb, pv[:, 1, :], a["mod"][0, D:2 * D], pvb, KC)
    load_pk(kb, pv[:, 2, :], a["g_mix"], pvb, KC)
    kb.dve.op(lambda e: e.scalar_tensor_tensor(out=pv[:, 3, :], in0=pv[:, 1, :], scalar=1.0, in1=pv[:, 2, :],
                                               op0=ALU.add, op1=ALU.mult), reads=[pvb], writes=[pvb])
    gqk = kb.sbuf("gqk", [128, 6, 128], F32)
    for sl in range(6):
        src = a["q_norm_g"] if sl < 3 else a["k_norm_g"]
        kb.sp.dma(gqk[:, sl, :], src.partition_broadcast(128), writes=[pvb], slot=pvb)
    gcw = kb.sbuf("gcw", [128, 6, 4], F32)
    for ch in range(6):
        kb.sp.dma(gcw[:, ch, :], a["gconv_w"][:, ch * 128:(ch + 1) * 128].rearrange("k p -> p k"), writes=[pvb],
                  slot=pvb, allow_slow_non_contiguous=True)
    cs = kb.sbuf("cs", [128, 2, NT, 16], F32)
    kb.push()
    posi = kb.sbuf("posi", [128, NT], I32)
    posf = kb.sbuf("posf", [128, NT], F32)
    ang = kb.sbuf("ang", [128, 2, NT, 16], F32)
    kq = kb.sbuf("kq", [128, 2, NT, 16], F32)
    ki = kb.sbuf("ki", [128, 2, NT, 16], I32)
    rpb = Buf("rope")
    kb.sp.dma(posi[:], a["pos"].rearrange("(t p) -> p t", p=128), writes=[rpb], slot=rpb,
              allow_slow_non_contiguous=True)
    kb.dve.op(lambda e: e.tensor_copy(out=posf[:], in_=posi[:]), reads=[rpb], writes=[rpb])
    for f in range(16):
        kb.dve.op(lambda e: e.tensor_scalar(out=ang[:, 0, :, f], in0=posf[:], scalar1=cst[:, C_MISC, f:f + 1],
                                            scalar2=None, op0=ALU.mult), reads=[rpb, cstb], writes=[rpb])
    kb.dve.op(lambda e: e.tensor_scalar(out=ang[:, 1], in0=ang[:, 0], scalar1=float(np.pi / 2), scalar2=None,
                                        op0=ALU.add), reads=[rpb], writes=[rpb])
    kb.dve.op(lambda e: e.tensor_scalar(out=kq[:], in0=ang[:], scalar1=float(1.0 / TWO_PI), scalar2=None,
                                        op0=ALU.mult), reads=[rpb], writes=[rpb])
    kb.dve.op(lambda e: e.tensor_copy(out=ki[:], in_=kq[:]), reads=[rpb], writes=[rpb])
    kb.dve.op(lambda e: e.tensor_copy(out=kq[:], in_=ki[:]), reads=[rpb], writes=[rpb])
    kb.dve.op(lambda e: e.scalar_tensor_tensor(out=ang[:], in0=kq[:], scalar=-CW1, in1=ang[:], op0=ALU.mult,
                                               op1=ALU.add), reads=[rpb], writes=[rpb])
    kb.dve.op(lambda e: e.scalar_tensor_tensor(out=ang[:], in0=kq[:], scalar=-CW2, in1=ang[:], op0=ALU.mult,
                                               op1=ALU.add), reads=[rpb], writes=[rpb])
    kb.dve.op(lambda e: e.tensor_scalar(out=kq[:], in0=ang[:], scalar1=float(np.pi), scalar2=-TWO_PI, op0=ALU.is_gt,
                                        op1=ALU.mult), reads=[rpb], writes=[rpb])
    kb.dve.op(lambda e: e.tensor_tensor(out=ang[:], in0=ang[:], in1=kq[:], op=ALU.add), reads=[rpb], writes=[rpb])
    kb.dve.op(lambda e: e.tensor_scalar(out=kq[:], in0=ang[:], scalar1=float(-np.pi), scalar2=TWO_PI, op0=ALU.is_lt,
                                        op1=ALU.mult), reads=[rpb], writes=[rpb])
    kb.dve.op(lambda e: e.tensor_tensor(out=ang[:], in0=ang[:], in1=kq[:], op=ALU.add), reads=[rpb], writes=[rpb])
    kb.dve.op(lambda e: e.tensor_scalar(out=ang[:], in0=ang[:], scalar1=3.14159, scalar2=-3.14159,
                                        op0=ALU.min, op1=ALU.max), reads=[rpb], writes=[rpb])
    kb.act.op(lambda e: e.activation(out=cs[:], in_=ang[:], func=AF.Sin), reads=[rpb], writes=[rpb])
    barrier(kb)
    kb.pop()

    hT = [kb.sbuf(f"hTa{i}", [128, KC, 512], BF16) for i in range(2)]
    hTb = [Buf(f"hTa{i}") for i in range(2)]
    gx = kb.sbuf("gx", [128, 6, 3 + 512], F32)
    gxb = [Buf(f"gx{c}") for c in range(6)]
    gc = kb.sbuf("gc", [128, 6, 512], F32)
    gcb = [Buf(f"gc{c}") for c in range(6)]
    sqa = [kb.sbuf(f"sqa{i}", [128, 512], F32) for i in range(2)]
    sqab = [Buf(f"sqa{i}") for i in range(2)]
    rinv = [kb.sbuf(f"rinv{i}", [128, 512], F32) for i in range(2)]
    rinvb = [Buf(f"rinv{i}") for i in range(2)]
    tq = [kb.sbuf(f"tq{i}", [128, 6, 128], F32) for i in range(2)]
    tqb = [Buf(f"tq{i}") for i in range(2)]
    tsq = kb.sbuf("tsq", [128, 6, 128], F32)
    tsqb = Buf("tsq")
    rq = [kb.sbuf(f"rq{i}", [128, 6], F32) for i in range(2)]
    rqb = [Buf(f"rq{i}") for i in range(2)]
    rt = kb.sbuf("rt", [128, 4, 6, 16], F32)
    rtb = Buf("rt")
    qko = [kb.sbuf(f"qko{i}", [128, 6, 128], BF16) for i in range(2)]
    qkob = [Buf(f"qko{i}") for i in range(2)]
    vo = [kb.sbuf(f"vo{i}", [128, 3, 128], BF16) for i in range(2)]
    vob = [Buf(f"vo{i}") for i in range(2)]
    zo = [kb.sbuf(f"zo{i}", [128, 256], F32) for i in range(2)]
    zob = [Buf(f"zo{i}") for i in range(2)]
    xnT_v = a["xnT"].rearrange("(k p) t -> p k t", p=128)
    gT_v = gT.rearrange("c p t -> p c t")
    kb.pool.op(lambda e: e.memset(gx[:, :, 0:3], 0.0), writes=gxb)
    def prep_h(g):
        t0 = g * 512
        h_, hb_ = hT[g % 2], hTb[g % 2]
        kb.sp.dma(h_[:], xnT_v[:, :, t0:t0 + 512], reads=[a["xnT_buf"]], writes=[hb_], slot=hb_)
        for k in range(KC):
            kb.act.op(lambda e: e.activation(out=h_[:, k, :], in_=h_[:, k, :], func=AF.Identity,
                                             scale=pv[:, 3, k:k + 1], bias=pv[:, 0, k:k + 1]),
                      reads=[hb_, pvb], writes=[hb_])
            if k % 4 == 3:
                yield

    def fm_stream(g):
        t0 = g * 512
        h_, hb_ = hT[g % 2], hTb[g % 2]
        for ch in range(6):
            bk = ch % 2
            p_ = PR.banks[bk]
            for k in range(KC):
                kb.pe.op(lambda e: e.matmul(p_[:], lhsT=wfm[:, k, ch * 128:(ch + 1) * 128], rhs=h_[:, k, :],
                                            start=(k == 0), stop=(k == KC - 1)),
                         reads=[wAb, hb_], writes=[bankb[bk]], signal=(k == KC - 1))
            if g > 0:
                kb.pool.op(lambda e: e.tensor_copy(out=gx[:, ch, 0:3], in_=gx[:, ch, 512:515]), reads=[gxb[ch]],
                           writes=[gxb[ch]])
            yield
            kb.act.op(lambda e: e.copy(out=gx[:, ch, 3:515], in_=p_[:]), reads=[bankb[bk]], writes=[gxb[ch]])
            yield
            for tp in range(4):
                if tp == 0:
                    kb.dve.op(lambda e: e.tensor_scalar(out=gc[:, ch, :], in0=gx[:, ch, 0:512], scalar1=gcw[:, ch, 0:1],
                                                        scalar2=None, op0=ALU.mult), reads=[gxb[ch], pvb],
                              writes=[gcb[ch]])
                else:
                    kb.dve.op(lambda e: e.scalar_tensor_tensor(out=gc[:, ch, :], in0=gx[:, ch, tp:tp + 512],
                                                               scalar=gcw[:, ch, tp:tp + 1], in1=gc[:, ch, :],
                                                               op0=ALU.mult, op1=ALU.add),
                              reads=[gxb[ch], pvb, gcb[ch]], writes=[gcb[ch]])
            yield
            kb.act.op(lambda e: e.activation(out=gc[:, ch, :], in_=gc[:, ch, :], func=AF.Silu), reads=[gcb[ch]],
                      writes=[gcb[ch]])
            if ch < 4:
                s_ = ch % 2
                kb.act.op(lambda e: e.activation(out=sqa[s_][:], in_=gc[:, ch, :], func=AF.Square), reads=[gcb[ch]],
                          writes=[sqab[s_]])
                yield
                bk2 = 2 + (ch % 2)
                kb.pe.op(lambda e: e.matmul(PR.banks[bk2][:], lhsT=cst[:, C_ONE, :], rhs=sqa[s_][:], start=True, stop=True),
                         reads=[cstb, sqab[s_]], writes=[bankb[bk2]])
                yield
                kb.act.op(lambda e: e.activation(out=rinv[s_][:], in_=PR.banks[bk2][:], func=AF.Sqrt, bias=epsc[:]),
                          reads=[bankb[bk2], cstb], writes=[rinvb[s_]])
                yield
                kb.dve.op(lambda e: e.reciprocal(out=rinv[s_][:], in_=rinv[s_][:]), reads=[rinvb[s_]], writes=[rinvb[s_]])
                sc = float(128 ** -0.5) if ch < 2 else 1.0
                kb.dve.op(lambda e: e.scalar_tensor_tensor(out=gc[:, ch, :], in0=gc[:, ch, :], scalar=sc, in1=rinv[s_][:],
                                                           op0=ALU.mult, op1=ALU.mult),
                          reads=[gcb[ch], rinvb[s_]], writes=[gcb[ch]])
            yield
            kb.pool.dma(gT_v[:, ch, t0:t0 + 512], gc[:, ch, :], reads=[gcb[ch]], writes=[gTb], slot=gcb[ch])

    def tok_stream(g):
        t0 = g * 512
        h_, hb_ = hT[g % 2], hTb[g % 2]
        for j in range(4):
            tile = g * 4 + j
            s_ = tile % 2
            r0 = t0 + j * 128
            widths = [(0, 512, 4), (512, 512, 5), (1024, NTOKC - 1024, 6)]
            for (c0, wd, bk) in widths:
                for k in range(KC):
                    kb.pe.op(lambda e: e.matmul(PR.banks[bk][:, 0:wd], lhsT=h_[:, k, j * 128:(j + 1) * 128],
                                                rhs=wtok[:, k, c0:c0 + wd], start=(k == 0), stop=(k == KC - 1)),
                             reads=[wAb, hb_], writes=[bankb[bk]], signal=(k == KC - 1))
                yield
            kb.act.op(lambda e: e.copy(out=tq[s_][:, 0:4, :], in_=PR.banks[4][:].rearrange('p (a d) -> p a d', a=4)),
                      writes=[tqb[s_], bankb[4]])
            kb.act.op(lambda e: e.copy(out=tq[s_][:, 4:6, :], in_=PR.banks[5][:, 0:256].rearrange('p (a d) -> p a d', a=2)),
                      writes=[tqb[s_], bankb[5]])
            yield
            kb.dve.op(lambda e: e.tensor_copy(out=vo[s_][:, 0:2, :], in_=PR.banks[5][:, 256:512].rearrange('p (a d) -> p a d', a=2)),
                      writes=[vob[s_], bankb[5]])
            kb.dve.op(lambda e: e.tensor_copy(out=vo[s_][:, 2, :], in_=PR.banks[6][:, 0:128]), writes=[vob[s_], bankb[6]])
            kb.pool.dma(v_tok[r0:r0 + 128, :, :], vo[s_][:], reads=[vob[s_]], writes=[vtb], slot=vob[s_])
            yield
            kb.act.op(lambda e: e.activation(out=zo[s_][:], in_=PR.banks[6][:, 128:384], func=AF.Silu), writes=[zob[s_], bankb[6]])
            kb.act.op(lambda e: e.copy(out=GB[:, tile, :], in_=PR.banks[6][:, 384:392]), writes=[GBb, bankb[6]])
            kb.pool.dma(zs_tok[r0:r0 + 128, :], zo[s_][:], reads=[zob[s_]], writes=[zsb], slot=zob[s_])
            kb.dve.op(lambda e: e.tensor_tensor(out=tsq[:], in0=tq[s_][:], in1=tq[s_][:], op=ALU.mult), reads=[tqb[s_]],
                      writes=[tsqb])
            kb.dve.op(lambda e: e.tensor_reduce(out=rq[s_][:], in_=tsq[:], axis=AX.X, op=ALU.add), reads=[tsqb],
                      writes=[rqb[s_]])
            yield
            kb.act.op(lambda e: e.activation(out=rq[s_][:], in_=rq[s_][:], func=AF.Sqrt, scale=1.0 / 128, bias=epsc[:]),
                      reads=[rqb[s_], cstb], writes=[rqb[s_]])
            yield
            kb.dve.op(lambda e: e.reciprocal(out=rq[s_][:], in_=rq[s_][:]), reads=[rqb[s_]], writes=[rqb[s_]])
            kb.dve.op(lambda e: e.tensor_tensor(out=tq[s_][:], in0=tq[s_][:],
                                                in1=rq[s_][:, :, None].to_broadcast([128, 6, 128]), op=ALU.mult),
                      reads=[tqb[s_], rqb[s_]], writes=[tqb[s_]])
            kb.dve.op(lambda e: e.tensor_tensor(out=tq[s_][:], in0=tq[s_][:], in1=gqk[:], op=ALU.mult),
                      reads=[tqb[s_], pvb], writes=[tqb[s_]])
            yield
            x1 = tq[s_][:, :, 0:16]
            x2 = tq[s_][:, :, 16:32]
            cos_ = cs[:, 1, tile:tile + 1, :].to_broadcast([128, 6, 16])
            sin_ = cs[:, 0, tile:tile + 1, :].to_broadcast([128, 6, 16])
            kb.dve.op(lambda e: e.tensor_tensor(out=rt[:, 0], in0=x1, in1=cos_, op=ALU.mult), reads=[tqb[s_], rpb], writes=[rtb])
            kb.dve.op(lambda e: e.tensor_tensor(out=rt[:, 1], in0=x2, in1=sin_, op=ALU.mult), reads=[tqb[s_], rpb], writes=[rtb])
            kb.dve.op(lambda e: e.tensor_tensor(out=rt[:, 2], in0=x2, in1=cos_, op=ALU.mult), reads=[tqb[s_], rpb], writes=[rtb])
            kb.dve.op(lambda e: e.tensor_tensor(out=rt[:, 3], in0=x1, in1=sin_, op=ALU.mult), reads=[tqb[s_], rpb], writes=[rtb])
            kb.dve.op(lambda e: e.tensor_tensor(out=tq[s_][:, :, 0:16], in0=rt[:, 0], in1=rt[:, 1], op=ALU.subtract),
                      reads=[rtb, tqb[s_]], writes=[tqb[s_]])
            kb.dve.op(lambda e: e.tensor_tensor(out=tq[s_][:, :, 16:32], in0=rt[:, 2], in1=rt[:, 3], op=ALU.add),
                      reads=[rtb, tqb[s_]], writes=[tqb[s_]])
            yield
            kb.act.op(lambda e: e.copy(out=qko[s_][:], in_=tq[s_][:]), reads=[tqb[s_]], writes=[qkob[s_]])
            kb.pool.dma(qk_tok[r0:r0 + 128, :, :], qko[s_][:], reads=[qkob[s_]], writes=[qkb], slot=qkob[s_])
            yield

    def run_rr(gens):
        while gens:
            nxt = []
            for gnr in gens:
                try:
                    next(gnr)
                    nxt.append(gnr)
                except StopIteration:
                    pass
            gens = nxt

    run_rr([prep_h(0)])
    for g in range(NG):
        gens = [fm_stream(g), tok_stream(g)]
        if g + 1 < NG:
            gens.append(prep_h(g + 1))
        run_rr(gens)
    if 'nogb' in DBG:
        barrier(kb)
        kb.pop()
        return
    ab = kb.sbuf("ab", [128, 2, 2], F32)
    abb = Buf("ab")
    kb.sp.dma(ab[:, 0, :], a["a_log"].partition_broadcast(128), writes=[abb], slot=abb)
    kb.sp.dma(ab[:, 1, :], a["dt_bias"].partition_broadcast(128), writes=[abb], slot=abb)
    kb.act.op(lambda e: e.activation(out=ab[:, 0, :], in_=ab[:, 0, :], func=AF.Exp), reads=[abb], writes=[abb])
    spt = kb.sbuf("spt", [128, 3, NT, 2], F32)
    sptb = Buf("spt")
    kb.act.op(lambda e: e.activation(out=GB[:, :, 0:2], in_=GB[:, :, 0:2], func=AF.Sigmoid), reads=[GBb], writes=[GBb])
    for h in range(2):
        kb.dve.op(lambda e: e.tensor_scalar(out=spt[:, 0, :, h], in0=GB[:, :, 2 + h], scalar1=ab[:, 1, h:h + 1],
                                            scalar2=None, op0=ALU.add), reads=[GBb, abb], writes=[sptb])
    kb.act.op(lambda e: e.activation(out=spt[:, 1], in_=spt[:, 0], func=AF.Abs), reads=[sptb], writes=[sptb])
    kb.act.op(lambda e: e.activation(out=spt[:, 1], in_=spt[:, 1], func=AF.Exp, scale=-1.0), reads=[sptb], writes=[sptb])
    kb.act.op(lambda e: e.activation(out=spt[:, 1], in_=spt[:, 1], func=AF.Ln, bias=onec[:]), reads=[sptb, cstb],
              writes=[sptb])
    kb.dve.op(lambda e: e.tensor_scalar(out=spt[:, 0], in0=spt[:, 0], scalar1=0.0, scalar2=None, op0=ALU.max),
              reads=[sptb], writes=[sptb])
    kb.dve.op(lambda e: e.tensor_tensor(out=spt[:, 0], in0=spt[:, 0], in1=spt[:, 1], op=ALU.add), reads=[sptb],
              writes=[sptb])
    for h in range(2):
        kb.dve.op(lambda e: e.tensor_scalar(out=GB[:, :, 2 + h], in0=spt[:, 0, :, h], scalar1=ab[:, 0, h:h + 1],
                                            scalar2=-1.0, op0=ALU.mult, op1=ALU.mult), reads=[sptb, abb, GBb],
                  writes=[GBb])
    barrier(kb)
    kb.pop()
    if "B" not in phases:
        return

    kb.push()
    SC = float(128 ** -0.5)
    qt_ = kb.sbuf("qtok", [128, 16, 128], BF16)
    kt_ = kb.sbuf("ktok", [128, 16, 128], BF16)
    qtb, ktb = Buf("qtok"), Buf("ktok")
    QT = kb.sbuf("QT", [128, 16, 128], BF16)
    QTb = Buf("QT")
    KT = [[kb.sbuf(f"KT{g}_{i}", [128, 16, 128], BF16) for i in range(2)] for g in range(3)]
    KTb = [[Buf(f"KT{g}_{i}") for i in range(2)] for g in range(3)]
    VV = [[kb.sbuf(f"VV{g}_{i}", [128, 16, 128], BF16) for i in range(2)] for g in range(3)]
    VVb = [[Buf(f"VV{g}_{i}") for i in range(2)] for g in range(3)]
    ND = kb.sbuf("ND", [128, 2, 2048], F32)
    NDb = Buf("ND")
    ex = [kb.sbuf(f"ex{i}", [128, 256], BF16) for i in range(2)]
    exb = [Buf(f"ex{i}") for i in range(2)]
    pT = [kb.sbuf(f"pT{i}", [128, 256], BF16) for i in range(2)]
    pTb = [Buf(f"pT{i}") for i in range(2)]
    oT = kb.sbuf("oT", [128, 2048], BF16)
    oTb = Buf("oT")
    ptr = [PR.banks[0][:].bitcast(BF16), PR.banks[1][:].bitcast(BF16)]
    blk = 0

    def load_blocks(dst, dstb, src3, n, g, slot):
        u0 = n * 2048
        if g == 0:
            kb.sp.dma(dst[:], src3[u0:u0 + 2048, slot, :].rearrange("(b i) d -> i b d", i=128), reads=[qkb, vtb],
                      writes=[dstb], slot=dstb)
        elif g == 1:
            for m in range(4):
                kb.sp.dma(dst[:, m * 4:(m + 1) * 4, :],
                          src3[u0 + m * 512:u0 + (m + 1) * 512, slot, :].rearrange("(i r) d -> i r d", r=4),
                          reads=[qkb, vtb], writes=[dstb], slot=dstb)
        else:
            kb.sp.dma(dst[:], src3[u0:u0 + 2048, slot, :].rearrange("(i r) d -> i r d", r=16), reads=[qkb, vtb],
                      writes=[dstb], slot=dstb)

    def nd_view(g, bi):
        if g == 0:
            return ND[:, :, bi * 128:(bi + 1) * 128]
        if g == 1:
            m, r = bi // 4, bi % 4
            return ND[:, :, m * 512:(m + 1) * 512].rearrange("p a (i r) -> p a i r", r=4)[:, :, :, r]
        return ND[:, :, :].rearrange("p a (i r) -> p a i r", r=16)[:, :, :, bi]

    for n in range(NU):
        cur = n % 2
        for g in range(3):
            load_blocks(qt_, qtb, qk_tok, n, g, g)
            load_blocks(kt_, ktb, qk_tok, n, g, 3 + g)
            load_blocks(VV[g][cur], VVb[g][cur], v_tok, n, g, g)
            for (src, srcb, dst, dstb) in ((qt_, qtb, QT, QTb), (kt_, ktb, KT[g][cur], KTb[g][cur])):
                for half in range(2):
                    for kk in range(8):
                        kb.pe.op(lambda e: e.transpose(ptr[half][:, kk * 128:(kk + 1) * 128], src[:, half * 8 + kk, :],
                                                       identbf[:]), reads=[srcb, identb32], writes=[bankb[half]],
                                 signal=(kk == 7))
                    if half == 0:
                        kb.dve.op(lambda e: e.tensor_copy(out=dst[:, 0:8, :], in_=ptr[half][:]), reads=[bankb[half]],
                                  writes=[dstb])
                    else:
                        kb.act.op(lambda e: e.copy(out=dst[:, 8:16, :], in_=ptr[half][:]), reads=[bankb[half]],
                                  writes=[dstb])
            for bi in range(16):
                if g == 0:
                    pb_, pu_ = (bi - 1, cur) if bi > 0 else (15, 1 - cur)
                    has_prev = not (n == 0 and bi == 0)
                elif g == 1:
                    pb_, pu_ = (bi - 4, cur) if bi >= 4 else (12 + bi, 1 - cur)
                    has_prev = not (n == 0 and bi < 4)
                else:
                    pb_, pu_ = bi, 1 - cur
                    has_prev = n > 0
                s_ = blk % 2
                blk += 1
                bs = 2 + s_
                bo = 4 + s_
                c0 = 0 if has_prev else 128
                if has_prev:
                    kb.pe.op(lambda e: e.matmul(PR.banks[bs][:, 0:128], lhsT=KT[g][pu_][:, pb_, :], rhs=QT[:, bi, :],
                                                start=True, stop=True), reads=[KTb[g][pu_], QTb], writes=[bankb[bs]],
                             signal=False)
                kb.pe.op(lambda e: e.matmul(PR.banks[bs][:, 128:256], lhsT=KT[g][cur][:, bi, :], rhs=QT[:, bi, :],
                                            start=True, stop=True), reads=[KTb[g][cur], QTb], writes=[bankb[bs]])
                kb.act.op(lambda e: e.activation(out=ex[s_][:, c0:256], in_=PR.banks[bs][:, c0:256], func=AF.Exp, scale=SC),
                          reads=[bankb[bs]], writes=[exb[s_]])
                kb.dve.op(lambda e: e.tensor_tensor(out=pT[s_][:, c0:256], in0=ex[s_][:, c0:256], in1=maskbf[:, c0:256],
                                                    op=ALU.mult), reads=[exb[s_], cstb], writes=[pTb[s_]])
                if has_prev:
                    kb.pe.op(lambda e: e.matmul(PR.banks[bo][:, 0:128], lhsT=VV[g][pu_][:, pb_, :], rhs=pT[s_][:, 0:128],
                                                start=True, stop=False), reads=[VVb[g][pu_], pTb[s_]],
                             writes=[bankb[bo]], signal=False)
                kb.pe.op(lambda e: e.matmul(PR.banks[bo][:, 0:128], lhsT=VV[g][cur][:, bi, :], rhs=pT[s_][:, 128:256],
                                            start=(not has_prev), stop=True), reads=[VVb[g][cur], pTb[s_]],
                         writes=[bankb[bo]], signal=False)
                if has_prev:
                    kb.pe.op(lambda e: e.matmul(PR.banks[bo][:, 128:256], lhsT=onesbf[:], rhs=pT[s_][:, 0:128],
                                                start=True, stop=False), reads=[cstb, pTb[s_]], writes=[bankb[bo]],
                             signal=False)
                kb.pe.op(lambda e: e.matmul(PR.banks[bo][:, 128:256], lhsT=onesbf[:], rhs=pT[s_][:, 128:256],
                                            start=(not has_prev), stop=True), reads=[cstb, pTb[s_]], writes=[bankb[bo]])
                src = PR.banks[bo][:, 0:256].rearrange("p (a q) -> p a q", a=2)
                if g == 0:
                    kb.act.op(lambda e: e.copy(out=nd_view(g, bi), in_=src), reads=[bankb[bo]], writes=[NDb])
                else:
                    kb.dve.op(lambda e: e.tensor_tensor(out=nd_view(g, bi), in0=nd_view(g, bi), in1=src, op=ALU.add),
                              reads=[bankb[bo], NDb], writes=[NDb])
        kb.dve.op(lambda e: e.reciprocal(out=ND[:, 1, :], in_=ND[:, 1, :]), reads=[NDb], writes=[NDb])
        kb.dve.op(lambda e: e.tensor_tensor(out=oT[:], in0=ND[:, 0, :], in1=ND[:, 1, :], op=ALU.mult), reads=[NDb],
                  writes=[oTb])
        kb.sp.dma(a["o_aT"][:, n * 2048:(n + 1) * 2048], oT[:], reads=[oTb], writes=[a["o_aT_buf"]], slot=oTb)
    barrier(kb)
    kb.pop()
    if "C" not in phases:
        return

    kb.push()
    gnb = kb.sbuf("gnb", [128, 128], F32)
    gnbb = Buf("gnb")
    kb.sp.dma(gnb[:], a["gdn_norm_g"].partition_broadcast(128), writes=[gnbb], slot=gnbb)
    St = [kb.sbuf(f"St{h}", [128, 128], F32) for h in range(2)]
    Stb = [Buf(f"St{h}") for h in range(2)]
    for h in range(2):
        kb.pool.op(lambda e: e.memset(St[h][:], 0.0), writes=[Stb[h]])

    class HS:
        pass

    def cmat(i):
        return cst[:, i, :]

    PSET = {}
    for h in range(2):
        for par in range(2):
            o = HS()
            n = f"{h}{par}"
            o.qkv = kb.sbuf(f"qkv{n}", [128, 3, 128], F32); o.qkvb = Buf(f"qkv{n}")
            o.gu = kb.sbuf(f"gu{n}", [128, 2, 128], F32); o.gub = Buf(f"gu{n}")
            o.ex = kb.sbuf(f"exc{n}", [128, 512], F32); o.exb = Buf(f"exc{n}")
            o.t1 = kb.sbuf(f"t1{n}", [128, 128], F32); o.t1b = Buf(f"t1{n}")
            o.dT = kb.sbuf(f"dT{n}", [128, 128], F32); o.dTb = Buf(f"dT{n}")
            o.qkm = kb.sbuf(f"qkm{n}", [128, 128], F32); o.qkmb = Buf(f"qkm{n}")
            o.bege = kb.sbuf(f"bege{n}", [128, 8], F32); o.begeb = Buf(f"bege{n}")
            o.y = kb.sbuf(f"y{n}", [128, 256], F32); o.yb = Buf(f"y{n}")
            o.kdec = kb.sbuf(f"kdec{n}", [128, 128], F32); o.kdecb = Buf(f"kdec{n}")
            o.qd = kb.sbuf(f"qd{n}", [128, 128], F32); o.qdb = Buf(f"qd{n}")
            o.pw = kb.sbuf(f"pw{n}", [128, 12, 128], F32); o.pwb = [Buf(f"pw{n}_{i}") for i in range(12)]
            o.wT = kb.sbuf(f"wT{n}", [128, 128], F32); o.wTb = Buf(f"wT{n}")
            o.gcol = kb.sbuf(f"gcol{n}", [128, 8], F32); o.gcolb = Buf(f"gcol{n}")
            o.zt = kb.sbuf(f"zt{n}", [128, 128], F32); o.ztb = Buf(f"zt{n}")
            bx, by = 2 * h, 2 * h + 1
            o.ra, o.rG = PR.reg(bx, 0, 2), PR.reg(bx, 2, 2)
            o.rTa, o.rTw = PR.reg(bx, 0, 1), PR.reg(bx, 1, 1)
            o.bxb = bankb[bx]
            o.rPD = PR.reg(by, 0, 4)
            o.rA, o.rB, o.rY = PR.reg(by, 0, 1), PR.reg(by, 1, 1), PR.reg(by, 2, 2)
            o.byb = bankb[by]
            PSET[(h, par)] = o
    SSET = []
    for h in range(2):
        o = HS()
        o.vn = kb.sbuf(f"vn{h}", [128, 128], F32); o.vnb = Buf(f"vn{h}")
        o.O = kb.sbuf(f"O{h}", [128, 128], F32); o.Ob = Buf(f"O{h}")
        o.junk = kb.sbuf(f"junk{h}", [128, 128], F32)
        o.ss = kb.sbuf(f"ssn{h}", [128, 8], F32); o.ssb = Buf(f"ssn{h}")
        o.on = kb.sbuf(f"on{h}", [128, 128], BF16); o.onb = Buf(f"on{h}")
        o.od = [kb.sbuf(f"od{h}_{i}", [128, 512], BF16) for i in range(2)]
        o.odb = [Buf(f"od{h}_{i}") for i in range(2)]
        bk = 4 + h
        o.rP1, o.rPO, o.rPS = PR.reg(bk, 0, 1), PR.reg(bk, 1, 1), PR.reg(bk, 2, 1)
        o.trb = PR.banks[bk][:].bitcast(BF16)[:, 768:1024]
        o.bb = bankb[bk]
        SSET.append(o)

    def prep(t, h):
        o = PSET[(h, t % 2)]
        c0 = t * 128
        bcol = GB[:, t, h:h + 1]
        for i, ch in enumerate((h, 2 + h, 4 + h)):
            kb.sp.dma(o.qkv[:, i, :], gT[ch, :, c0:c0 + 128], reads=[gTb], writes=[o.qkvb], slot=o.qkvb)
        kb.sp.dma(o.zt[:], zs_tok[c0:c0 + 128, h * 128:(h + 1) * 128], reads=[zsb], writes=[o.ztb], slot=o.ztb)
        QTt, KTt, VTt = o.qkv[:, 0, :], o.qkv[:, 1, :], o.qkv[:, 2, :]
        kb.pool.op(lambda e: e.tensor_copy(out=o.gcol[:, 0:1], in_=GB[:, t, 2 + h:3 + h]), reads=[GBb], writes=[o.gcolb])
        gcol = o.gcol[:, 0:1]
        kb.pe.op(lambda e: e.transpose(o.ra[:, 0:128], KTt, ident32[:]), reads=[o.qkvb, identb32], writes=[o.bxb], signal=False)
        kb.pe.op(lambda e: e.transpose(o.ra[:, 128:256], VTt, ident32[:]), reads=[o.qkvb, identb32], writes=[o.bxb])
        kb.dve.op(lambda e: e.tensor_scalar(out=o.gu[:, 0, :], in0=cmat(C_U2), scalar1=gcol, scalar2=None, op0=ALU.mult),
                  reads=[cstb, o.gcolb], writes=[o.gub])
        kb.dve.op(lambda e: e.tensor_scalar(out=o.gu[:, 1, :], in0=cmat(C_SL2), scalar1=gcol, scalar2=None, op0=ALU.mult),
                  reads=[cstb, o.gcolb], writes=[o.gub])
        yield
        rPD = o.rPD
        kb.pe.op(lambda e: e.matmul(rPD[:, 0:128], lhsT=o.gu[:, 0, :], rhs=cmat(C_SL2), start=True, stop=True),
                 reads=[o.gub, cstb], writes=[o.byb], signal=False)
        kb.pe.op(lambda e: e.matmul(rPD[:, 128:160], lhsT=o.gu[:, 0, :], rhs=cst[:, C_ONE, 0:32], start=True, stop=True),
                 reads=[o.gub, cstb], writes=[o.byb], signal=False)
        kb.pe.op(lambda e: e.matmul(rPD[:, 160:192], lhsT=o.gu[:, 1, :], rhs=cst[:, C_ONE, 0:32], start=True, stop=True),
                 reads=[o.gub, cstb], writes=[o.byb], signal=False)
        kb.pe.op(lambda e: e.matmul(rPD[:, 256:384], lhsT=cmat(C_SL2), rhs=o.gu[:, 0, :], start=True, stop=True),
                 reads=[o.gub, cstb], writes=[o.byb], signal=False)
        kb.pe.op(lambda e: e.matmul(rPD[:, 384:512], lhsT=cmat(C_ONE), rhs=o.gu[:, 0, :], start=True, stop=True),
                 reads=[o.gub, cstb], writes=[o.byb])
        kb.pe.op(lambda e: e.matmul(o.rG[:, 0:128], lhsT=KTt, rhs=KTt, start=True, stop=True), reads=[o.qkvb],
                 writes=[o.bxb], signal=False)
        kb.pe.op(lambda e: e.matmul(o.rG[:, 128:256], lhsT=KTt, rhs=QTt, start=True, stop=True), reads=[o.qkvb],
                 writes=[o.bxb])
        yield
        kb.act.op(lambda e: e.activation(out=o.ex[:, 0:192], in_=rPD[:, 0:192], func=AF.Exp), reads=[o.byb], writes=[o.exb])
        kb.act.op(lambda e: e.activation(out=o.ex[:, 256:512], in_=rPD[:, 256:512], func=AF.Exp), reads=[o.byb], writes=[o.exb])
        yield
        A0, B0 = o.pw[:, 0, :], o.pw[:, 1, :]
        kb.dve.op(lambda e: e.tensor_tensor(out=o.t1[:], in0=o.ex[:, 0:128], in1=cmat(C_SL2), op=ALU.mult),
                  reads=[o.exb, cstb], writes=[o.t1b])
        kb.dve.op(lambda e: e.scalar_tensor_tensor(out=A0, in0=o.rG[:, 0:128], scalar=bcol, in1=o.t1[:], op0=ALU.mult,
                                                   op1=ALU.mult), reads=[o.bxb, GBb, o.t1b], writes=[o.pwb[0]])
        kb.pool.op(lambda e: e.tensor_tensor(out=o.dT[:], in0=o.ex[:, 256:384], in1=cmat(C_U2), op=ALU.mult),
                   reads=[o.exb, cstb], writes=[o.dTb])
        kb.pool.op(lambda e: e.tensor_tensor(out=o.bege[:, 0:1], in0=bcol, in1=o.ex[:, 128:129], op=ALU.mult),
                   reads=[GBb, o.exb], writes=[o.begeb])
        kb.pool.op(lambda e: e.tensor_tensor(out=o.qd[:], in0=QTt, in1=o.ex[:, 384:512], op=ALU.mult),
                   reads=[o.qkvb, o.exb], writes=[o.qdb])
        yield
        kb.dve.op(lambda e: e.tensor_tensor(out=o.qkm[:], in0=o.rG[:, 128:256], in1=o.dT[:], op=ALU.mult),
                  reads=[o.bxb, o.dTb], writes=[o.qkmb])
        kb.dve.op(lambda e: e.tensor_scalar(out=o.y[:, 0:128], in0=o.ra[:, 128:256], scalar1=bcol, scalar2=None,
                                            op0=ALU.mult), reads=[o.bxb, GBb], writes=[o.yb])
        kb.dve.op(lambda e: e.tensor_scalar(out=o.y[:, 128:256], in0=o.ra[:, 0:128], scalar1=o.bege[:, 0:1], scalar2=None,
                                            op0=ALU.mult), reads=[o.bxb, o.begeb], writes=[o.yb])
        kb.dve.op(lambda e: e.tensor_scalar(out=o.kdec[:], in0=o.ra[:, 0:128], scalar1=o.ex[:, 160:161], scalar2=None,
                                            op0=ALU.mult), reads=[o.bxb, o.exb], writes=[o.kdecb])
        yield
        kb.pe.op(lambda e: e.transpose(o.rTa[:], A0, ident32[:]), reads=[o.pwb[0], identb32], writes=[o.bxb])
        yield
        kb.act.op(lambda e: e.copy(out=B0, in_=o.rTa[:]), reads=[o.bxb], writes=[o.pwb[1]])
        yield
        for l in range(1, 6):
            Ap, Bp = o.pw[:, 2 * (l - 1), :], o.pw[:, 2 * (l - 1) + 1, :]
            Apb, Bpb = o.pwb[2 * (l - 1)], o.pwb[2 * (l - 1) + 1]
            if l < 5:
                kb.pe.op(lambda e: e.matmul(o.rA[:], lhsT=Bp, rhs=Ap, start=True, stop=True), reads=[Apb, Bpb],
                         writes=[o.byb], signal=False)
            kb.pe.op(lambda e: e.matmul(o.rB[:], lhsT=Ap, rhs=Bp, start=True, stop=True), reads=[Apb, Bpb],
                     writes=[o.byb])
            yield
            if l < 5:
                kb.act.op(lambda e: e.copy(out=o.pw[:, 2 * l, :], in_=o.rA[:]), reads=[o.byb], writes=[o.pwb[2 * l]])
            kb.act.op(lambda e: e.copy(out=o.pw[:, 2 * l + 1, :], in_=o.rB[:]), reads=[o.byb], writes=[o.pwb[2 * l + 1]])
            yield
        for l in (5, 4, 3, 2, 1, 0):
            kb.pe.op(lambda e: e.matmul(o.rY[:], lhsT=o.pw[:, 2 * l + 1, :], rhs=o.y[:], start=True, stop=True),
                     reads=[o.pwb[2 * l + 1], o.yb], writes=[o.byb])
            yield
            kb.dve.op(lambda e: e.tensor_tensor(out=o.y[:], in0=o.y[:], in1=o.rY[:],
                                                op=(ALU.add if l > 0 else ALU.subtract)), reads=[o.byb, o.yb],
                      writes=[o.yb])
            yield
        kb.pe.op(lambda e: e.transpose(o.rTw[:], o.y[:, 128:256], ident32[:]), reads=[o.yb, identb32], writes=[o.bxb])
        yield
        kb.act.op(lambda e: e.copy(out=o.wT[:], in_=o.rTw[:]), reads=[o.bxb], writes=[o.wTb])
        kb.pool.op(lambda e: e.tensor_tensor(out=o.zt[:], in0=o.zt[:], in1=gnb[:], op=ALU.mult), reads=[o.ztb, gnbb],
                   writes=[o.ztb])
        yield

    def scan(t, h):
        o = PSET[(h, t % 2)]
        sc = SSET[h]
        for X, (lo, hi) in enumerate(((0, 64), (64, 128))):
            kb.pe.op(lambda e: e.matmul(sc.rP1[:], lhsT=o.wT[:], rhs=St[h][:], start=True, stop=True),
                     reads=[o.wTb, Stb[h]], writes=[sc.bb])
            yield
            kb.dve.op(lambda e: e.tensor_tensor(out=sc.vn[lo:hi, :], in0=o.y[lo:hi, 0:128], in1=sc.rP1[lo:hi, :],
                                                op=ALU.subtract), reads=[sc.bb, o.yb], writes=[sc.vnb])
            yield
            kb.pe.op(lambda e: e.matmul(sc.rPS[:], lhsT=o.kdec[lo:hi, :], rhs=sc.vn[lo:hi, :], start=True, stop=True),
                     reads=[o.kdecb, sc.vnb], writes=[sc.bb], signal=False)
            kb.pe.op(lambda e: e.matmul(sc.rPO[:], lhsT=o.qd[:], rhs=St[h][:], start=True, stop=False),
                     reads=[o.qdb, Stb[h]], writes=[sc.bb], signal=False)
            kb.pe.op(lambda e: e.matmul(sc.rPO[:], lhsT=o.qkm[lo:hi, :], rhs=sc.vn[lo:hi, :], start=False, stop=True),
                     reads=[o.qkmb, sc.vnb], writes=[sc.bb])
            yield
            kb.dve.op(lambda e: e.scalar_tensor_tensor(out=St[h][:], in0=St[h][:], scalar=o.ex[:, 447 + 64 * X:448 + 64 * X],
                                                       in1=sc.rPS[:], op0=ALU.mult, op1=ALU.add),
                      reads=[sc.bb, o.exb, Stb[h]], writes=[Stb[h]])
            kb.act.op(lambda e: e.copy(out=sc.O[lo:hi, :], in_=sc.rPO[lo:hi, :]), reads=[sc.bb], writes=[sc.Ob])
            yield
        kb.act.op(lambda e: e.activation(out=sc.junk[:], in_=sc.O[:], func=AF.Square, accum_out=sc.ss[:, 0:1]), reads=[sc.Ob],
                  writes=[sc.ssb])
        kb.act.op(lambda e: e.activation(out=sc.ss[:, 0:1], in_=sc.ss[:, 0:1], func=AF.Sqrt, scale=1.0 / 128, bias=epsc[:]),
                  reads=[sc.ssb, cstb], writes=[sc.ssb])
        yield
        kb.dve.op(lambda e: e.reciprocal(out=sc.ss[:, 0:1], in_=sc.ss[:, 0:1]), reads=[sc.ssb], writes=[sc.ssb])
        kb.dve.op(lambda e: e.scalar_tensor_tensor(out=sc.on[:], in0=sc.O[:], scalar=sc.ss[:, 0:1], in1=o.zt[:],
                                                   op0=ALU.mult, op1=ALU.mult), reads=[sc.Ob, sc.ssb, o.ztb],
                  writes=[sc.onb])
        yield
        kb.pe.op(lambda e: e.transpose(sc.trb[:, 0:128], sc.on[:], identbf[:]), reads=[sc.onb, identb32], writes=[sc.bb])
        yield
        jj = t % 4
        q4 = (t // 4) % 2
        kb.act.op(lambda e: e.copy(out=sc.od[q4][:, jj * 128:(jj + 1) * 128], in_=sc.trb[:, 0:128]), reads=[sc.bb],
                  writes=[sc.odb[q4]])
        if jj == 3:
            kb.sp.dma(a["o_dT"][h * 128:(h + 1) * 128, (t - 3) * 128:(t + 1) * 128], sc.od[q4][:], reads=[sc.odb[q4]],
                      writes=[a["o_dT_buf"]], slot=sc.odb[q4])
        yield

    for step in range(NT + 1):
        gens = []
        if step >= 1:
            gens += [scan(step - 1, 0), scan(step - 1, 1)]
        if step < NT:
            gens += [prep(step, 0), prep(step, 1)]
        while gens:
            nxt = []
            for gnr in gens:
                try:
                    next(gnr)
                    nxt.append(gnr)
                except StopIteration:
                    pass
            gens = nxt
    kb.pop()


def build_p1(S, phases="ABC"):
    kb = KB()
    a = {}
    a["xnT"] = kb.din("xnT", [D, S], BF16)
    a["mod"] = kb.din("mod", [2, 3 * D], F32)
    a["g_mix"] = kb.din("g_mix", [D], F32)
    a["pos"] = kb.din("pos", [S], I32)
    a["w_tok"] = kb.din("w_tok", [D, NTOKC], F32)
    a["w_fm"] = kb.din("w_fm", [D, 768], F32)
    a["q_norm_g"] = kb.din("q_norm_g", [128], F32)
    a["k_norm_g"] = kb.din("k_norm_g", [128], F32)
    a["gconv_w"] = kb.din("gconv_w", [4, 768], F32)
    a["a_log"] = kb.din("a_log", [2], F32)
    a["dt_bias"] = kb.din("dt_bias", [2], F32)
    a["gdn_norm_g"] = kb.din("gdn_norm_g", [128], F32)
    a["consts"] = kb.din("consts", [128, 9, 128], F32)
    a["o_aT"] = kb.dout("o_aT", [128, S], BF16)
    a["o_dT"] = kb.dout("o_dT", [256, S], BF16)
    for n in ["xnT_buf", "o_aT_buf", "o_dT_buf", "dbg_buf"]:
        a[n] = Buf(n)
    if 'dump' in DBG:
        a["dbg"] = kb.dout("dbg", [2, 4, 128, 128], F32)
    emit_p1(kb, S, a, phases)
    return kb.finish([a["o_aT_buf"], a["o_dT_buf"], a["dbg_buf"]])


SEQ = 16384
NB = 2
NCORE = 8
TPC = SEQ * NB // NCORE
O_Q, O_K, O_V, O_GQ, O_GK, O_GV, O_BETA, O_ALPHA, O_Z, O_UC, O_GL = (
    0, 1536, 3072, 4608, 5632, 6656, 7680, 7688, 7696, 8720, 10768)
_PROGS = {}


def _prog(name, builder, *args):
    key = (name,) + args
    if key not in _PROGS:
        _PROGS[key] = builder(*args)
    return _PROGS[key]


def _c(a):
    return np.ascontiguousarray(a)


def _run(nc, in_maps):
    res = run_bass_kernel_spmd(nc, in_maps, core_ids=list(range(NCORE)))
    return res.results


def kernel(x, c, positions, mix_mod_w, mix_mod_b, mix_norm_g, w_in, q_norm_g, k_norm_g, w_attn_o, gdn_conv_w,
           gdn_a_log, gdn_dt_bias, gdn_norm_g, w_gdn_o, conv_dw_w, conv_dw_b, conv_ln_g, conv_ln_b, w_conv_o, w_out,
           ffn_mod_w, ffn_mod_b, ffn_norm_g, w_gate_up, w_down):
    f32 = np.float32
    x = np.asarray(x, f32)
    c = np.asarray(c, f32)
    positions = np.asarray(positions, np.int32)
    cores = [(i // 4, i % 4) for i in range(NCORE)]
    consts = host_consts()

    p0 = _prog("p0", build_p0, TPC)
    mats = [mix_mod_w[0], ffn_mod_w[0], mix_mod_w[1], ffn_mod_w[1]]
    biases = [mix_mod_b[0], ffn_mod_b[0], mix_mod_b[1], ffn_mod_b[1]]
    in_maps = []
    for i, (b, j) in enumerate(cores):
        in_maps.append({
            "x": _c(x[b, j * TPC:(j + 1) * TPC]),
            "c": _c(c),
            "wm": _c(np.stack([np.asarray(m, f32)[:, 768 * i:768 * (i + 1)] for m in mats])),
            "bm": _c(np.stack([np.asarray(v, f32)[768 * i:768 * (i + 1)] for v in biases])),
        })
    r0 = _run(p0, in_maps)
    mod_all = np.concatenate([r0[i]["modo"] for i in range(NCORE)], axis=-1)
    xnT_full = [np.concatenate([r0[b * 4 + j]["xnT"] for j in range(4)], axis=1) for b in range(NB)]
    x_cur = [x[b] for b in range(NB)]

    p1 = _prog("p1", build_p1, SEQ)
    p2 = _prog("p2", build_p2, TPC)
    for l in range(2):
        win = np.asarray(w_in[l], f32)
        gcw = np.asarray(gdn_conv_w[l], f32)
        in_maps = []
        for i, (b, j) in enumerate(cores):
            heads_a = [(g * 4 + j) * 128 for g in range(3)]
            hd = [2 * j, 2 * j + 1]
            cols = ([O_Q + o for o in heads_a] + [O_K + o for o in heads_a] + [O_V + o for o in heads_a]
                    + [O_Z + h * 128 for h in hd])
            w_tok = np.concatenate([win[:, o:o + 128] for o in cols]
                                   + [win[:, O_BETA + hd[0]:O_BETA + hd[0] + 2], win[:, O_ALPHA + hd[0]:O_ALPHA + hd[0] + 2]],
                                   axis=1)
            fcols = [O_GQ + h * 128 for h in hd] + [O_GK + h * 128 for h in hd] + [O_GV + h * 128 for h in hd]
            w_fm = np.concatenate([win[:, o:o + 128] for o in fcols], axis=1)
            gconv = np.concatenate([gcw[:, o - O_GQ:o - O_GQ + 128] for o in fcols], axis=1)
            in_maps.append({
                "xnT": _c(xnT_full[b]),
                "mod": _c(np.stack([mod_all[2 * l, b], mod_all[2 * l + 1, b]])),
                "g_mix": _c(np.asarray(mix_norm_g[l], f32)),
                "pos": _c(positions[b]),
                "w_tok": _c(w_tok), "w_fm": _c(w_fm),
                "q_norm_g": _c(np.asarray(q_norm_g[l], f32)), "k_norm_g": _c(np.asarray(k_norm_g[l], f32)),
                "gconv_w": _c(gconv),
                "a_log": _c(np.asarray(gdn_a_log[l], f32)[hd[0]:hd[0] + 2]),
                "dt_bias": _c(np.asarray(gdn_dt_bias[l], f32)[hd[0]:hd[0] + 2]),
                "gdn_norm_g": _c(np.asarray(gdn_norm_g[l], f32)),
                "consts": consts,
            })
        r1 = _run(p1, in_maps)
        oaT = [np.concatenate([r1[b * 4 + j]["o_aT"] for j in range(4)], axis=0) for b in range(NB)]
        odT = [np.concatenate([r1[b * 4 + j]["o_dT"] for j in range(4)], axis=0) for b in range(NB)]
        in_maps = []
        shared = {
            "g_mix": _c(np.asarray(mix_norm_g[l], f32)), "g_ffn": _c(np.asarray(ffn_norm_g[l], f32)),
            "w_uc": _c(win[:, O_UC:O_UC + 2048]), "w_gl": _c(win[:, O_GL:O_GL + 3 * D]),
            "w_ao": _c(np.asarray(w_attn_o[l], f32)), "w_go": _c(np.asarray(w_gdn_o[l], f32)),
            "w_co": _c(np.asarray(w_conv_o[l], f32)), "w_out": _c(np.asarray(w_out[l], f32)),
            "w_gu": _c(np.asarray(w_gate_up[l], f32)), "w_dn": _c(np.asarray(w_down[l], f32)),
            "conv_w": _c(np.asarray(conv_dw_w[l], f32)), "conv_b": _c(np.asarray(conv_dw_b[l], f32)),
            "ln_g": _c(np.asarray(conv_ln_g[l], f32)), "ln_b": _c(np.asarray(conv_ln_b[l], f32)),
        }
        for i, (b, j) in enumerate(cores):
            t0 = j * TPC
            xh = np.zeros((D, HALO + TPC), dtype=xnT_full[b].dtype)
            if j > 0:
                xh[:, :] = xnT_full[b][:, t0 - HALO:t0 + TPC]
            else:
                xh[:, HALO:] = xnT_full[b][:, 0:TPC]
            m = dict(shared)
            m.update({
                "x": _c(x_cur[b][t0:t0 + TPC]),
                "xnT_h": xh,
                "halo_flag": np.full((128, 1), 1.0 if j > 0 else 0.0, f32),
                "o_aT": _c(oaT[b][:, t0:t0 + TPC]), "o_dT": _c(odT[b][:, t0:t0 + TPC]),
                "mod": _c(np.stack([mod_all[2 * l, b], mod_all[2 * l + 1, b]])),
            })
            in_maps.append(m)
        r2 = _run(p2, in_maps)
        x_cur = [np.concatenate([r2[b * 4 + j]["xout"] for j in range(4)], axis=0) for b in range(NB)]
        xnT_full = [np.concatenate([r2[b * 4 + j]["xnT_out"] for j in range(4)], axis=1) for b in range(NB)]
    return np.stack(x_cur).astype(f32)
```
